# Optimizing a Trainium2 kernel written in Bass

```python
import math
import jax, jax.numpy as jnp
from jax import lax
import numpy as np

D_MODEL = 1024
BATCH = 32
SEQ = 256
DEPTH = 2
DEC_BATCH = 8
DEC_SEQ = 2048
PAST_LEN = 256

GRID_W = 64
HEAD_DIM = 64
N_HEADS_A = 8
N_KV_A = 2
N_HEADS_B = 8
N_KV_CACHE = N_KV_A + N_HEADS_B
WIN_R = 8
WIN_C = 16
Q_BLOCK = 128
ROPE_THETA = 10000.0
N_HEADS_C = 8
DK_C = 128
DV_C = 128
CONV_K = 3
CHUNK = 64
D_FF = 4 * D_MODEL
N_MOD = 6
EPS = 1e-6
ATTN_IN = (N_HEADS_A + 2 * N_KV_A + 3 * N_HEADS_B) * HEAD_DIM
ATTN_OUT = (N_HEADS_A + N_HEADS_B) * HEAD_DIM
DELTA_QKV = N_HEADS_C * (2 * DK_C + DV_C)
DELTA_OUT = N_HEADS_C * DV_C
DELTA_IN = DELTA_QKV + DELTA_OUT + 4 * N_HEADS_C

kernel_name = 'hybrid_diffusion_prefix_step'


def split_last(x, sizes):
    offs, acc = [], 0
    for s in sizes[:-1]:
        acc += s
        offs.append(acc)
    return jnp.split(x, offs, axis=-1)


def rms_norm(x, g):
    xf = x.astype(jnp.float32)
    y = xf * lax.rsqrt(jnp.mean(xf * xf, axis=-1, keepdims=True) + EPS)
    return (y * g.astype(jnp.float32)).astype(x.dtype)


def l2_normalize(x):
    return x * lax.rsqrt(jnp.sum(x * x, axis=-1, keepdims=True) + EPS)


def modulation(cvec, w, b):
    m = jax.nn.silu(cvec) @ w + b
    return jnp.split(m[:, None, :], N_MOD, axis=-1)


def squared_relu_mlp(h, w1, w2):
    return jnp.square(jax.nn.relu(h @ w1)) @ w2


def axial_rope_tables(T, dtype):
    t = jnp.arange(T, dtype=jnp.int32)
    pos = jnp.stack([t // GRID_W, t % GRID_W], axis=-1).astype(jnp.float32)
    axis_dim = HEAD_DIM // 2
    inv_freq = 1.0 / (ROPE_THETA ** (jnp.arange(0, axis_dim, 2, dtype=jnp.float32) / axis_dim))
    ang = pos[:, :, None] * inv_freq
    return jnp.cos(ang).astype(dtype), jnp.sin(ang).astype(dtype)


def apply_axial_rope(x, cos, sin):
    B, T, H, _ = x.shape
    xs = x.reshape(B, T, H, 2, 2, HEAD_DIM // 4)
    x1, x2 = xs[..., 0, :], xs[..., 1, :]
    c = cos[None, :, None]
    s = sin[None, :, None]
    out = jnp.stack([x1 * c - x2 * s, x2 * c + x1 * s], axis=-2)
    return out.reshape(B, T, H, HEAD_DIM)


def blocked_attention(q, k, v):
    B, T = q.shape[:2]
    nb = T // Q_BLOCK
    qb = jnp.swapaxes(q.reshape(B, nb, Q_BLOCK, *q.shape[2:]), 0, 1)
    scale = HEAD_DIM ** -0.5

    def one_block(qi):
        s = jnp.einsum('bqkgd,bskd->bkgqs', qi, k).astype(jnp.float32) * scale
        p = jax.nn.softmax(s, axis=-1).astype(v.dtype)
        return jnp.einsum('bkgqs,bskd->bqkgd', p, v)

    o = lax.map(one_block, qb)
    return jnp.swapaxes(o, 0, 1).reshape(q.shape)


def neighbourhood_attention(q, k, v, k_ctx, v_ctx, rel_bias):
    B, T, H, hd = q.shape
    rows = T // GRID_W
    wr = min(WIN_R, rows)
    qg = q.reshape(B, rows, GRID_W, H, hd)
    kg = k.reshape(B, rows, GRID_W, H, hd)
    vg = v.reshape(B, rows, GRID_W, H, hd)
    cols = jnp.arange(GRID_W, dtype=jnp.int32)
    col_start = jnp.clip(cols - WIN_C // 2, 0, GRID_W - WIN_C)
    col_idx = col_start[:, None] + jnp.arange(WIN_C, dtype=jnp.int32)
    col_off = col_idx - cols[:, None] + (WIN_C - 1)
    n_loc = wr * WIN_C
    scale = hd ** -0.5

    def row_block(r):
        rs = jnp.clip(r - wr // 2, 0, rows - wr)
        kb = lax.dynamic_slice_in_dim(kg, rs, wr, axis=1)
        vb = lax.dynamic_slice_in_dim(vg, rs, wr, axis=1)
        kn = kb[:, :, col_idx]
        vn = vb[:, :, col_idx]
        qr = lax.dynamic_index_in_dim(qg, r, axis=1, keepdims=False)
        row_off = rs + jnp.arange(wr, dtype=jnp.int32) - r + (WIN_R - 1)
        bias = rel_bias[:, row_off][:, :, col_off].astype(jnp.float32)
        s_loc = jnp.einsum('bqhd,brqwhd->bhqrw', qr, kn).astype(jnp.float32) * scale
        s_loc = s_loc + jnp.transpose(bias, (0, 2, 1, 3))[None]
        s_ctx = jnp.einsum('bqhd,bphd->bhqp', qr, k_ctx).astype(jnp.float32) * scale
        s = jnp.concatenate([s_loc.reshape(B, H, GRID_W, n_loc), s_ctx], axis=-1)
        p = jax.nn.softmax(s, axis=-1).astype(v.dtype)
        p_loc = p[..., :n_loc].reshape(B, H, GRID_W, wr, WIN_C)
        p_ctx = p[..., n_loc:]
        return (jnp.einsum('bhqrw,brqwhd->bqhd', p_loc, vn)
                + jnp.einsum('bhqp,bphd->bqhd', p_ctx, v_ctx))

    out = lax.map(row_block, jnp.arange(rows, dtype=jnp.int32))
    return jnp.transpose(out, (1, 0, 2, 3, 4)).reshape(B, T, H, hd)


def attn_project(h, w_in, qn_a, kn_a, qn_b, kn_b):
    B, T, _ = h.shape
    sizes = tuple(n * HEAD_DIM for n in (N_HEADS_A, N_KV_A, N_KV_A, N_HEADS_B, N_HEADS_B, N_HEADS_B))
    parts = [p.reshape(B, T, -1, HEAD_DIM) for p in split_last(h @ w_in, sizes)]
    q_a, k_a, v_a, q_b, k_b, v_b = parts
    return (rms_norm(q_a, qn_a), rms_norm(k_a, kn_a), v_a,
            rms_norm(q_b, qn_b), rms_norm(k_b, kn_b), v_b)


def attn_mixer_context(h, w_in, qn_a, kn_a, qn_b, kn_b, w_out):
    B, T, _ = h.shape
    q_a, k_a, v_a, q_b, k_b, v_b = attn_project(h, w_in, qn_a, kn_a, qn_b, kn_b)
    o_a = blocked_attention(q_a.reshape(B, T, N_KV_A, N_HEADS_A // N_KV_A, HEAD_DIM), k_a, v_a)
    o_b = blocked_attention(q_b[:, :, :, None, :], k_b, v_b)
    o = jnp.concatenate([o_a.reshape(B, T, -1), o_b.reshape(B, T, -1)], axis=-1)
    return (o @ w_out, jnp.concatenate([k_a, k_b], axis=2), jnp.concatenate([v_a, v_b], axis=2))


def attn_mixer_latent(h, ctx_k, ctx_v, w_in, qn_a, kn_a, qn_b, kn_b, rel_bias, w_out):
    B, T, _ = h.shape
    q_a, k_a, v_a, q_b, k_b, v_b = attn_project(h, w_in, qn_a, kn_a, qn_b, kn_b)
    cos, sin = axial_rope_tables(T, h.dtype)
    q_a = apply_axial_rope(q_a, cos, sin)
    k_a = apply_axial_rope(k_a, cos, sin)
    k_all = jnp.concatenate([k_a, ctx_k[:, :, :N_KV_A]], axis=1)
    v_all = jnp.concatenate([v_a, ctx_v[:, :, :N_KV_A]], axis=1)
    o_a = blocked_attention(q_a.reshape(B, T, N_KV_A, N_HEADS_A // N_KV_A, HEAD_DIM), k_all, v_all)
    o_b = neighbourhood_attention(q_b, k_b, v_b, ctx_k[:, :, N_KV_A:], ctx_v[:, :, N_KV_A:], rel_bias)
    o = jnp.concatenate([o_a.reshape(B, T, -1), o_b.reshape(B, T, -1)], axis=-1)
    return o @ w_out


def centred_depthwise_conv(x, w):
    K = w.shape[0]
    return lax.conv_general_dilated(x, w.astype(x.dtype)[:, None, :], window_strides=(1,),
                                    padding=[((K - 1) // 2, K // 2)],
                                    dimension_numbers=('NWC', 'WIO', 'NWC'),
                                    feature_group_count=x.shape[-1])


def gated_delta_chunked(q, k, v, log_a, beta, s0):
    B, T, H, _ = q.shape
    DV = v.shape[-1]
    N = T // CHUNK

    def to_chunks(x):
        x = x.reshape(B, N, CHUNK, H, *x.shape[3:])
        return jnp.moveaxis(jnp.moveaxis(x, 1, 0), 3, 2)

    qc, kc, vc = to_chunks(q), to_chunks(k), to_chunks(v)
    bc = to_chunks(beta)
    g = jnp.cumsum(to_chunks(log_a), axis=-1)
    incl = jnp.tril(jnp.ones((CHUNK, CHUNK), dtype=bool))
    strict = jnp.tril(jnp.ones((CHUNK, CHUNK), dtype=bool), -1)
    diff = g[..., :, None] - g[..., None, :]
    decay = jnp.where(incl, jnp.exp(jnp.where(incl, diff, 0.0)), 0.0)
    kbeta = kc * bc[..., None]
    a_mat = jnp.where(strict, jnp.einsum('nbhid,nbhjd->nbhij', kbeta, kc) * decay, 0.0)
    m = a_mat + jnp.eye(CHUNK, dtype=a_mat.dtype)
    u = lax.linalg.triangular_solve(m, vc * bc[..., None], left_side=True, lower=True, unit_diagonal=True)
    w = lax.linalg.triangular_solve(m, kbeta * jnp.exp(g)[..., None], left_side=True, lower=True,
                                    unit_diagonal=True)

    def step(s, inp):
        qi, ki, ui, wi, gi, di = inp
        v_new = ui - jnp.einsum('bhck,bhkv->bhcv', wi, s)
        att = jnp.einsum('bhik,bhjk->bhij', qi, ki) * di
        o = (jnp.einsum('bhck,bhkv->bhcv', qi * jnp.exp(gi)[..., None], s)
             + jnp.einsum('bhij,bhjv->bhiv', att, v_new))
        g_last = gi[..., -1:]
        s = (s * jnp.exp(g_last)[..., None]
             + jnp.einsum('bhck,bhcv->bhkv', ki * jnp.exp(g_last - gi)[..., None], v_new))
        return s, o

    s_fin, o = lax.scan(step, s0, (qc, kc, u, w, g, decay))
    o = jnp.moveaxis(jnp.moveaxis(o, 2, 3), 0, 1).reshape(B, T, H, DV)
    return o, s_fin


def delta_project(h, w_in, conv_w, a_log, dt_bias):
    B, T, _ = h.shape
    qkv, z, ab = split_last(h @ w_in, (DELTA_QKV, DELTA_OUT, 4 * N_HEADS_C))
    qkv = jax.nn.silu(centred_depthwise_conv(qkv, conv_w))
    q, k, v = split_last(qkv, (N_HEADS_C * DK_C, N_HEADS_C * DK_C, N_HEADS_C * DV_C))
    q = l2_normalize(q.reshape(B, T, N_HEADS_C, DK_C).astype(jnp.float32)) * (DK_C ** -0.5)
    k = l2_normalize(k.reshape(B, T, N_HEADS_C, DK_C).astype(jnp.float32))
    v = v.reshape(B, T, N_HEADS_C, DV_C).astype(jnp.float32)
    ab = ab.astype(jnp.float32).reshape(B, T, 2, 2, N_HEADS_C)
    log_a = -jnp.exp(a_log.astype(jnp.float32)) * jax.nn.softplus(ab[:, :, 0] + dt_bias.astype(jnp.float32))
    beta = jax.nn.sigmoid(ab[:, :, 1])
    return q, k, v, z, log_a, beta


def bidirectional_delta(q, k, v, log_a, beta, s0_f, s0_b):
    flip = lambda t: jnp.flip(t, axis=1)
    o_f, s_f = gated_delta_chunked(q, k, v, log_a[:, :, 0], beta[:, :, 0], s0_f)
    o_b, s_b = gated_delta_chunked(flip(q), flip(k), flip(v), flip(log_a[:, :, 1]), flip(beta[:, :, 1]), s0_b)
    return o_f + flip(o_b), s_f, s_b


def delta_output(o, z, out_norm, w_out):
    B, T = o.shape[:2]
    zh = z.reshape(B, T, N_HEADS_C, DV_C)
    y = rms_norm(o.astype(z.dtype), out_norm) * jax.nn.silu(zh)
    return y.reshape(B, T, DELTA_OUT) @ w_out


def delta_mixer_context(h, w_in, conv_w, a_log, dt_bias, out_norm, w_out):
    q, k, v, z, log_a, beta = delta_project(h, w_in, conv_w, a_log, dt_bias)
    s0 = jnp.zeros((h.shape[0], N_HEADS_C, DK_C, DV_C), jnp.float32)
    o, s_f, s_b = bidirectional_delta(q, k, v, log_a, beta, s0, s0)
    return delta_output(o, z, out_norm, w_out), jnp.stack([s_f, s_b], axis=1)


def delta_mixer_latent(h, state, w_in, conv_w, a_log, dt_bias, out_norm, w_out):
    q, k, v, z, log_a, beta = delta_project(h, w_in, conv_w, a_log, dt_bias)
    st = state.astype(jnp.float32)
    o, _, _ = bidirectional_delta(q, k, v, log_a, beta, st[:, 0], st[:, 1])
    return delta_output(o, z, out_norm, w_out)


def setup_inputs(seed: int = 0) -> dict:
    key = jax.random.key(seed)
    ks = jax.random.split(key, 32)
    nrm = lambda i, shape, scale: jax.random.normal(ks[i], shape, jnp.float32) * scale
    gain = lambda i, n: 1.0 + nrm(i, (n,), 0.05)
    dt = jnp.exp(jax.random.uniform(ks[26], (2, N_HEADS_C), jnp.float32, math.log(1e-3), math.log(1e-1)))
    return {
        'x_prompt': nrm(0, (BATCH, SEQ, D_MODEL), 1.0),
        'x_sample': nrm(1, (DEC_BATCH, DEC_SEQ, D_MODEL), 1.0),
        'c': nrm(2, (DEC_BATCH, D_MODEL), 1.0),
        'cache_l0_k': nrm(3, (DEC_BATCH, PAST_LEN, N_KV_CACHE, HEAD_DIM), 1.0),
        'cache_l0_v': nrm(4, (DEC_BATCH, PAST_LEN, N_KV_CACHE, HEAD_DIM), 1.0),
        'state_l1': nrm(5, (DEC_BATCH, 2, N_HEADS_C, DK_C, DV_C), 0.1),
        'c_ctx': nrm(6, (D_MODEL,), 1.0),
        'l0_mod_w': nrm(7, (D_MODEL, N_MOD * D_MODEL), 0.5 * D_MODEL ** -0.5),
        'l0_mod_b': nrm(8, (N_MOD * D_MODEL,), 0.02),
        'l0_norm1': gain(9, D_MODEL),
        'l0_w_in': nrm(10, (D_MODEL, ATTN_IN), D_MODEL ** -0.5),
        'l0_q_norm_a': gain(11, HEAD_DIM),
        'l0_k_norm_a': gain(12, HEAD_DIM),
        'l0_q_norm_b': gain(13, HEAD_DIM),
        'l0_k_norm_b': gain(14, HEAD_DIM),
        'l0_rel_bias': nrm(15, (N_HEADS_B, 2 * WIN_R - 1, 2 * WIN_C - 1), 0.1),
        'l0_w_out': nrm(16, (ATTN_OUT, D_MODEL), ATTN_OUT ** -0.5),
        'l0_norm2': gain(17, D_MODEL),
        'l0_mlp_w1': nrm(18, (D_MODEL, D_FF), D_MODEL ** -0.5),
        'l0_mlp_w2': nrm(19, (D_FF, D_MODEL), D_FF ** -0.5),
        'l1_mod_w': nrm(20, (D_MODEL, N_MOD * D_MODEL), 0.5 * D_MODEL ** -0.5),
        'l1_mod_b': nrm(21, (N_MOD * D_MODEL,), 0.02),
        'l1_norm1': gain(22, D_MODEL),
        'l1_w_in': nrm(23, (D_MODEL, DELTA_IN), D_MODEL ** -0.5),
        'l1_conv_w': nrm(24, (CONV_K, DELTA_QKV), CONV_K ** -0.5),
        'l1_a_log': jnp.log(jax.random.uniform(ks[25], (2, N_HEADS_C), jnp.float32, 1.0, 16.0)),
        'l1_dt_bias': dt + jnp.log(-jnp.expm1(-dt)),
        'l1_out_norm': gain(27, DV_C),
        'l1_w_out': nrm(28, (DELTA_OUT, D_MODEL), DELTA_OUT ** -0.5),
        'l1_norm2': gain(29, D_MODEL),
        'l1_mlp_w1': nrm(30, (D_MODEL, D_FF), D_MODEL ** -0.5),
        'l1_mlp_w2': nrm(31, (D_FF, D_MODEL), D_FF ** -0.5),
    }


def reference(x_prompt, x_sample, c, cache_l0_k, cache_l0_v, state_l1, c_ctx,
              l0_mod_w, l0_mod_b, l0_norm1, l0_w_in, l0_q_norm_a, l0_k_norm_a, l0_q_norm_b, l0_k_norm_b,
              l0_rel_bias, l0_w_out, l0_norm2, l0_mlp_w1, l0_mlp_w2,
              l1_mod_w, l1_mod_b, l1_norm1, l1_w_in, l1_conv_w, l1_a_log, l1_dt_bias, l1_out_norm,
              l1_w_out, l1_norm2, l1_mlp_w1, l1_mlp_w2):
    common = ((l0_mod_w, l0_mod_b, l0_norm1, l0_norm2, l0_mlp_w1, l0_mlp_w2),
              (l1_mod_w, l1_mod_b, l1_norm1, l1_norm2, l1_mlp_w1, l1_mlp_w2))
    xp, xs = x_prompt, x_sample
    for layer in range(DEPTH):
        mod_w, mod_b, norm1, norm2, mlp_w1, mlp_w2 = common[layer]
        sh1_p, sc1_p, g1_p, sh2_p, sc2_p, g2_p = modulation(c_ctx[None, :], mod_w, mod_b)
        sh1_s, sc1_s, g1_s, sh2_s, sc2_s, g2_s = modulation(c, mod_w, mod_b)
        hp = rms_norm(xp, norm1) * (1 + sc1_p) + sh1_p
        hs = rms_norm(xs, norm1) * (1 + sc1_s) + sh1_s
        if layer % 2 == 0:
            mp, new_k, new_v = attn_mixer_context(hp, l0_w_in, l0_q_norm_a, l0_k_norm_a,
                                                  l0_q_norm_b, l0_k_norm_b, l0_w_out)
            ms = attn_mixer_latent(hs, cache_l0_k, cache_l0_v, l0_w_in, l0_q_norm_a, l0_k_norm_a,
                                   l0_q_norm_b, l0_k_norm_b, l0_rel_bias, l0_w_out)
        else:
            mp, new_s = delta_mixer_context(hp, l1_w_in, l1_conv_w, l1_a_log, l1_dt_bias, l1_out_norm, l1_w_out)
            ms = delta_mixer_latent(hs, state_l1, l1_w_in, l1_conv_w, l1_a_log, l1_dt_bias,
                                    l1_out_norm, l1_w_out)
        xp = xp + g1_p * mp
        xs = xs + g1_s * ms
        xp = xp + g2_p * squared_relu_mlp(rms_norm(xp, norm2) * (1 + sc2_p) + sh2_p, mlp_w1, mlp_w2)
        xs = xs + g2_s * squared_relu_mlp(rms_norm(xs, norm2) * (1 + sc2_s) + sh2_s, mlp_w1, mlp_w2)
    return (xp, xs, new_k, new_v, new_s.astype(x_prompt.dtype))
```

```python
from contextlib import ExitStack
import os
import numpy as np
import concourse.bass as bass
import concourse.mybir as mybir
from concourse.bass_utils import run_bass_kernel_spmd

F32 = mybir.dt.float32
BF16 = mybir.dt.bfloat16
AF = mybir.ActivationFunctionType
ALU = mybir.AluOpType
AX = mybir.AxisListType
ENGS = ("pe", "act", "dve", "pool", "sp")

NTOK = 3072
NPR = 1024
NB = 6
EPS = 1e-6
NEG = -30000.0
STORE_Q = os.environ.get('KSTQ', 'act')
FAST_RECIP = False
SAME_ENGINE_INORDER = bool(int(os.environ.get('KSAME', '0')))


class Res:
    __slots__ = ("name", "last_w", "readers", "dsem", "dcount", "excl", "norecycle")

    def __init__(self, name, excl=False):
        self.name = name
        self.excl = excl
        self.norecycle = False
        self.last_w = None
        self.readers = []
        self.dsem = None
        self.dcount = 0


class Prog:
    def __init__(self, nc):
        self.nc = nc
        self.ops = {e: [] for e in ENGS}
        self.cnt = {e: 0 for e in ENGS}
        self.waited = {e: {} for e in ENGS}
        self.semkeys = list(ENGS)
        self.owners = []
        self.free = []
        self.nobar = set()

    def _dep_tokens(self, reads, writes):
        toks = []
        for r in reads:
            if r.last_w is not None:
                toks.append(r.last_w)
            if r.excl:
                toks.extend(r.readers)
        for w in writes:
            if w.last_w is not None:
                toks.append(w.last_w)
            toks.extend(w.readers)
        return toks

    def _add_waits(self, eng, toks, skip=None):
        need = {}
        for (k, v) in toks:
            if skip is not None and k == skip:
                continue
            if v > need.get(k, 0):
                need[k] = v
        waits = []
        wd = self.waited[eng]
        for k, v in need.items():
            if wd.get(k, 0) >= v:
                continue
            wd[k] = v
            waits.append((k, v))
        return waits

    def op(self, eng, fn, reads=(), writes=(), same_ok=False):
        toks = self._dep_tokens(reads, writes)
        if SAME_ENGINE_INORDER and eng != "pool":
            same_ok = True
        waits = self._add_waits(eng, toks, skip=(eng if same_ok else None))
        self.cnt[eng] += 1
        tok = (eng, self.cnt[eng])
        self.ops[eng].append((fn, waits, (eng, 1, self.cnt[eng])))
        for r in reads:
            r.readers.append(tok)
        for w in writes:
            w.last_w = tok
            w.readers = []
        return tok

    def dma(self, eng, fn, owner, reads=(), writes=()):
        if owner.dsem is None:
            if self.free and not owner.norecycle and eng != "pool":
                owner.dsem, owner.dcount = self.free.pop()
            else:
                owner.dsem = "d%d" % len(self.semkeys)
                owner.dcount = 0
                self.semkeys.append(owner.dsem)
            self.owners.append(owner)
        toks = self._dep_tokens(reads, writes)
        waits = self._add_waits(eng, toks)
        owner.dcount += 16
        tok = (owner.dsem, owner.dcount)
        self.ops[eng].append((fn, waits, (owner.dsem, 16)))
        for r in reads:
            r.readers.append(tok)
        for w in writes:
            w.last_w = tok
            w.readers = []
        return tok

    def barrier(self):
        final = {}
        for eng in ENGS:
            for (fn, waits, inc) in self.ops[eng]:
                if inc is not None:
                    final[inc[0]] = final.get(inc[0], 0) + inc[1]
        items = [(k, v) for k, v in final.items() if k not in self.nobar]
        for eng in ENGS:
            waits = self._add_waits(eng, items)
            self.ops[eng].append((None, waits, None))
        keep = []
        for o in self.owners:
            if o.dsem in self.nobar or o.norecycle:
                keep.append(o)
            else:
                self.free.append((o.dsem, o.dcount))
                o.dsem = None
        self.owners = keep

    def finish_wait(self, eng, resources):
        toks = [r.last_w for r in resources if r.last_w is not None]
        waits = self._add_waits(eng, toks)
        self.ops[eng].append((None, waits, None))

    def emit(self, stack):
        nc = self.nc
        sems = {}
        for k in self.semkeys:
            sems[k] = stack.enter_context(nc.semaphore(k))
        block = stack.enter_context(nc.Block())
        handles = {"pe": block.tensor, "act": block.scalar, "dve": block.vector,
                   "pool": block.gpsimd, "sp": block.sync}
        needed = {e: set() for e in ENGS}
        for e in ENGS:
            for fn, waits, inc in self.ops[e]:
                for (k, v) in waits:
                    if k in needed:
                        needed[k].add(v)
        rank = {e: {v: i + 1 for i, v in enumerate(sorted(needed[e]))} for e in ENGS}
        for e in ENGS:
            ops = self.ops[e]

            def body(eh, ops=ops):
                for fn, waits, inc in ops:
                    for (k, v) in waits:
                        eh.wait_ge(sems[k], rank[k][v] if k in rank else v)
                    if fn is not None:
                        ins = fn(eh)
                        if len(inc) == 3:
                            if inc[2] in needed[inc[0]]:
                                ins.then_inc(sems[inc[0]], 1)
                        else:
                            ins.then_inc(sems[inc[0]], inc[1])
            handles[e](body)


class T:
    def __init__(self, t, name):
        self.t = t
        self.r = Res(name)

    def __getitem__(self, k):
        return self.t[k]


class Builder:
    def __init__(self, stage):
        self.stage = stage
        self.nc = bass.Bass("TRN2", target_bir_lowering=False)
        self.P = Prog(self.nc)
        self.st = ExitStack()
        self.rot = {}
        self.swown = {}

    @staticmethod
    def interleave(gens, width):
        active = []
        it = iter(gens)
        more = True
        while True:
            while more and len(active) < width:
                try:
                    active.append(next(it))
                except StopIteration:
                    more = False
            if not active:
                break
            for g_ in list(active):
                try:
                    next(g_)
                except StopIteration:
                    active.remove(g_)

    @staticmethod
    def interleave2(main, gens, width):
        active = []
        it = iter(gens)
        more = True
        main_alive = main is not None
        while True:
            while more and len(active) < width:
                try:
                    active.append(next(it))
                except StopIteration:
                    more = False
            if not active and not main_alive:
                break
            if main_alive:
                try:
                    next(main)
                except StopIteration:
                    main_alive = False
            for g_ in list(active):
                try:
                    next(g_)
                except StopIteration:
                    active.remove(g_)

    def sb(self, name, shape, dt):
        return T(self.st.enter_context(self.nc.sbuf_tensor("s_" + name, list(shape), dt)), name)

    def sbn(self, name, n, shape, dt):
        return [self.sb("%s%d" % (name, i), shape, dt) for i in range(n)]

    def nxt(self, lst):
        k = id(lst)
        i = self.rot.get(k, 0)
        self.rot[k] = i + 1
        return lst[i % len(lst)]

    def din(self, name, shape, dt=F32):
        return self.nc.dram_tensor(name, list(shape), dt, kind="ExternalInput").ap()

    def dout(self, name, shape, dt=F32):
        return self.nc.dram_tensor(name, list(shape), dt, kind="ExternalOutput").ap()

    def dscr(self, name, shape, dt=BF16):
        kind = "ExternalOutput" if (os.environ.get("KDEBUG") and name in os.environ["KDEBUG"].split(",")) else "Internal"
        return self.nc.dram_tensor(name, list(shape), dt, kind=kind).ap()

    def mm(self, out, lhsT, rhs, start, stop, reads, writes):
        self.P.op("pe", lambda e: e.matmul(out, lhsT, rhs, start=start, stop=stop),
                  reads=reads, writes=writes, same_ok=True)

    def tr(self, out, in_, ident, reads, writes):
        self.P.op("pe", lambda e: e.transpose(out, in_, ident), reads=reads, writes=writes, same_ok=True)

    def act(self, out, in_, func, reads, writes, scale=None, bias=None):
        kw = {}
        if scale is not None:
            kw["scale"] = scale
        if bias is not None:
            kw["bias"] = bias
        self.P.op("act", lambda e: e.activation(out=out, in_=in_, func=func, **kw), reads=reads, writes=writes)

    def tt(self, eng, out, in0, in1, op, reads, writes):
        self.P.op(eng, lambda e: e.tensor_tensor(out=out, in0=in0, in1=in1, op=op), reads=reads, writes=writes)

    def ts(self, eng, out, in0, s1, s2, op0, op1, reads, writes):
        if op1 is None:
            self.P.op(eng, lambda e: e.tensor_scalar(out=out, in0=in0, scalar1=s1, scalar2=None, op0=op0),
                      reads=reads, writes=writes)
        else:
            self.P.op(eng, lambda e: e.tensor_scalar(out=out, in0=in0, scalar1=s1, scalar2=s2, op0=op0, op1=op1),
                      reads=reads, writes=writes)

    def stt(self, out, in0, scalar, in1, op0, op1, reads, writes):
        self.P.op("dve", lambda e: e.scalar_tensor_tensor(out=out, in0=in0, scalar=scalar, in1=in1, op0=op0, op1=op1),
                  reads=reads, writes=writes)

    def cp(self, eng, out, in_, reads, writes):
        if eng == "act":
            self.P.op("act", lambda e: e.copy(out=out, in_=in_), reads=reads, writes=writes)
        else:
            self.P.op(eng, lambda e: e.tensor_copy(out=out, in_=in_), reads=reads, writes=writes)

    def recip(self, out, in_, reads, writes):
        if FAST_RECIP:
            self.P.op("dve", lambda e: e.reciprocal_approx_fast(out, in_), reads=reads, writes=writes)
        else:
            self.P.op("dve", lambda e: e.reciprocal(out=out, in_=in_), reads=reads, writes=writes)

    def memset(self, eng, ap, val, writes):
        self.P.op(eng, lambda e: e.memset(ap, val), writes=writes)

    def ld(self, out, in_, owner, reads=(), writes=(), q="sp"):
        if q == "sp" and STORE_Q != "sp" and any(r is owner for r in reads):
            q = STORE_Q
        if q == "pool":
            key = id(owner)
            if key not in self.swown:
                self.swown[key] = Res("sw_" + owner.name)
                self.swown[key].norecycle = True
            owner = self.swown[key]
        self.P.dma(q, lambda e: e.dma_start(out=out, in_=in_), owner, reads=reads, writes=writes)

    def build(self):
        nc, P = self.nc, self.P
        stage = self.stage
        xT_d = self.din("xT", [1024, NTOK])
        cT_d = self.din("cT", [128, 16])
        ck_d = self.din("ck", [256, 640])
        cv_d = self.din("cv", [256, 640])
        st_d = self.din("state", [16, 128, 128])
        Lw = []
        for l in range(2):
            d = {}
            d["mod_w"] = self.din("l%d_mod_w" % l, [1024, 6144])
            d["mod_b"] = self.din("l%d_mod_bT" % l, [128, 48])
            d["n1"] = self.din("l%d_n1T" % l, [128, 8])
            d["n2"] = self.din("l%d_n2T" % l, [128, 8])
            d["w_in"] = self.din("l%d_w_in" % l, [1024, 2304 if l == 0 else 4128])
            d["w_out"] = self.din("l%d_w_out" % l, [1024, 1024])
            d["w1"] = self.din("l%d_w1" % l, [1024, 4096])
            d["w2"] = self.din("l%d_w2" % l, [4096, 1024])
            Lw.append(d)
        gains_d = self.din("gains", [128, 4])
        gk_tm_d = self.din("gk_tm", [128, 640])
        cos_d = self.din("cosT", [128, 2048])
        sin_d = self.din("sinT", [128, 2048])
        perm_d = self.din("perm", [128, 128])
        bias_d = self.din("bias_exp", [8, 128, 2048])
        ident_d = self.din("ident", [128, 128])
        l1c_d = self.din("l1c", [128, 1024])
        l1m_d = self.din("l1m", [128, 384])
        convw_d = self.din("convw", [128, 72])
        alog_d = self.din("alog_b", [128, 16])
        dtb_d = self.din("dtb_b", [128, 16])
        onorm_d = self.din("onorm", [128, 1])

        yT_d = self.dout("yT", [1024, NTOK])
        nk_d = self.dout("nk", [NPR, 640])
        nv_d = self.dout("nv", [NPR, 640])
        ns_d = self.dout("ns", [4, 16, 128, 128])

        Wb = []
        for l in range(2):
            d = {}
            d["w_in"] = self.dscr("wb%d_in" % l, [1024, 2304 if l == 0 else 4128])
            d["w_out"] = self.dscr("wb%d_out" % l, [1024, 1024])
            d["w1"] = self.dscr("wb%d_w1" % l, [1024, 4096])
            d["w2"] = self.dscr("wb%d_w2" % l, [4096, 1024])
            Wb.append(d)
        QKT = self.dscr("QKT", [13, 128, NTOK])
        V0 = self.dscr("V0", [NTOK, 640])
        KC = self.dscr("KC", [5, 128, 256])
        OT = self.dscr("OT", [8, 128, NTOK])
        rWb = [{k: Res("rWb%d%s" % (l, k)) for k in Wb[l]} for l in range(2)]
        rQKT = [[Res("rQKT") for _ in range(NB)] for _ in range(13)]
        rV0 = [Res("rV0") for _ in range(NB)]
        rKC = Res("rKC")
        rOT = [[Res("rOT") for _ in range(NB)] for _ in range(8)]

        X1 = self.dscr("X1", [1024, NTOK], F32)
        rX1 = [Res("rX1_%d" % b_) for b_ in range(NB)]
        rXin = [Res("rXin%d" % b_) for b_ in range(NB)]
        rY = [Res("rY%d" % b_) for b_ in range(NB)]

        ones_bf = self.sb("ones_bf", [128, 128], BF16)
        bd_bf = self.sb("bd_bf", [128, 128], BF16)
        perm_bf = self.sb("perm_bf", [128, 128], BF16)
        ident_f = self.sb("ident_f", [128, 128], F32)
        stage_f = self.sb("stage_f", [128, 128], F32)
        gains = self.sb("gains", [128, 4], F32)
        gk_tm = self.sb("gk_tm", [128, 640], F32)
        cT = self.sb("cT", [128, 8, 2], F32)
        scT = self.sb("scT", [128, 8, 2], F32)
        modT = self.sb("modT", [128, 48, 2], F32)
        modb = self.sb("modb", [128, 48], F32)
        n1 = self.sb("n1", [128, 8], F32)
        n2 = self.sb("n2", [128, 8], F32)
        G1 = self.sb("G1", [128, 8, 2], F32)
        G2 = self.sb("G2", [128, 8, 2], F32)
        epsc = self.sb("epsc", [128, 1], F32)

        ps = [T(self.st.enter_context(nc.psum_tensor("ps%d" % i, [128, 512], F32)), "ps%d" % i) for i in range(8)]
        for p_ in ps:
            p_.r.excl = True
        psA = ps[0:4]
        psB = ps[4:6]
        psC = ps[6:8]
        psBC = ps[4:8]
        psB1 = ps[5:8]

        xb = self.sbn("xb", 2, [128, 8, 512], F32)
        wbuf = self.sbn("wbuf", 3, [128, 4096], BF16)
        sqb = self.sb("sqb", [128, 8, 512], BF16)
        hT = self.sbn("hT", 2, [128, 8, 512], BF16)
        Rb = self.sb("Rb", [128, 512], F32)
        rtmp = self.sbn("rtmp", 5, [128, 512], F32)
        ftmp = self.sbn("ftmp", 4, [128, 512], F32)
        btmp = self.sbn("btmp", 4, [128, 512], BF16)
        ostg = self.sbn("ostg", 3, [128, 512], BF16)
        arena = self.st.enter_context(nc.sbuf_tensor("s_arena", [128, 20480], F32))

        def aview(off_b, shape, dt, name):
            n = 1
            for d_ in shape[1:]:
                n *= d_
            esz = 4 if dt == F32 else 2
            nb = n * esz
            assert off_b % 4 == 0 and off_b + nb <= 81920, (name, off_b, nb)
            ap = arena[:, off_b // 4:(off_b + nb + 3) // 4]
            if dt != F32:
                ap = ap.bitcast(dt)
            if len(shape) == 3:
                ap = ap.rearrange("p (a b) -> p a b", b=shape[2])
            elif len(shape) == 4:
                ap = ap.rearrange("p (a b c) -> p a b c", b=shape[2], c=shape[3])
            return T(ap, name), off_b + ((nb + 31) // 32) * 32

        barrier = P.barrier

        self.memset("dve", ones_bf[:, :], 1.0, [ones_bf.r])
        self.memset("dve", bd_bf[:, :], 0.0, [bd_bf.r])
        self.memset("dve", bd_bf[0:64, 0:64], 1.0, [bd_bf.r])
        self.memset("dve", bd_bf[64:128, 64:128], 1.0, [bd_bf.r])
        self.memset("dve", epsc[:, :], EPS, [epsc.r])
        self.ld(stage_f[:, :], perm_d, stage_f.r, writes=[stage_f.r])
        self.cp("dve", perm_bf[:, :], stage_f[:, :], [stage_f.r], [perm_bf.r])
        self.ld(ident_f[:, :], ident_d, ident_f.r, writes=[ident_f.r])
        self.ld(gains[:, :], gains_d, gains.r, writes=[gains.r])
        self.ld(gk_tm[:, :], gk_tm_d, gk_tm.r, writes=[gk_tm.r])
        self.ld(cT[:, :, :], cT_d.rearrange("p (c s) -> p c s", s=2), cT.r, writes=[cT.r])
        self.act(scT[:, :, :], cT[:, :, :], AF.Silu, [cT.r], [scT.r])

        for l in range(2):
            for k in ("w_in", "w_out", "w1", "w2"):
                cr = Res("cast%d%s" % (l, k))
                src, dst = Lw[l][k], Wb[l][k]
                nrow = src.shape[0]
                for r0 in range(0, nrow, 256):
                    P.dma("pool", (lambda s_, d_: (lambda e: e.dma_start(out=d_, in_=s_)))(src[r0:r0 + 256, :], dst[r0:r0 + 256, :]),
                          cr, writes=[rWb[l][k]])
                    P.nobar.add(cr.dsem)

        def xdram(Xd):
            return Xd.rearrange("(c p) t -> p c t", p=128)

        def load_x(Xd, rXd, b):
            x = self.nxt(xb)
            self.ld(x[:, :, :], xdram(Xd)[:, :, b * 512:(b + 1) * 512], x.r, reads=[rXd[b]], writes=[x.r])
            return x

        def store_x(x, Xd, rXd, b):
            self.ld(xdram(Xd)[:, :, b * 512:(b + 1) * 512], x[:, :, :], x.r, reads=[x.r], writes=[rXd[b]])

        def modulation(l):
            mw0, o = aview(0, [128, 8, 256], F32, "mw0")
            mw1, o = aview(o, [128, 8, 256], F32, "mw1")
            mwbuf = [mw0, mw1]
            self.ld(modb[:, :], Lw[l]["mod_b"], modb.r, writes=[modb.r])
            self.ld(n1[:, :], Lw[l]["n1"], n1.r, writes=[n1.r])
            self.ld(n2[:, :], Lw[l]["n2"], n2.r, writes=[n2.r])
            pm = psC[0]
            mw = Lw[l]["mod_w"].rearrange("(c p) n -> p c n", p=128)
            for g in range(24):
                wt = self.nxt(mwbuf)
                self.ld(wt[:, :, :], mw[:, :, g * 256:(g + 1) * 256], wt.r, writes=[wt.r])
                for j in range(2):
                    ch = g * 2 + j
                    for kc in range(8):
                        self.mm(pm[:, ch * 2:ch * 2 + 2], wt[:, kc, j * 128:(j + 1) * 128], scT[:, kc, :],
                                kc == 0, kc == 7, [wt.r, scT.r], [pm.r])
            pmv = pm[:, 0:96].rearrange("p (c s) -> p c s", s=2)
            for s_ in range(2):
                self.tt("dve", modT[:, :, s_], pmv[:, :, s_], modb[:, :], ALU.add, [pm.r, modb.r], [modT.r])
            for s_ in range(2):
                self.stt(G1[:, :, s_], modT[:, 8:16, s_], 1.0, n1[:, :], ALU.add, ALU.mult, [modT.r, n1.r], [G1.r])
                self.stt(G2[:, :, s_], modT[:, 32:40, s_], 1.0, n2[:, :], ALU.add, ALU.mult, [modT.r, n2.r], [G2.r])
            barrier()

        def norm_mod(x, b, G, shift_base, pa=None):
            s_ = 0 if b < 2 else 1
            self.act(sqb[:, :, :], x[:, :, :], AF.Square, [x.r], [sqb.r])
            if pa is None:
                pa = self.nxt(psB)
            for kc in range(8):
                self.mm(pa[:, :], ones_bf[:, :], sqb[:, kc, :], kc == 0, kc == 7, [ones_bf.r, sqb.r], [pa.r])
            rt = self.nxt(rtmp)
            self.act(rt[:, :], pa[:, :], AF.Ln, [pa.r, epsc.r], [rt.r], scale=1.0 / 1024.0, bias=epsc[:, 0:1])
            self.act(Rb[:, :], rt[:, :], AF.Exp, [rt.r], [Rb.r], scale=-0.5)
            h = self.nxt(hT)
            for kc in range(8):
                ft = self.nxt(ftmp)
                self.tt("dve", ft[:, :], x[:, kc, :], Rb[:, :], ALU.mult, [x.r, Rb.r], [ft.r])
                self.act(h[:, kc, :], ft[:, :], AF.Identity, [ft.r, G.r, modT.r], [h.r],
                         scale=G[:, kc, s_:s_ + 1], bias=modT[:, shift_base + kc, s_:s_ + 1])
            return h

        def load_w(Wd, rW, KC_, n0, ncols):
            wt = self.nxt(wbuf)
            v = wt.t[:, 0:KC_ * ncols].rearrange("p (c n) -> p c n", n=ncols)
            self.ld(v, Wd.rearrange("(c p) n -> p c n", p=128)[:, :, n0:n0 + ncols], wt.r, reads=[rW], writes=[wt.r])
            return wt, v

        def resid_mlp(l, x, b):
            s_ = 0 if b < 2 else 1
            hid, _ = aview(0, [128, 32, 512], BF16, "hid")
            h2 = norm_mod(x, b, G2, 24)
            for g in range(8):
                wt, wv = load_w(Wb[l]["w1"], rWb[l]["w1"], 8, g * 512, 512)
                for j in range(4):
                    pm = self.nxt(psA)
                    for kc in range(8):
                        self.mm(pm[:, :], wv[:, kc, j * 128:(j + 1) * 128], h2[:, kc, :], kc == 0, kc == 7, [wt.r, h2.r], [pm.r])
                    rl = self.nxt(ftmp)
                    self.act(rl[:, :], pm[:, :], AF.Relu, [pm.r], [rl.r])
                    self.tt("pool", hid[:, g * 4 + j, :], rl[:, :], rl[:, :], ALU.mult, [rl.r], [hid.r])
            for n in range(8):
                wt = self.nxt(wbuf)
                wv = wt.t[:, 0:4096].rearrange("p (c n) -> p c n", n=128)
                self.ld(wv, Wb[l]["w2"].rearrange("(c p) n -> p c n", p=128)[:, :, n * 128:(n + 1) * 128], wt.r,
                        reads=[rWb[l]["w2"]], writes=[wt.r])
                pm = self.nxt(psA)
                for hc in range(32):
                    self.mm(pm[:, :], wv[:, hc, :], hid[:, hc, :], hc == 0, hc == 31, [wt.r, hid.r], [pm.r])
                self.stt(x[:, n, :], pm[:, :], modT[:, 40 + n, s_:s_ + 1], x[:, n, :], ALU.mult, ALU.add,
                         [pm.r, modT.r, x.r], [x.r])

        def out_proj_resid(l, x, b, OTd, rOTd, ot=None):
            s_ = 0 if b < 2 else 1
            if ot is None:
                ot = self.nxt(hT)
                self.ld(ot[:, :, :], OTd.rearrange("c p t -> p c t")[:, :, b * 512:(b + 1) * 512], ot.r,
                        reads=[rOTd[c_][b] for c_ in range(8)], writes=[ot.r])
            for g in range(2):
                wt, wv = load_w(Wb[l]["w_out"], rWb[l]["w_out"], 8, g * 512, 512)
                for j in range(4):
                    n = g * 4 + j
                    pm = self.nxt(psA)
                    for kc in range(8):
                        self.mm(pm[:, :], wv[:, kc, j * 128:(j + 1) * 128], ot[:, kc, :], kc == 0, kc == 7, [wt.r, ot.r], [pm.r])
                    self.stt(x[:, n, :], pm[:, :], modT[:, 16 + n, s_:s_ + 1], x[:, n, :], ALU.mult, ALU.add,
                             [pm.r, modT.r, x.r], [x.r])

        if stage >= 2:
            modulation(0)

        o = 0
        cosb, o = aview(o, [128, 512], F32, "cosb")
        sinb, o = aview(o, [128, 512], F32, "sinb")
        vst0, o = aview(o, [128, 640], BF16, "vst0")
        vst1, o = aview(o, [128, 640], BF16, "vst1")
        vst = [vst0, vst1]
        nv0, o = aview(o, [128, 640], F32, "nv0")
        nv1, o = aview(o, [128, 640], F32, "nv1")
        nvst = [nv0, nv1]
        nk0, o = aview(o, [128, 640], F32, "nk0")
        nk1, o = aview(o, [128, 640], F32, "nk1")
        nkst = [nk0, nk1]
        ksq, o = aview(o, [128, 640], F32, "ksq")
        kss, o = aview(o, [128, 10], F32, "kss")
        krs, o = aview(o, [128, 10], F32, "krs")

        def l0_fm_chunk(b, h, wv, j, qi, gi, rope):
            sl = slice(b * 512, (b + 1) * 512)
            pm = self.nxt(psA)
            for kc in range(8):
                self.mm(pm[:, :], wv[0][:, kc, j * 128:(j + 1) * 128], h[:, kc, :], kc == 0, kc == 7,
                        [wv[1].r, h.r], [pm.r])
            yield
            sq = self.nxt(btmp)
            self.act(sq[:, :], pm[:, :], AF.Square, [pm.r], [sq.r])
            pn = self.nxt(psB)
            self.mm(pn[:, :], bd_bf[:, :], sq[:, :], True, True, [bd_bf.r, sq.r], [pn.r])
            rt = self.nxt(rtmp)
            self.act(rt[:, :], pn[:, :], AF.Ln, [pn.r, epsc.r], [rt.r], scale=1.0 / 64.0, bias=epsc[:, 0:1])
            rr = self.nxt(rtmp)
            self.act(rr[:, :], rt[:, :], AF.Exp, [rt.r], [rr.r], scale=-0.5)
            qn = self.nxt(ftmp)
            self.tt("dve", qn[:, :], pm[:, :], rr[:, :], ALU.mult, [pm.r, rr.r], [qn.r])
            ob = self.nxt(ostg)
            if not rope:
                self.act(ob[:, :], qn[:, :], AF.Identity, [qn.r, gains.r], [ob.r], scale=gains[:, gi:gi + 1])
            else:
                qg = self.nxt(ftmp)
                self.ts("dve", qg[:, :], qn[:, :], gains[:, gi:gi + 1], None, ALU.mult, None, [qn.r, gains.r], [qg.r])
                qb_ = self.nxt(btmp)
                self.cp("act", qb_[:, :], qg[:, :], [qg.r], [qb_.r])
                pr = self.nxt(psB)
                self.mm(pr[:, :], perm_bf[:, :], qb_[:, :], True, True, [perm_bf.r, qb_.r], [pr.r])
                a = self.nxt(ftmp)
                self.tt("dve", a[:, :], qg[:, :], cosb[:, :], ALU.mult, [qg.r, cosb.r], [a.r])
                b2 = self.nxt(rtmp)
                self.tt("dve", b2[:, :], pr[:, :], sinb[:, :], ALU.mult, [pr.r, sinb.r], [b2.r])
                self.tt("pool", ob[:, :], a[:, :], b2[:, :], ALU.add, [a.r, b2.r], [ob.r])
            self.ld(QKT[qi, :, sl], ob[:, :], ob.r, reads=[ob.r], writes=[rQKT[qi][b]])

        def k_norm_tm(pk, ncols, c0, tok0):
            nh = ncols // 64
            self.act(ksq[:, 0:ncols], pk[:, 0:ncols], AF.Square, [pk.r], [ksq.r])
            self.P.op("dve", lambda e: e.tensor_reduce(out=kss[:, 0:nh], in_=ksq[:, 0:ncols].rearrange("p (h d) -> p h d", d=64),
                                                        axis=AX.X, op=ALU.add), reads=[ksq.r], writes=[kss.r])
            self.act(krs[:, 0:nh], kss[:, 0:nh], AF.Sqrt, [kss.r, epsc.r], [krs.r], scale=1.0 / 64.0, bias=epsc[:, 0:1])
            self.recip(kss[:, 0:nh], krs[:, 0:nh], [krs.r], [kss.r])
            nk = self.nxt(nkst)
            for hh in range(nh):
                self.stt(nk[:, c0 + hh * 64:c0 + (hh + 1) * 64], pk[:, hh * 64:(hh + 1) * 64], kss[:, hh:hh + 1],
                         gk_tm[:, c0 + hh * 64:c0 + (hh + 1) * 64], ALU.mult, ALU.mult, [pk.r, kss.r, gk_tm.r], [nk.r])
            self.ld(nk_d[tok0:tok0 + 128, c0:c0 + ncols], nk[:, c0:c0 + ncols], nk.r, reads=[nk.r], writes=[Res("nko")])

        KNB = int(os.environ.get('KNB', NB))
        for b in range(KNB if stage >= 3 else 0):
            samp = b >= 2
            x = load_x(xT_d, rXin, b)
            h = norm_mod(x, b, G1, 0)
            if samp:
                t0 = (b - 2) * 512
                self.ld(cosb[:, :], cos_d[:, t0:t0 + 512], cosb.r, writes=[cosb.r])
                self.ld(sinb[:, :], sin_d[:, t0:t0 + 512], sinb.r, writes=[sinb.r])
            wt, wv = load_w(Wb[0]["w_in"], rWb[0]["w_in"], 8, 0, 512)
            self.interleave((l0_fm_chunk(b, h, (wv, wt), j, j, 0, samp) for j in range(4)), int(os.environ.get('KL0W', 3)))
            wt1, wv1 = load_w(Wb[0]["w_in"], rWb[0]["w_in"], 8, 512, 512)
            self.interleave(iter([l0_fm_chunk(b, h, (wv1, wt1), 0, 4, 1, samp), l0_fm_chunk(b, h, (wv1, wt1), 2, 5, 2, False),
                                  l0_fm_chunk(b, h, (wv1, wt1), 3, 6, 2, False)]), int(os.environ.get('KL0W', 3)))
            for t in range(4):
                tsl = slice(t * 128, (t + 1) * 128)
                tok0 = b * 512 + t * 128
                pv = self.nxt(psA)
                for kc in range(8):
                    self.mm(pv[:, 0:128], h[:, kc, tsl], wv1[:, kc, 128:256], kc == 0, kc == 7, [h.r, wt1.r], [pv.r])
                vs = self.nxt(vst)
                if samp:
                    self.cp("act", vs[:, 0:128], pv[:, 0:128], [pv.r], [vs.r])
                else:
                    nv = self.nxt(nvst)
                    self.cp("dve", nv[:, 0:128], pv[:, 0:128], [pv.r], [nv.r])
                    self.ld(nv_d[tok0:tok0 + 128, 0:128], nv[:, 0:128], nv.r, reads=[nv.r], writes=[Res("nvo")])
                    self.cp("act", vs[:, 0:128], nv[:, 0:128], [nv.r], [vs.r])
                self.ld(V0[tok0:tok0 + 128, 0:128], vs[:, 0:128], vs.r, reads=[vs.r], writes=[rV0[b]])
                if not samp:
                    pk = self.nxt(psA)
                    for kc in range(8):
                        self.mm(pk[:, 0:128], h[:, kc, tsl], wv1[:, kc, 0:128], kc == 0, kc == 7, [h.r, wt1.r], [pk.r])
                    k_norm_tm(pk, 128, 0, tok0)
            wt2, wv2 = load_w(Wb[0]["w_in"], rWb[0]["w_in"], 8, 1024, 512)
            self.interleave(iter([l0_fm_chunk(b, h, (wv2, wt2), 0, 7, 2, False), l0_fm_chunk(b, h, (wv2, wt2), 1, 8, 2, False),
                                  l0_fm_chunk(b, h, (wv2, wt2), 2, 9, 3, False), l0_fm_chunk(b, h, (wv2, wt2), 3, 10, 3, False)]), int(os.environ.get('KL0W', 3)))
            wt3, wv3 = load_w(Wb[0]["w_in"], rWb[0]["w_in"], 8, 1536, 512)
            self.interleave(iter([l0_fm_chunk(b, h, (wv3, wt3), 0, 11, 3, False), l0_fm_chunk(b, h, (wv3, wt3), 1, 12, 3, False)]), int(os.environ.get('KL0W', 3)))
            if not samp:
                for t in range(4):
                    tsl = slice(t * 128, (t + 1) * 128)
                    tok0 = b * 512 + t * 128
                    pk2 = self.nxt(psA)
                    for kc in range(8):
                        self.mm(pk2[:, 0:256], h[:, kc, tsl], wv2[:, kc, 256:512], kc == 0, kc == 7, [h.r, wt2.r], [pk2.r])
                    for kc in range(8):
                        self.mm(pk2[:, 256:512], h[:, kc, tsl], wv3[:, kc, 0:256], kc == 0, kc == 7, [h.r, wt3.r], [pk2.r])
                    k_norm_tm(pk2, 512, 128, tok0)
            wt4, wv4 = load_w(Wb[0]["w_in"], rWb[0]["w_in"], 8, 2048, 256)
            for t in range(4):
                tsl = slice(t * 128, (t + 1) * 128)
                tok0 = b * 512 + t * 128
                pv2 = self.nxt(psA)
                for kc in range(8):
                    self.mm(pv2[:, 0:256], h[:, kc, tsl], wv3[:, kc, 256:512], kc == 0, kc == 7, [h.r, wt3.r], [pv2.r])
                for kc in range(8):
                    self.mm(pv2[:, 256:512], h[:, kc, tsl], wv4[:, kc, 0:256], kc == 0, kc == 7, [h.r, wt4.r], [pv2.r])
                vs = self.nxt(vst)
                if samp:
                    self.cp("act", vs[:, 128:640], pv2[:, :], [pv2.r], [vs.r])
                else:
                    nv = self.nxt(nvst)
                    self.cp("dve", nv[:, 128:640], pv2[:, :], [pv2.r], [nv.r])
                    self.ld(nv_d[tok0:tok0 + 128, 128:640], nv[:, 128:640], nv.r, reads=[nv.r], writes=[Res("nvo")])
                    self.cp("act", vs[:, 128:640], nv[:, 128:640], [nv.r], [vs.r])
                self.ld(V0[tok0:tok0 + 128, 128:640], vs[:, 128:640], vs.r, reads=[vs.r], writes=[rV0[b]])
        barrier()

        if stage >= 4:
            o = 0
            ck_sb, o = aview(o, [128, 2, 640], F32, "ck_sb")
            kct, o = aview(o, [128, 5, 256], BF16, "kct")
            self.ld(ck_sb[:, :, :], ck_d.rearrange("(t p) c -> p t c", p=128), ck_sb.r, writes=[ck_sb.r])
            for ci in range(5):
                pt = self.nxt(psA)
                for t in range(2):
                    self.tr(pt[:, t * 128:(t + 1) * 128], ck_sb[:, t, ci * 128:(ci + 1) * 128], ident_f[:, :],
                            [ck_sb.r, ident_f.r], [pt.r])
                self.cp("dve", kct[:, ci, :], pt[:, 0:256], [pt.r], [kct.r])
            self.ld(KC.rearrange("c p t -> p c t"), kct[:, :, :], kct.r, reads=[kct.r], writes=[rKC])
            barrier()

        def attn_unit(q_ap, k_fn, v_fn, nkc, nq, half, out_ap, rds, out_r, po=None, po_off=0, norm=True):
            psx = self.nxt(psA)
            for kc in range(nkc):
                self.mm(psx[:, kc * nq:(kc + 1) * nq], k_fn(kc), q_ap, True, True, rds, [psx.r])
            pT = self.nxt(btmp)
            self.act(pT[:, 0:nkc * nq], psx[:, 0:nkc * nq], AF.Exp, [psx.r], [pT.r], scale=0.125)
            if po is None:
                po = self.nxt(psB)
            for kc in range(nkc):
                self.mm(po[:, po_off:po_off + nq], v_fn(kc), pT[:, kc * nq:(kc + 1) * nq], kc == 0, kc == nkc - 1,
                        rds + [pT.r], [po.r])
            if norm:
                normalize(po, po_off, nq, half, out_ap, out_r)
            return po

        def normalize(po, po_off, nq, half, out_ap, out_r):
            rc = self.nxt(rtmp)
            if half == 0:
                self.recip(rc[0:64, 0:nq], po[64:128, po_off:po_off + nq], [po.r], [rc.r])
                self.tt("dve", out_ap, po[0:64, po_off:po_off + nq], rc[0:64, 0:nq], ALU.mult, [po.r, rc.r], [out_r])
            else:
                self.recip(rc[64:128, 0:nq], po[0:64, po_off:po_off + nq], [po.r], [rc.r])
                self.tt("dve", out_ap, po[64:128, po_off:po_off + nq], rc[64:128, 0:nq], ALU.mult, [po.r, rc.r], [out_r])

        if stage >= 4:
            o = 0
            qk2 = []
            for i in range(2):
                t_, o = aview(o, [128, 13, 256], BF16, "qk%d" % i)
                qk2.append(t_)
            ka2 = []
            for i in range(2):
                t_, o = aview(o, [128, 2, 256], BF16, "ka2_%d" % i)
                ka2.append(t_)
            vaE, vaO = [], []
            for i in range(2):
                t_, o = aview(o, [128, 2, 10, 128], BF16, "vaE%d" % i)
                vaE.append(t_)
                t_, o = aview(o, [128, 2, 10, 128], BF16, "vaO%d" % i)
                vaO.append(t_)
            ostp = []
            for i in range(2):
                t_, o = aview(o, [128, 8, 256], BF16, "ostp%d" % i)
                ostp.append(t_)
            for i in range(2):
                self.memset("pool", vaE[i][:, :, :, 64:128], 1.0, [vaE[i].r])
                self.memset("pool", vaO[i][:, :, :, 0:64], 1.0, [vaO[i].r])
            QKTv = QKT.rearrange("c p t -> p c t")
            for s_ in range(4):
                tb = s_ * 256
                b = s_ // 2
                qk = qk2[s_ % 2]
                k2 = ka2[s_ % 2]
                vE, vO = vaE[s_ % 2], vaO[s_ % 2]
                ost = ostp[s_ % 2]
                self.ld(qk[:, :, :], QKTv[:, :, tb:tb + 256], qk.r, reads=[rQKT[c_][b] for c_ in range(13)], writes=[qk.r])
                for g in range(2):
                    for hf in range(2):
                        self.ld(k2[hf * 64:(hf + 1) * 64, g, :], QKT[4, g * 64:(g + 1) * 64, tb:tb + 256], k2.r,
                                reads=[rQKT[4][b]], writes=[k2.r])
                for t in range(2):
                    src = V0[tb + t * 128:tb + (t + 1) * 128, :].rearrange("p (h d) -> p h d", d=64)
                    self.ld(vE[:, t, :, 0:64], src, vE.r, reads=[rV0[b]], writes=[vE.r])
                    self.ld(vO[:, t, :, 64:128], src, vO.r, reads=[rV0[b]], writes=[vO.r])
                for hd in range(16):
                    isA = hd < 8
                    h_ = hd if isA else hd - 8
                    hf = h_ % 2
                    psl = slice(hf * 64, (hf + 1) * 64)
                    va = vE if hf == 0 else vO
                    if isA:
                        g = h_ // 4
                        q_ap = qk[psl, h_ // 2, :]
                        k_fn = (lambda kc, g=g, psl=psl: k2[psl, g, kc * 128:(kc + 1) * 128])
                        v_fn = (lambda kc, g=g, va=va: va[:, kc, g, :])
                        rds = [qk.r, k2.r, va.r]
                    else:
                        q_ap = qk[psl, 5 + h_ // 2, :]
                        k_fn = (lambda kc, h_=h_, psl=psl: qk[psl, 9 + h_ // 2, kc * 128:(kc + 1) * 128])
                        v_fn = (lambda kc, h_=h_, va=va: va[:, kc, 2 + h_, :])
                        rds = [qk.r, va.r]
                    och = (h_ // 2) if isA else 4 + h_ // 2
                    attn_unit(q_ap, k_fn, v_fn, 2, 256, hf, ost[psl, och, :], rds, ost.r)
                self.ld(OT.rearrange("c p t -> p c t")[:, :, tb:tb + 256], ost[:, :, :], ost.r, reads=[ost.r],
                        writes=[rOT[c_][b] for c_ in range(8)])
            barrier()

        if stage >= 5:
            o = 0
            ka2s, o = aview(o, [128, 2, 2304], BF16, "ka2s")
            vEA, o = aview(o, [128, 18, 2, 128], BF16, "vEA")
            vOA, o = aview(o, [128, 18, 2, 128], BF16, "vOA")
            qa2 = []
            osa = []
            for i in range(2):
                t_, o = aview(o, [128, 2048], BF16, "qa%d" % i)
                qa2.append(t_)
                t_, o = aview(o, [128, 2048], BF16, "osa%d" % i)
                osa.append(t_)
            self.memset("pool", vEA[:, :, :, 64:128], 1.0, [vEA.r])
            self.memset("pool", vOA[:, :, :, 0:64], 1.0, [vOA.r])
            allq = [rQKT[4][b_] for b_ in range(2, 6)]
            for g in range(2):
                for hf in range(2):
                    self.ld(ka2s[hf * 64:(hf + 1) * 64, g, 0:2048], QKT[4, g * 64:(g + 1) * 64, 1024:3072], ka2s.r,
                            reads=allq, writes=[ka2s.r])
                    self.ld(ka2s[hf * 64:(hf + 1) * 64, g, 2048:2304], KC[0, g * 64:(g + 1) * 64, :], ka2s.r,
                            reads=[rKC], writes=[ka2s.r])
                srcv = V0[1024:3072, g * 64:(g + 1) * 64].rearrange("(t p) d -> p t d", p=128)
                self.ld(vEA[:, 0:16, g, 0:64], srcv, vEA.r, reads=rV0[2:6], writes=[vEA.r])
                self.ld(vOA[:, 0:16, g, 64:128], srcv, vOA.r, reads=rV0[2:6], writes=[vOA.r])
                srcc = cv_d[:, g * 64:(g + 1) * 64].rearrange("(t p) d -> p t d", p=128)
                self.ld(vEA[:, 16:18, g, 0:64], srcc, vEA.r, writes=[vEA.r], q="pool")
                self.ld(vOA[:, 16:18, g, 64:128], srcc, vOA.r, writes=[vOA.r], q="pool")
            for hp in range(4):
                qa = qa2[hp % 2]
                os_ = osa[hp % 2]
                self.ld(qa[:, :], QKT[hp, :, 1024:3072], qa.r, reads=[rQKT[hp][b_] for b_ in range(2, 6)], writes=[qa.r])
                for hf in range(2):
                    h_ = hp * 2 + hf
                    g = h_ // 4
                    psl = slice(hf * 64, (hf + 1) * 64)
                    va = vEA if hf == 0 else vOA
                    for qb_i in range(4):
                        po = self.nxt(psB)

                        def s_mm(kc, g=g, psl=psl, qb_i=qb_i, qa=qa):
                            psx = self.nxt(psA)
                            self.mm(psx[:, :], ka2s[psl, g, kc * 128:(kc + 1) * 128], qa[psl, qb_i * 512:(qb_i + 1) * 512],
                                    True, True, [ka2s.r, qa.r], [psx.r])
                            return psx
                        pend = [s_mm(0), s_mm(1)]
                        for kc in range(18):
                            psx = pend.pop(0)
                            if kc + 2 < 18:
                                pend.append(s_mm(kc + 2))
                            pT = self.nxt(btmp)
                            self.act(pT[:, :], psx[:, :], AF.Exp, [psx.r], [pT.r], scale=0.125)
                            self.mm(po[:, :], va[:, kc, g, :], pT[:, :], kc == 0, kc == 17, [va.r, pT.r], [po.r])
                        normalize(po, 0, 512, hf, os_[psl, qb_i * 512:(qb_i + 1) * 512], os_.r)
                self.ld(OT[hp, :, 1024:3072], os_[:, :], os_.r, reads=[os_.r], writes=[rOT[hp][b_] for b_ in range(2, 6)])
            barrier()

        if stage >= 6:
            o = 0
            qb2, kb2, osb, bia = [], [], [], []
            for i in range(2):
                t_, o = aview(o, [128, 2048], BF16, "qb%d" % i)
                qb2.append(t_)
                t_, o = aview(o, [128, 2304], BF16, "kb%d" % i)
                kb2.append(t_)
                t_, o = aview(o, [128, 2048], BF16, "osb%d" % i)
                osb.append(t_)
            bia_t, o = aview(o, [128, 2048], F32, "bia")
            vsets = []
            for hf in range(2):
                e_, o = aview(o, [128, 16, 128], BF16, "vbE%d" % hf)
                d_, o = aview(o, [128, 15, 128], BF16, "vbOd%d" % hf)
                c_, o = aview(o, [128, 2, 128], BF16, "vbC%d" % hf)
                vsets.append((e_, d_, c_))
                onesl = slice(64, 128) if hf == 0 else slice(0, 64)
                for t_ in (e_, d_, c_):
                    self.memset("pool", t_[:, :, onesl], 1.0, [t_.r])
            for hp in range(4):
                qb_t = qb2[hp % 2]
                kb_t = kb2[hp % 2]
                os_ = osb[hp % 2]
                rq = [rQKT[5 + hp][b_] for b_ in range(2, 6)]
                rk = [rQKT[9 + hp][b_] for b_ in range(2, 6)]
                self.ld(qb_t[:, :], QKT[5 + hp, :, 1024:3072], qb_t.r, reads=rq, writes=[qb_t.r])
                self.ld(kb_t[:, 0:2048], QKT[9 + hp, :, 1024:3072], kb_t.r, reads=rk, writes=[kb_t.r])
                self.ld(kb_t[:, 2048:2304], KC[1 + hp, :, :], kb_t.r, reads=[rKC], writes=[kb_t.r])
                for hf in range(2):
                    h_ = hp * 2 + hf
                    psl = slice(hf * 64, (hf + 1) * 64)
                    vsl = slice(0, 64) if hf == 0 else slice(64, 128)
                    vE_, vD_, vC_ = vsets[hf]
                    c0 = 128 + h_ * 64
                    self.ld(vE_[:, :, vsl], V0[1024:3072, c0:c0 + 64].rearrange("(t p) d -> p t d", p=128), vE_.r,
                            reads=rV0[2:6], writes=[vE_.r])
                    self.ld(vD_[:, :, vsl], V0[1088:3008, c0:c0 + 64].rearrange("(t p) d -> p t d", p=128), vD_.r,
                            reads=rV0[2:6], writes=[vD_.r])
                    self.ld(vC_[:, :, vsl], cv_d[:, c0:c0 + 64].rearrange("(t p) d -> p t d", p=128), vC_.r,
                            writes=[vC_.r], q="pool")
                    self.ld(bia_t[:, :], bias_d[h_], bia_t.r, writes=[bia_t.r])
                    po = None

                    def b_scores(r, psl=psl, qb_t=qb_t, kb_t=kb_t):
                        rs = min(max(r - 4, 0), 24)
                        psx = self.nxt(psA)
                        qv = qb_t[psl, r * 64:(r + 1) * 64]
                        for i in range(4):
                            k0 = (rs + 2 * i) * 64
                            self.mm(psx[:, i * 64:(i + 1) * 64], kb_t[psl, k0:k0 + 128], qv, True, True, [kb_t.r, qb_t.r], [psx.r])
                        for c_ in range(2):
                            self.mm(psx[:, 256 + c_ * 64:256 + (c_ + 1) * 64], kb_t[psl, 2048 + c_ * 128:2048 + (c_ + 1) * 128], qv,
                                    True, True, [kb_t.r, qb_t.r], [psx.r])
                        return psx
                    pend = [b_scores(0), b_scores(1)]
                    for r in range(32):
                        rs = min(max(r - 4, 0), 24)
                        cl = r - rs
                        psx = pend.pop(0)
                        if r + 2 < 32:
                            pend.append(b_scores(r + 2))
                        sb_ = self.nxt(ftmp)
                        self.stt(sb_[:, 0:256], psx[:, 0:256], 0.125, bia_t[:, cl * 256:(cl + 1) * 256], ALU.mult, ALU.add,
                                 [psx.r, bia_t.r], [sb_.r])
                        pT = self.nxt(btmp)
                        self.act(pT[:, 0:256], sb_[:, 0:256], AF.Exp, [sb_.r], [pT.r])
                        self.act(pT[:, 256:384], psx[:, 256:384], AF.Exp, [psx.r], [pT.r], scale=0.125)
                        if r % 8 == 0:
                            po = self.nxt(psB)
                        off = (r % 8) * 64
                        for i in range(4):
                            rr_ = rs + 2 * i
                            vt = vE_[:, rr_ // 2, :] if rs % 2 == 0 else vD_[:, (rr_ - 1) // 2, :]
                            vr = vE_.r if rs % 2 == 0 else vD_.r
                            self.mm(po[:, off:off + 64], vt, pT[:, i * 64:(i + 1) * 64], i == 0, False, [vr, pT.r], [po.r])
                        for c_ in range(2):
                            self.mm(po[:, off:off + 64], vC_[:, c_, :], pT[:, 256 + c_ * 64:256 + (c_ + 1) * 64], False, c_ == 1,
                                    [vC_.r, pT.r], [po.r])
                        if r % 8 == 7:
                            r0 = r - 7
                            normalize(po, 0, 512, hf, os_[psl, r0 * 64:(r0 + 8) * 64], os_.r)
                self.ld(OT[4 + hp, :, 1024:3072], os_[:, :], os_.r, reads=[os_.r], writes=[rOT[4 + hp][b_] for b_ in range(2, 6)])
            barrier()

        L1 = stage >= 20
        if stage >= 7:
            for b in range(NB):
                x = load_x(xT_d, rXin, b)
                out_proj_resid(0, x, b, OT, rOT)
                if stage >= 8:
                    resid_mlp(0, x, b)
                store_x(x, X1 if L1 else yT_d, rX1 if L1 else rY, b)
            barrier()
        else:
            for b in range(NB):
                x = load_x(xT_d, rXin, b)
                store_x(x, yT_d, rY, b)

        if L1:
            NCH = 48
            QKV1 = self.dscr("QKV1", [24, 128, NTOK])
            ZT = self.dscr("ZT", [8, 128, NTOK])
            QN = self.dscr("QN", [8, 128, NTOK])
            KN = self.dscr("KN", [8, 128, NTOK])
            KTM = self.dscr("KTM", [NTOK, 8, 128])
            VTM = self.dscr("VTM", [NTOK, 8, 128])
            TBS = self.dscr("TBS", [NCH, 128, 8, 64])
            ATT = self.dscr("ATT", [NCH, 128, 8, 64])
            QG = self.dscr("QG", [2, NCH, 128, 8, 64])
            OTF = self.dscr("OTF", [2, 8, 128, NTOK])
            rQKV1 = [[Res("rQKV1") for _ in range(NB)] for _ in range(24)]
            rZT = [[Res("rZT") for _ in range(NB)] for _ in range(8)]
            rQN = [Res("rQN%d" % i) for i in range(8)]
            rKN = [Res("rKN%d" % i) for i in range(8)]
            rKTM, rVTM = Res("rKTM"), Res("rVTM")
            rTBS = [Res("rTBS") for _ in range(NCH)]
            rATT = [Res("rATT") for _ in range(NCH)]
            rQG = [[Res("rQG") for _ in range(NCH)] for _ in range(2)]
            rOTF = [[Res("rOTF") for _ in range(NCH)] for _ in range(2)]

            l1c = self.sb("l1c", [128, 1024], F32)
            self.ld(l1c[:, :], l1c_d, l1c.r, writes=[l1c.r])
            tri2 = l1c[:, 0:64]
            triT2 = l1c[:, 64:128]
            ident2 = l1c[:, 128:192]
            nmincl = l1c[:, 192:256]
            mstrict = l1c[:, 256:320]
            nmstrT = l1c[:, 320:384]
            bdones = l1c[:, 384:512]
            sel0 = l1c[:, 512:640]
            sel1 = l1c[:, 640:768]
            onesf = l1c[:, 768:896]
            identf = l1c[:, 896:1024]
            l1m = self.sb("l1m", [128, 6, 64], F32)
            self.ld(l1m[:, :, :], l1m_d.rearrange("p (l q) -> p l q", q=64), l1m.r, writes=[l1m.r])
            convw = self.sb("convw", [128, 24, 3], F32)
            self.ld(convw[:, :, :], convw_d.rearrange("p (c j) -> p c j", j=3), convw.r, writes=[convw.r])
            nexpA = self.sb("nexpA", [128, 16], F32)
            dtb = self.sb("dtb", [128, 16], F32)
            onorm = self.sb("onorm", [128, 1], F32)
            onec = self.sb("onec", [128, 1], F32)
            self.memset("dve", onec[:, :], 1.0, [onec.r])
            self.ld(nexpA[:, :], alog_d, nexpA.r, writes=[nexpA.r])
            self.ld(dtb[:, :], dtb_d, dtb.r, writes=[dtb.r])
            self.ld(onorm[:, :], onorm_d, onorm.r, writes=[onorm.r])
            self.act(nexpA[:, :], nexpA[:, :], AF.Exp, [nexpA.r], [nexpA.r])
            self.ts("dve", nexpA[:, :], nexpA[:, :], -1.0, None, ALU.mult, None, [nexpA.r], [nexpA.r])
            la_tm = self.sb("la_tm", [128, 24, 16], F32)
            be_tm = self.sb("be_tm", [128, 24, 16], F32)
            negeg = self.sb("negeg", [128, NCH, 8], F32)
            ksc = self.sb("ksc", [128, NCH, 8], F32)
            egl = self.sb("egl", [128, NCH, 16], F32)

            modulation(1)

            def a1_block(b):
                x = load_x(X1, rX1, b)
                h = norm_mod(x, b, G1, 0, pa=ps[4])
                sl = slice(b * 512, (b + 1) * 512)
                for grp in range(8):
                    wt, wv = load_w(Wb[1]["w_in"], rWb[1]["w_in"], 8, grp * 512, 512)
                    for j in range(4):
                        c = grp * 4 + j
                        pm = self.nxt(psA)
                        for kc in range(8):
                            self.mm(pm[:, :], wv[:, kc, j * 128:(j + 1) * 128], h[:, kc, :], kc == 0, kc == 7, [wt.r, h.r], [pm.r])
                        ob = self.nxt(ostg)
                        if c < 24:
                            self.cp("act", ob[:, :], pm[:, :], [pm.r], [ob.r])
                            self.ld(QKV1[c, :, sl], ob[:, :], ob.r, reads=[ob.r], writes=[rQKV1[c][b]])
                        else:
                            self.act(ob[:, :], pm[:, :], AF.Silu, [pm.r], [ob.r])
                            self.ld(ZT[c - 24, :, sl], ob[:, :], ob.r, reads=[ob.r], writes=[rZT[c - 24][b]])
                        yield
                wt, wv = load_w(Wb[1]["w_in"], rWb[1]["w_in"], 8, 4096, 32)
                for t in range(4):
                    tt_ = b * 4 + t
                    tsl = slice(t * 128, (t + 1) * 128)
                    pab = self.nxt(psA)
                    for kc in range(8):
                        self.mm(pab[:, 0:32], h[:, kc, tsl], wv[:, kc, 0:32], kc == 0, kc == 7, [h.r, wt.r], [pab.r])
                    t1 = self.nxt(rtmp)
                    self.tt("dve", t1[:, 0:16], pab[:, 0:16], dtb[:, :], ALU.add, [pab.r, dtb.r], [t1.r])
                    self.act(t1[:, 16:32], t1[:, 0:16], AF.Exp, [t1.r], [t1.r])
                    self.act(t1[:, 32:48], t1[:, 16:32], AF.Ln, [t1.r, onec.r], [t1.r], bias=onec[:, 0:1])
                    self.tt("dve", la_tm[:, tt_, :], t1[:, 32:48], nexpA[:, :], ALU.mult, [t1.r, nexpA.r], [la_tm.r])
                    self.act(be_tm[:, tt_, :], pab[:, 16:32], AF.Sigmoid, [pab.r], [be_tm.r])
                    yield

            if stage >= 21:
                o = 0
                diagW, o = aview(o, [128, 72, 128], BF16, "diagW")
                raws, sfps, tmbs, kns = [], [], [], []
                for i in range(5):
                    t_, o = aview(o, [128, 514], BF16, "raw%d" % i)
                    raws.append(t_)
                    t_, o = aview(o, [128, 512], F32, "sfp%d" % i)
                    sfps.append(t_)
                    t_, o = aview(o, [128, 4, 128], BF16, "tmb%d" % i)
                    tmbs.append(t_)
                    t_, o = aview(o, [128, 512], F32, "kn%d" % i)
                    kns.append(t_)
                for c in range(24):
                    for j in range(3):
                        self.ts("dve", diagW[:, c * 3 + j, :], identf, convw[:, c, j:j + 1], None, ALU.mult, None,
                                [l1c.r, convw.r], [diagW.r])
                pieces = [(0, 256, 0, 256), (256, 256, 256, 256), (512, 256, 512, 256), (768, 256, 768, 256)]
                pieces += [(1024 + i * 512, 512, 1024, 2048) for i in range(4)]
                def b1_unit(p0, L, s0, Tq, c):
                        bl = p0 // 512
                        raw = self.nxt(raws)
                        sfp = self.nxt(sfps)
                        lo = max(p0 - 1, s0)
                        hi = min(p0 + L + 1, s0 + Tq)
                        rdeps = [rQKV1[c][bb] for bb in range(max(bl - 1, 0), min(bl + 2, NB))]
                        if lo > p0 - 1:
                            self.memset("pool", raw[:, 0:1], 0.0, [raw.r])
                        if hi < p0 + L + 1:
                            self.memset("pool", raw[:, L + 1:L + 2], 0.0, [raw.r])
                        self.ld(raw[:, lo - (p0 - 1):hi - (p0 - 1)], QKV1[c, :, lo:hi], raw.r, reads=rdeps, writes=[raw.r])
                        pc_ = self.nxt(psA)
                        for j in range(3):
                            self.mm(pc_[:, 0:L], diagW[:, c * 3 + j, :], raw[:, j:j + L], j == 0, j == 2, [diagW.r, raw.r], [pc_.r])
                        self.act(sfp[:, 0:L], pc_[:, 0:L], AF.Silu, [pc_.r], [sfp.r])
                        yield
                        src_tm = sfp
                        if c < 16:
                            sq = self.nxt(btmp)
                            self.tt("pool", sq[:, 0:L], sfp[:, 0:L], sfp[:, 0:L], ALU.mult, [sfp.r], [sq.r])
                            pn = self.nxt(psB1)
                            self.mm(pn[:, 0:L], ones_bf[:, :], sq[:, 0:L], True, True, [ones_bf.r, sq.r], [pn.r])
                            rt = self.nxt(rtmp)
                            self.act(rt[:, 0:L], pn[:, 0:L], AF.Ln, [pn.r, epsc.r], [rt.r], bias=epsc[:, 0:1])
                            rr = self.nxt(rtmp)
                            self.act(rr[:, 0:L], rt[:, 0:L], AF.Exp, [rt.r], [rr.r], scale=-0.5)
                            ob = self.nxt(ostg)
                            if c < 8:
                                self.stt(ob[:, 0:L], sfp[:, 0:L], 128.0 ** -0.5, rr[:, 0:L], ALU.mult, ALU.mult, [sfp.r, rr.r], [ob.r])
                                self.ld(QN[c, :, p0:p0 + L], ob[:, 0:L], ob.r, reads=[ob.r], writes=[rQN[c]])
                            else:
                                kn = self.nxt(kns)
                                self.tt("dve", kn[:, 0:L], sfp[:, 0:L], rr[:, 0:L], ALU.mult, [sfp.r, rr.r], [kn.r])
                                self.cp("pool", ob[:, 0:L], kn[:, 0:L], [kn.r], [ob.r])
                                self.ld(KN[c - 8, :, p0:p0 + L], ob[:, 0:L], ob.r, reads=[ob.r], writes=[rKN[c - 8]])
                                src_tm = kn
                        yield
                        if c >= 8:
                            hh = (c - 8) % 8
                            dst, rdst = (KTM, rKTM) if c < 16 else (VTM, rVTM)
                            pt = self.nxt(psA)
                            nt = L // 128
                            for i in range(nt):
                                self.tr(pt[:, i * 128:(i + 1) * 128], src_tm[:, i * 128:(i + 1) * 128], identf,
                                        [src_tm.r, l1c.r], [pt.r])
                            tmb = self.nxt(tmbs)
                            self.cp("act", tmb[:, 0:nt, :], pt[:, 0:L].rearrange("p (i d) -> p i d", d=128), [pt.r], [tmb.r])
                            self.ld(dst[p0:p0 + L, hh, :].rearrange("(i p) d -> p i d", p=128), tmb[:, 0:nt, :], tmb.r,
                                    reads=[tmb.r], writes=[rdst])
                def units(pidx):
                    return (b1_unit(pieces[pi][0], pieces[pi][1], pieces[pi][2], pieces[pi][3], c) for pi in pidx for c in range(24))

                self.interleave2(a1_block(0), iter([]), 3)
                self.interleave2(a1_block(1), iter([]), 3)
                self.interleave2(a1_block(2), units([0, 1]), 4)
                self.interleave2(a1_block(3), units([2, 3]), 4)
                self.interleave2(a1_block(4), units([4]), 4)
                self.interleave2(a1_block(5), units([5]), 4)
                self.interleave(units([6, 7]), 4)
                barrier()

            if stage >= 22:
                o = 0
                NSET = 3
                kTp, qTp = [], []
                for i in range(1):
                    t_, o = aview(o, [128, 8, 512], BF16, "kTp%d" % i)
                    kTp.append(t_)
                    t_, o = aview(o, [128, 8, 512], BF16, "qTp%d" % i)
                    qTp.append(t_)
                sets = []
                for i in range(NSET):
                    d_ = {}
                    for nm in ("b0", "b1", "b2", "b3", "b4"):
                        d_[nm], o = aview(o, [128, 8, 64], F32, "%s_%d" % (nm, i))
                    d_["eg"] = d_["b2"]
                    for nm in ("P0", "Q0", "T", "U", "X", "Xp", "attb", "tbb", "qg0", "qg1"):
                        d_[nm], o = aview(o, [128, 8, 64], BF16, "%s_%d" % (nm, i))
                    d_["g2"], o = aview(o, [128, 8], F32, "g2_%d" % i)
                    d_["be2"], o = aview(o, [128, 8], F32, "be2_%d" % i)
                    d_["sm"], o = aview(o, [128, 16], F32, "sm_%d" % i)
                    sets.append(d_)

                def bc_h(ap2):
                    return ap2.unsqueeze(2).to_broadcast([128, 8, 64])

                def bc_m(ap2):
                    return ap2.unsqueeze(1).to_broadcast([128, 8, 64])

                def v3(bank):
                    return bank[:, :].rearrange("p (h i) -> p h i", i=64)

                DS = (slice(0, 64), slice(64, 128))
                for pc in range(6):
                    kT = self.nxt(kTp)
                    qT = self.nxt(qTp)
                    self.ld(kT[:, :, :], KN.rearrange("h p t -> p h t")[:, :, pc * 512:(pc + 1) * 512], kT.r, reads=rKN, writes=[kT.r])
                    self.ld(qT[:, :, :], QN.rearrange("h p t -> p h t")[:, :, pc * 512:(pc + 1) * 512], qT.r, reads=rQN, writes=[qT.r])
                    def chunk_gen(ci, pc=pc, kT=kT, qT=qT):
                        cg = pc * 8 + ci
                        W = sets[ci % NSET] if ci < 6 else sets[ci - 6]
                        g2, be2, sm = W["g2"], W["be2"], W["sm"]
                        tt_ = cg // 2
                        hb = (cg % 2) * 64
                        hsl = slice(hb, hb + 64)
                        csl = slice(ci * 64, (ci + 1) * 64)
                        la_c = la_tm[hsl, tt_, :]
                        pg = self.nxt(ps)
                        self.mm(pg[0:64, 0:8], tri2[hsl, :], la_c[:, 0:8], True, True, [l1c.r, la_tm.r], [pg.r])
                        self.mm(pg[64:128, 0:8], triT2[hsl, :], la_c[:, 8:16], True, True, [l1c.r, la_tm.r], [pg.r])
                        self.mm(pg[:, 16:32], onesf[hsl, :], la_c, True, True, [l1c.r, la_tm.r], [pg.r])
                        self.cp("dve", g2[:, :], pg[:, 0:8], [pg.r], [g2.r])
                        self.act(egl[:, cg, :], pg[:, 16:32], AF.Exp, [pg.r], [egl.r])
                        self.tt("dve", sm[0:64, 0:8], pg[0:64, 16:24], g2[0:64, :], ALU.subtract, [pg.r, g2.r], [sm.r])
                        self.tt("dve", sm[64:128, 0:8], pg[64:128, 24:32], g2[64:128, :], ALU.subtract, [pg.r, g2.r], [sm.r])
                        self.act(ksc[:, cg, :], sm[:, 0:8], AF.Exp, [sm.r], [ksc.r])
                        self.act(sm[:, 8:16], g2[:, :], AF.Exp, [g2.r], [sm.r])
                        self.ts("pool", negeg[:, cg, :], sm[:, 8:16], -1.0, None, ALU.mult, None, [sm.r], [negeg.r])
                        self.cp("act", be2[0:64, :], be_tm[hsl, tt_, 0:8], [be_tm.r], [be2.r])
                        self.cp("act", be2[64:128, :], be_tm[hsl, tt_, 8:16], [be_tm.r], [be2.r])
                        yield
                        dG, dB = W["b0"], W["b1"]
                        self.tt("dve", dG[:, :, :], bc_m(ident2), bc_h(g2[:, :]), ALU.mult, [l1c.r, g2.r], [dG.r])
                        self.tt("dve", dB[:, :, :], bc_m(ident2), bc_h(be2[:, :]), ALU.mult, [l1c.r, be2.r], [dB.r])
                        dGf = dG[:, :, :].rearrange("p h i -> p (h i)")
                        dBf = dB[:, :, :].rearrange("p h i -> p (h i)")
                        for d in range(2):
                            pe_ = self.nxt(ps)
                            self.mm(pe_[:, :], sel0 if d == 0 else sel1, dGf, True, True, [l1c.r, dG.r], [pe_.r])
                            eg = W["eg"]
                            self.act(eg[:, :, :], v3(pe_), AF.Exp, [pe_.r], [eg.r])
                            qg_ = W["qg%d" % d]
                            self.tt("pool", qg_[:, :, :], qT[:, :, csl], eg[:, :, :], ALU.mult, [qT.r, eg.r], [qg_.r])
                            self.ld(QG[d, cg], qg_[:, :, :], qg_.r, reads=[qg_.r], writes=[rQG[d][cg]])
                        yield
                        pGr = self.nxt(ps)
                        pBr = self.nxt(ps)
                        self.mm(pGr[:, :], bdones, dGf, True, True, [l1c.r, dG.r], [pGr.r])
                        self.mm(pBr[:, :], bdones, dBf, True, True, [l1c.r, dB.r], [pBr.r])
                        E, E1, DTi = W["b2"], W["b3"], W["b4"]
                        self.tt("dve", E[:, :, :], v3(pGr), bc_h(g2[:, :]), ALU.subtract, [pGr.r, g2.r], [E.r])
                        self.tt("pool", E1[:, :, :], E[:, :, :], bc_m(nmincl), ALU.add, [E.r, l1c.r], [E1.r])
                        self.act(DTi[:, :, :], E1[:, :, :], AF.Exp, [E1.r], [DTi.r])
                        E2 = W["b3"]
                        self.tt("pool", E2[:, :, :], bc_m(nmstrT), E[:, :, :], ALU.subtract, [E.r, l1c.r, DTi.r], [E2.r])
                        Dm = W["b0"]
                        self.act(Dm[:, :, :], E2[:, :, :], AF.Exp, [E2.r], [Dm.r])
                        self.tt("pool", Dm[:, :, :], Dm[:, :, :], bc_h(be2[:, :]), ALU.mult, [Dm.r, be2.r], [Dm.r])
                        BrM = W["b1"]
                        self.tt("dve", BrM[:, :, :], v3(pBr), bc_m(mstrict), ALU.mult, [pBr.r, l1c.r], [BrM.r])
                        self.tt("pool", BrM[:, :, :], BrM[:, :, :], DTi[:, :, :], ALU.mult, [BrM.r, DTi.r], [BrM.r])
                        pkk = self.nxt(ps)
                        pqk = self.nxt(ps)
                        for hh in range(8):
                            for d in range(2):
                                self.mm(pkk[DS[d], hh * 64:(hh + 1) * 64], kT[:, hh, csl], kT[:, hh, csl], True, True, [kT.r], [pkk.r])
                        for hh in range(8):
                            for d in range(2):
                                self.mm(pqk[DS[d], hh * 64:(hh + 1) * 64], kT[:, hh, csl], qT[:, hh, csl], True, True, [kT.r, qT.r], [pqk.r])
                        P0, Q0, Tm, Um, Xm, Xpm = W["P0"], W["Q0"], W["T"], W["U"], W["X"], W["Xp"]
                        attb, tbb = W["attb"], W["tbb"]
                        self.tt("dve", P0[:, :, :], v3(pkk), Dm[:, :, :], ALU.mult, [pkk.r, Dm.r], [P0.r])
                        self.tt("dve", Q0[:, :, :], v3(pkk), BrM[:, :, :], ALU.mult, [pkk.r, BrM.r], [Q0.r])
                        self.tt("dve", attb[:, :, :], v3(pqk), DTi[:, :, :], ALU.mult, [pqk.r, DTi.r], [attb.r])
                        self.ld(ATT[cg], attb[:, :, :], attb.r, reads=[attb.r], writes=[rATT[cg]])

                        def mk(li):
                            return bc_m(l1m[:, li, :])
                        self.tt("pool", Xm[:, :, :], P0[:, :, :], mk(0), ALU.mult, [P0.r, l1m.r], [Xm.r])
                        self.tt("pool", Tm[:, :, :], bc_m(ident2), Xm[:, :, :], ALU.subtract, [l1c.r, Xm.r], [Tm.r])
                        self.tt("dve", Xpm[:, :, :], Q0[:, :, :], mk(0), ALU.mult, [Q0.r, l1m.r], [Xpm.r])
                        self.tt("dve", Um[:, :, :], bc_m(ident2), Xpm[:, :, :], ALU.subtract, [l1c.r, Xpm.r], [Um.r])
                        yield
                        for li in range(1, 6):
                            last = (li == 5)
                            if not last:
                                pX = self.nxt(ps)
                                for hh in range(8):
                                    for d in range(2):
                                        self.mm(pX[DS[d], hh * 64:(hh + 1) * 64], Q0[DS[d], hh, :], Tm[DS[d], hh, :], True, True,
                                                [Q0.r, Tm.r], [pX.r])
                                self.tt("dve", Xm[:, :, :], v3(pX), mk(li), ALU.mult, [pX.r, l1m.r], [Xm.r])
                            pXp = self.nxt(ps)
                            for hh in range(8):
                                for d in range(2):
                                    self.mm(pXp[DS[d], hh * 64:(hh + 1) * 64], P0[DS[d], hh, :], Um[DS[d], hh, :], True, True,
                                            [P0.r, Um.r], [pXp.r])
                            self.tt("dve", Xpm[:, :, :], v3(pXp), mk(li), ALU.mult, [pXp.r, l1m.r], [Xpm.r])
                            yield
                            if not last:
                                pY = self.nxt(ps)
                                for hh in range(8):
                                    for d in range(2):
                                        self.mm(pY[DS[d], hh * 64:(hh + 1) * 64], Um[DS[d], hh, :], Xm[DS[d], hh, :], True, True,
                                                [Um.r, Xm.r], [pY.r])
                            pYp = self.nxt(ps)
                            for hh in range(8):
                                for d in range(2):
                                    self.mm(pYp[DS[d], hh * 64:(hh + 1) * 64], Tm[DS[d], hh, :], Xpm[DS[d], hh, :], True, True,
                                            [Tm.r, Xpm.r], [pYp.r])
                            if not last:
                                self.tt("dve", Tm[:, :, :], Tm[:, :, :], v3(pY), ALU.subtract, [Tm.r, pY.r], [Tm.r])
                            self.tt("dve", Um[:, :, :], Um[:, :, :], v3(pYp), ALU.subtract, [Um.r, pYp.r], [Um.r])
                            yield
                        self.tt("pool", tbb[:, :, :], Um[:, :, :], bc_h(be2[:, :]), ALU.mult, [Um.r, be2.r], [tbb.r])
                        self.ld(TBS[cg], tbb[:, :, :], tbb.r, reads=[tbb.r], writes=[rTBS[cg]])
                    for grp_ in ((0, 1, 2), (3, 4, 5), (6, 7)):
                        gens = [chunk_gen(ci) for ci in grp_]
                        while gens:
                            for g_ in list(gens):
                                try:
                                    next(g_)
                                except StopIteration:
                                    gens.remove(g_)
                barrier()

            if stage >= 23:
                o = 0
                Sf, o = aview(o, [128, 16, 128], F32, "Sf")
                Sb_, o = aview(o, [128, 16, 128], BF16, "Sb")
                NR = 3
                kqs, qgs, ktms, vtms, tbts, atts, osts = [], [], [], [], [], [], []
                for i in range(NR):
                    t_, o = aview(o, [128, 2, 8, 64], BF16, "kq%d" % i)
                    kqs.append(t_)
                    t_, o = aview(o, [128, 2, 8, 64], BF16, "qg%d" % i)
                    qgs.append(t_)
                    t_, o = aview(o, [128, 8, 128], BF16, "ktm%d" % i)
                    ktms.append(t_)
                    t_, o = aview(o, [128, 8, 128], BF16, "vtm%d" % i)
                    vtms.append(t_)
                    t_, o = aview(o, [128, 8, 64], BF16, "tbt%d" % i)
                    tbts.append(t_)
                    t_, o = aview(o, [128, 8, 64], BF16, "att%d" % i)
                    atts.append(t_)
                    t_, o = aview(o, [128, 2, 8, 64], BF16, "ost%d" % i)
                    osts.append(t_)
                rps, vns, vsns, osfs = [], [], [], []
                for i in range(2):
                    t_, o = aview(o, [128, 8, 128], BF16, "rp%d" % i)
                    rps.append(t_)
                    t_, o = aview(o, [128, 8, 128], BF16, "vn%d" % i)
                    vns.append(t_)
                    t_, o = aview(o, [128, 8, 128], BF16, "vsn%d" % i)
                    vsns.append(t_)
                    t_, o = aview(o, [128, 8, 64], F32, "osf%d" % i)
                    osfs.append(t_)
                KNv = KN.rearrange("h p t -> p h t")
                SfR = [Res("SfR%d" % u) for u in range(16)]
                SbR = [Res("SbR%d" % u) for u in range(16)]
                rpR = {id(t_): [Res("rpR") for _ in range(16)] for t_ in rps}
                vnR = {id(t_): [Res("vnR") for _ in range(2)] for t_ in vns}
                vsnR = {id(t_): [Res("vsnR") for _ in range(4)] for t_ in vsns}
                seqs = [(0, 4), (4, 4), (8, 4), (12, 4), (16, 32)]
                for si, (cb, N) in enumerate(seqs):
                    if si < 4:
                        self.memset("dve", Sf[:, :, :], 0.0, SfR)
                    else:
                        self.ld(Sf[:, :, :], st_d.rearrange("u p v -> p u v"), Sf.r, writes=SfR)
                    self.cp("act", Sb_[:, :, :], Sf[:, :, :], SfR, SbR)
                    for s_ in range(N):
                        cc = (cb + s_, cb + N - 1 - s_)
                        kq = self.nxt(kqs)
                        qg = self.nxt(qgs)
                        ktm = self.nxt(ktms)
                        vtm = self.nxt(vtms)
                        tbt = self.nxt(tbts)
                        att = self.nxt(atts)
                        ost = self.nxt(osts)
                        rp = self.nxt(rps)
                        vn = self.nxt(vns)
                        vsn = self.nxt(vsns)
                        for d in range(2):
                            c_ = cc[d]
                            tk = slice(c_ * 64, (c_ + 1) * 64)
                            ds_ = slice(d * 64, (d + 1) * 64)
                            self.ld(kq[:, d, :, :], KNv[:, :, tk], kq.r, reads=rKN, writes=[kq.r])
                            self.ld(qg[:, d, :, :], QG[d, c_], qg.r, reads=[rQG[d][c_]], writes=[qg.r])
                            self.ld(ktm[ds_, :, :], KTM[tk, :, :], ktm.r, reads=[rKTM], writes=[ktm.r])
                            self.ld(vtm[ds_, :, :], VTM[tk, :, :], vtm.r, reads=[rVTM], writes=[vtm.r])
                            self.ld(tbt[ds_, :, :], TBS[c_, ds_, :, :], tbt.r, reads=[rTBS[c_]], writes=[tbt.r])
                            self.ld(att[ds_, :, :], ATT[c_, ds_, :, :], att.r, reads=[rATT[c_]], writes=[att.r])
                        DS = (slice(0, 64), slice(64, 128))
                        for g4 in range(2):
                            for d in range(2):
                                for h4 in range(4):
                                    hh = g4 * 4 + h4
                                    self.mm(psA[g4][DS[d], h4 * 128:(h4 + 1) * 128], kq[:, d, hh, :], Sb_[:, d * 8 + hh, :], True, True,
                                            [kq.r, SbR[d * 8 + hh]], [psA[g4].r])
                        for g4 in range(2):
                            for d in range(2):
                                for h4 in range(4):
                                    hh = g4 * 4 + h4
                                    self.stt(rp[DS[d], hh, :], psA[g4][DS[d], h4 * 128:(h4 + 1) * 128], negeg[DS[d], cc[d], hh:hh + 1],
                                             vtm[DS[d], hh, :], ALU.mult, ALU.add, [psA[g4].r, negeg.r, vtm.r], [rpR[id(rp)][d * 8 + hh]])
                        for g4 in range(2):
                            for d in range(2):
                                for h4 in range(4):
                                    hh = g4 * 4 + h4
                                    self.mm(psA[2 + g4][DS[d], h4 * 128:(h4 + 1) * 128], tbt[DS[d], hh, :], rp[DS[d], hh, :], True, True,
                                            [tbt.r, rpR[id(rp)][d * 8 + hh]], [psA[2 + g4].r])
                        for g4 in range(2):
                            hs4 = slice(g4 * 4, g4 * 4 + 4)
                            pv3 = psA[2 + g4][:, :].rearrange("p (h v) -> p h v", v=128)
                            self.cp("act", vn[:, hs4, :], pv3, [psA[2 + g4].r], [vnR[id(vn)][g4]])
                            for d in range(2):
                                self.tt("dve", vsn[DS[d], hs4, :], pv3[DS[d]],
                                        ksc[DS[d], cc[d], hs4].unsqueeze(2).to_broadcast([64, 4, 128]), ALU.mult,
                                        [psA[2 + g4].r, ksc.r], [vsnR[id(vsn)][g4 * 2 + d]])
                        for d in range(2):
                            poS, poA = (psB[0], psB[1]) if d == 0 else (psC[0], psC[1])
                            for hh in range(8):
                                self.mm(poS[:, hh * 64:(hh + 1) * 64], Sb_[:, d * 8 + hh, :], qg[:, d, hh, :], True, True, [SbR[d * 8 + hh], qg.r], [poS.r])
                            for hh in range(8):
                                self.mm(poA[:, hh * 64:(hh + 1) * 64], vn[DS[d], hh, :], att[DS[d], hh, :], True, True, [vnR[id(vn)][hh // 4], att.r], [poA.r])
                        for u in range(16):
                            d, hh = u // 8, u % 8
                            bk = psA[u // 4]
                            self.mm(bk[:, (u % 4) * 128:(u % 4 + 1) * 128], ktm[DS[d], hh, :], vsn[DS[d], hh, :], True, True,
                                    [ktm.r, vsnR[id(vsn)][(hh // 4) * 2 + d]], [bk.r])
                        for d in range(2):
                            poS, poA = (psB[0], psB[1]) if d == 0 else (psC[0], psC[1])
                            osf = self.nxt(osfs)
                            self.cp("act", osf[:, :, :], poS[:, :].rearrange("p (h i) -> p h i", i=64), [poS.r], [osf.r])
                            self.tt("dve", ost[:, d, :, :], poA[:, :].rearrange("p (h i) -> p h i", i=64), osf[:, :, :], ALU.add,
                                    [poA.r, osf.r], [ost.r])
                            tk = slice(cc[d] * 64, (cc[d] + 1) * 64)
                            self.ld(OTF[d].rearrange("h p t -> p h t")[:, :, tk], ost[:, d, :, :], ost.r, reads=[ost.r],
                                    writes=[rOTF[d][cc[d]]])
                        for u in range(16):
                            d = u // 8
                            bk = psA[u // 4]
                            self.stt(Sf[:, u, :], Sf[:, u, :], egl[:, cc[d], u:u + 1], bk[:, (u % 4) * 128:(u % 4 + 1) * 128], ALU.mult, ALU.add,
                                     [SfR[u], egl.r, bk.r], [SfR[u]])
                        for k4 in range(4):
                            self.cp("act", Sb_[:, k4 * 4:(k4 + 1) * 4, :], Sf[:, k4 * 4:(k4 + 1) * 4, :], SfR[k4 * 4:(k4 + 1) * 4], SbR[k4 * 4:(k4 + 1) * 4])
                    if si < 4:
                        self.ld(ns_d[si].rearrange("u p v -> p u v"), Sf[:, :, :], Sf.r, reads=SfR, writes=[Res("nso")])
                barrier()

            if stage >= 24:
                o = 0
                hid_end = 32768
                o = hid_end
                of_, o = aview(o, [128, 8, 512], BF16, "of")
                ob_, o = aview(o, [128, 8, 512], BF16, "ob")
                zs_, o = aview(o, [128, 8, 512], BF16, "zs")
                ot2, o = aview(o, [128, 8, 512], BF16, "ot2")
                for b in range(NB):
                    sl = slice(b * 512, (b + 1) * 512)
                    crange = range(b * 8, b * 8 + 8)
                    self.ld(of_[:, :, :], OTF[0].rearrange("h p t -> p h t")[:, :, sl], of_.r, reads=[rOTF[0][c_] for c_ in crange], writes=[of_.r])
                    self.ld(ob_[:, :, :], OTF[1].rearrange("h p t -> p h t")[:, :, sl], ob_.r, reads=[rOTF[1][c_] for c_ in crange], writes=[ob_.r])
                    self.ld(zs_[:, :, :], ZT.rearrange("h p t -> p h t")[:, :, sl], zs_.r, reads=[rZT[hh][b] for hh in range(8)], writes=[zs_.r])
                    for hh in range(8):
                        oh = self.nxt(ftmp)
                        self.tt("dve", oh[:, :], of_[:, hh, :], ob_[:, hh, :], ALU.add, [of_.r, ob_.r], [oh.r])
                        sq = self.nxt(btmp)
                        self.act(sq[:, :], oh[:, :], AF.Square, [oh.r], [sq.r])
                        pn = self.nxt(psB)
                        self.mm(pn[:, :], ones_bf[:, :], sq[:, :], True, True, [ones_bf.r, sq.r], [pn.r])
                        rt = self.nxt(rtmp)
                        self.act(rt[:, :], pn[:, :], AF.Ln, [pn.r, epsc.r], [rt.r], scale=1.0 / 128.0, bias=epsc[:, 0:1])
                        rr = self.nxt(rtmp)
                        self.act(rr[:, :], rt[:, :], AF.Exp, [rt.r], [rr.r], scale=-0.5)
                        yh = self.nxt(ftmp)
                        self.tt("dve", yh[:, :], oh[:, :], rr[:, :], ALU.mult, [oh.r, rr.r], [yh.r])
                        self.stt(ot2[:, hh, :], yh[:, :], onorm[:, 0:1], zs_[:, hh, :], ALU.mult, ALU.mult,
                                 [yh.r, onorm.r, zs_.r], [ot2.r])
                    x = load_x(X1, rX1, b)
                    out_proj_resid(1, x, b, None, None, ot=ot2)
                    resid_mlp(1, x, b)
                    store_x(x, yT_d, rY, b)
                barrier()

        self.final_wait()
        P.emit(self.st)
        return nc

    def final_wait(self):
        P = self.P
        toks = []
        for eng in ENGS:
            for (fn, waits, inc) in P.ops[eng]:
                pass
        final = {}
        for eng in ENGS:
            for (fn, waits, inc) in P.ops[eng]:
                if inc is not None:
                    final[inc[0]] = final.get(inc[0], 0) + inc[1]
        waits = [(k, v) for k, v in final.items() if k != "sp"]
        P.ops["sp"].append((None, waits, None))


_CACHE = {}


def _host_consts():
    GRID_W, HD = 64, 64
    t = np.arange(2048)
    pos = np.stack([t // GRID_W, t % GRID_W], -1).astype(np.float32)
    axis_dim = HD // 2
    inv_freq = (1.0 / (10000.0 ** (np.arange(0, axis_dim, 2, dtype=np.float32) / axis_dim))).astype(np.float32)
    ang = pos[:, :, None] * inv_freq
    cos = np.cos(ang).astype(np.float32)
    sin = np.sin(ang).astype(np.float32)
    cosT = np.zeros((128, 2048), np.float32)
    sinT = np.zeros((128, 2048), np.float32)
    perm = np.zeros((128, 128), np.float32)
    for p in range(128):
        d = p % 64
        ax = d // 32
        j = d % 16
        cosT[p] = cos[:, ax, j]
        sinT[p] = sin[:, ax, j]
        if (d % 32) < 16:
            perm[p + 16, p] = -1.0
        else:
            perm[p - 16, p] = 1.0
    ident = np.eye(128, dtype=np.float32)
    return cosT, sinT, perm, ident


def _l1_consts():
    c = np.zeros((128, 1024), np.float32)
    p = np.arange(128)[:, None]
    t = p % 64
    d = p // 64
    i = np.arange(64)[None, :]
    c[:, 0:64] = (t <= i)
    c[:, 64:128] = (t >= i)
    c[:, 128:192] = (t == i)
    incl = np.where(d == 0, i >= t, i <= t)
    c[:, 192:256] = np.where(incl, 0.0, NEG)
    c[:, 256:320] = np.where(d == 0, i > t, i < t)
    strT = np.where(d == 0, i < t, i > t)
    c[:, 320:384] = np.where(strT, 0.0, NEG)
    q = np.arange(128)[None, :]
    c[:, 384:512] = (p // 64 == q // 64)
    c[:, 512:640] = (p < 64) * np.ones((1, 128))
    c[:, 640:768] = (p >= 64) * np.ones((1, 128))
    c[:, 768:896] = 1.0
    c[:, 896:1024] = (p == q)
    return c


def _l1_masks():
    m = np.zeros((128, 6, 64), np.float32)
    i = (np.arange(128) % 64)[:, None]
    j = np.arange(64)[None, :]
    for li, sz in enumerate((1, 2, 4, 8, 16, 32)):
        m[:, li, :] = ((i // (2 * sz)) == (j // (2 * sz))) & ((i // sz) != (j // sz))
    return m.reshape(128, 384)


def _bias_expand(rel_bias):
    H = 8
    e = np.arange(2)[:, None, None, None, None]
    kc = np.arange(64)[None, :, None, None, None]
    cl = np.arange(8)[None, None, :, None, None]
    i = np.arange(4)[None, None, None, :, None]
    qc = np.arange(64)[None, None, None, None, :]
    dr = -cl + 2 * i + e
    cs = np.clip(qc - 8, 0, 48)
    valid = (kc >= cs) & (kc < cs + 16)
    valid = np.broadcast_to(valid, (2, 64, 8, 4, 64))
    ri = np.broadcast_to(dr + 7, (2, 64, 8, 4, 64))
    ci = np.clip(np.broadcast_to(kc - qc + 15, (2, 64, 8, 4, 64)), 0, 30)
    out = np.empty((H, 2, 64, 8, 4, 64), np.float32)
    for h in range(H):
        g = rel_bias[h][ri, ci]
        out[h] = np.where(valid, g, np.float32(NEG))
    return np.ascontiguousarray(out.reshape(H, 128, 2048))


def kernel(x_prompt, x_sample, c, cache_l0_k, cache_l0_v, state_l1, c_ctx,
           l0_mod_w, l0_mod_b, l0_norm1, l0_w_in, l0_q_norm_a, l0_k_norm_a, l0_q_norm_b, l0_k_norm_b,
           l0_rel_bias, l0_w_out, l0_norm2, l0_mlp_w1, l0_mlp_w2,
           l1_mod_w, l1_mod_b, l1_norm1, l1_w_in, l1_conv_w, l1_a_log, l1_dt_bias, l1_out_norm,
           l1_w_out, l1_norm2, l1_mlp_w1, l1_mlp_w2, _stage=99):
    f = lambda a: np.ascontiguousarray(np.asarray(a, dtype=np.float32))
    key = ("nc", _stage)
    if key not in _CACHE:
        _CACHE[key] = Builder(_stage).build()
    nc = _CACHE[key]
    cosT, sinT, perm, ident = _host_consts()
    colT = lambda v, n: f(np.asarray(v).reshape(n, 128).T)
    t2 = lambda v: np.tile(np.asarray(v, np.float32), 2)
    gains = f(np.stack([t2(l0_q_norm_a), t2(l0_k_norm_a), t2(l0_q_norm_b), t2(l0_k_norm_b)], -1))
    gk_row = np.concatenate([np.tile(np.asarray(l0_k_norm_a, np.float32), 2), np.tile(np.asarray(l0_k_norm_b, np.float32), 8)])
    gk_tm = f(np.broadcast_to(gk_row[None, :], (128, 640)))
    bias_exp = _bias_expand(np.asarray(l0_rel_bias, np.float32))
    shared = {
        "l0_mod_w": f(l0_mod_w), "l0_mod_bT": colT(l0_mod_b, 48), "l0_n1T": colT(l0_norm1, 8), "l0_n2T": colT(l0_norm2, 8),
        "l0_w_in": f(l0_w_in), "l0_w_out": f(l0_w_out), "l0_w1": f(l0_mlp_w1), "l0_w2": f(l0_mlp_w2),
        "l1_mod_w": f(l1_mod_w), "l1_mod_bT": colT(l1_mod_b, 48), "l1_n1T": colT(l1_norm1, 8), "l1_n2T": colT(l1_norm2, 8),
        "l1_w_in": f(l1_w_in), "l1_w_out": f(l1_w_out), "l1_w1": f(l1_mlp_w1), "l1_w2": f(l1_mlp_w2),
        "l1c": _l1_consts(), "l1m": _l1_masks(),
        "convw": f(np.asarray(l1_conv_w, np.float32).reshape(3, 24, 128).transpose(2, 1, 0).reshape(128, 72)),
        "alog_b": f(np.broadcast_to(np.asarray(l1_a_log, np.float32).reshape(1, 16), (128, 16))),
        "dtb_b": f(np.broadcast_to(np.asarray(l1_dt_bias, np.float32).reshape(1, 16), (128, 16))),
        "onorm": f(np.asarray(l1_out_norm, np.float32).reshape(128, 1)),
        "gains": gains, "gk_tm": gk_tm, "cosT": cosT, "sinT": sinT, "perm": perm, "bias_exp": bias_exp, "ident": ident,
    }
    xp = np.asarray(x_prompt, np.float32)
    xs = np.asarray(x_sample, np.float32)
    in_maps = []
    for i in range(8):
        xi = np.concatenate([xp[4 * i:4 * i + 4].reshape(NPR, 1024), xs[i]], 0)
        cpair = np.stack([np.asarray(c_ctx, np.float32), np.asarray(c, np.float32)[i]], -1)
        cTi = cpair.reshape(8, 128, 2).transpose(1, 0, 2).reshape(128, 16)
        m = dict(shared)
        m["xT"] = f(xi.T)
        m["cT"] = f(cTi)
        m["ck"] = f(np.asarray(cache_l0_k, np.float32)[i].reshape(256, 640))
        m["cv"] = f(np.asarray(cache_l0_v, np.float32)[i].reshape(256, 640))
        m["state"] = f(np.asarray(state_l1, np.float32)[i].reshape(16, 128, 128))
        in_maps.append(m)
    res = run_bass_kernel_spmd(nc, in_maps, core_ids=list(range(8)))
    yp = np.empty((32, 256, 1024), np.float32)
    ys = np.empty((8, 2048, 1024), np.float32)
    nk = np.empty((32, 256, 10, 64), np.float32)
    nv = np.empty((32, 256, 10, 64), np.float32)
    ns = np.empty((32, 2, 8, 128, 128), np.float32)
    for i in range(8):
        r = res.results[i]
        y = np.asarray(r["yT"]).T
        yp[4 * i:4 * i + 4] = y[:NPR].reshape(4, 256, 1024)
        ys[i] = y[NPR:]
        nk[4 * i:4 * i + 4] = np.asarray(r["nk"]).reshape(4, 256, 10, 64)
        nv[4 * i:4 * i + 4] = np.asarray(r["nv"]).reshape(4, 256, 10, 64)
        ns[4 * i:4 * i + 4] = np.asarray(r["ns"]).reshape(4, 2, 8, 128, 128)
    return (yp, ys, nk, nv, ns)
```

```python
from contextlib import ExitStack
import os
import numpy as np
import concourse.bass as bass
import concourse.mybir as mybir
from concourse.bass_utils import run_bass_kernel_spmd

F32 = mybir.dt.float32
BF16 = mybir.dt.bfloat16
AF = mybir.ActivationFunctionType
ALU = mybir.AluOpType
AX = mybir.AxisListType
ENGS = ("pe", "act", "dve", "pool", "sp")

NTOK = 3072
NPR = 1024
NB = 6
EPS = 1e-6
NEG = -30000.0
STORE_Q = os.environ.get('KSTQ', 'act')
FAST_RECIP = False
SAME_ENGINE_INORDER = bool(int(os.environ.get('KSAME', '1')))


class Res:
    __slots__ = ("name", "last_w", "readers", "dsem", "dcount", "excl", "norecycle")

    def __init__(self, name, excl=False):
        self.name = name
        self.excl = excl
        self.norecycle = False
        self.last_w = None
        self.readers = []
        self.dsem = None
        self.dcount = 0


class Prog:
    def __init__(self, nc):
        self.nc = nc
        self.ops = {e: [] for e in ENGS}
        self.cnt = {e: 0 for e in ENGS}
        self.waited = {e: {} for e in ENGS}
        self.semkeys = list(ENGS)
        self.owners = []
        self.free = []
        self.nobar = set()

    def _dep_tokens(self, reads, writes):
        toks = []
        for r in reads:
            if r.last_w is not None:
                toks.append(r.last_w)
            if r.excl:
                toks.extend(r.readers)
        for w in writes:
            if w.last_w is not None:
                toks.append(w.last_w)
            toks.extend(w.readers)
        return toks

    def _add_waits(self, eng, toks, skip=None):
        need = {}
        for (k, v) in toks:
            if skip is not None and k == skip:
                continue
            if v > need.get(k, 0):
                need[k] = v
        waits = []
        wd = self.waited[eng]
        for k, v in need.items():
            if wd.get(k, 0) >= v:
                continue
            wd[k] = v
            waits.append((k, v))
        return waits

    def op(self, eng, fn, reads=(), writes=(), same_ok=False):
        toks = self._dep_tokens(reads, writes)
        if SAME_ENGINE_INORDER and eng != "pool":
            same_ok = True
        waits = self._add_waits(eng, toks, skip=(eng if same_ok else None))
        self.cnt[eng] += 1
        tok = (eng, self.cnt[eng])
        self.ops[eng].append((fn, waits, (eng, 1)))
        for r in reads:
            r.readers.append(tok)
        for w in writes:
            w.last_w = tok
            w.readers = []
        return tok

    def dma(self, eng, fn, owner, reads=(), writes=()):
        if owner.dsem is None:
            if self.free and not owner.norecycle and eng != "pool":
                owner.dsem, owner.dcount = self.free.pop()
            else:
                owner.dsem = "d%d" % len(self.semkeys)
                owner.dcount = 0
                self.semkeys.append(owner.dsem)
            self.owners.append(owner)
        toks = self._dep_tokens(reads, writes)
        waits = self._add_waits(eng, toks)
        owner.dcount += 16
        tok = (owner.dsem, owner.dcount)
        self.ops[eng].append((fn, waits, (owner.dsem, 16)))
        for r in reads:
            r.readers.append(tok)
        for w in writes:
            w.last_w = tok
            w.readers = []
        return tok

    def barrier(self):
        final = {}
        for eng in ENGS:
            for (fn, waits, inc) in self.ops[eng]:
                if inc is not None:
                    final[inc[0]] = final.get(inc[0], 0) + inc[1]
        items = [(k, v) for k, v in final.items() if k not in self.nobar]
        for eng in ENGS:
            waits = self._add_waits(eng, items)
            self.ops[eng].append((None, waits, None))
        keep = []
        for o in self.owners:
            if o.dsem in self.nobar or o.norecycle:
                keep.append(o)
            else:
                self.free.append((o.dsem, o.dcount))
                o.dsem = None
        self.owners = keep

    def finish_wait(self, eng, resources):
        toks = [r.last_w for r in resources if r.last_w is not None]
        waits = self._add_waits(eng, toks)
        self.ops[eng].append((None, waits, None))

    def emit(self, stack):
        nc = self.nc
        sems = {}
        for k in self.semkeys:
            sems[k] = stack.enter_context(nc.semaphore(k))
        block = stack.enter_context(nc.Block())
        handles = {"pe": block.tensor, "act": block.scalar, "dve": block.vector,
                   "pool": block.gpsimd, "sp": block.sync}
        for e in ENGS:
            ops = self.ops[e]

            def body(eh, ops=ops):
                for fn, waits, inc in ops:
                    for (k, v) in waits:
                        eh.wait_ge(sems[k], v)
                    if fn is not None:
                        ins = fn(eh)
                        ins.then_inc(sems[inc[0]], inc[1])
            handles[e](body)


class T:
    def __init__(self, t, name):
        self.t = t
        self.r = Res(name)

    def __getitem__(self, k):
        return self.t[k]


class Builder:
    def __init__(self, stage):
        self.stage = stage
        self.nc = bass.Bass("TRN2", target_bir_lowering=False)
        self.P = Prog(self.nc)
        self.st = ExitStack()
        self.rot = {}
        self.swown = {}

    @staticmethod
    def interleave(gens, width):
        active = []
        it = iter(gens)
        more = True
        while True:
            while more and len(active) < width:
                try:
                    active.append(next(it))
                except StopIteration:
                    more = False
            if not active:
                break
            for g_ in list(active):
                try:
                    next(g_)
                except StopIteration:
                    active.remove(g_)

    @staticmethod
    def interleave2(main, gens, width):
        active = []
        it = iter(gens)
        more = True
        main_alive = main is not None
        while True:
            while more and len(active) < width:
                try:
                    active.append(next(it))
                except StopIteration:
                    more = False
            if not active and not main_alive:
                break
            if main_alive:
                try:
                    next(main)
                except StopIteration:
                    main_alive = False
            for g_ in list(active):
                try:
                    next(g_)
                except StopIteration:
                    active.remove(g_)

    def sb(self, name, shape, dt):
        return T(self.st.enter_context(self.nc.sbuf_tensor("s_" + name, list(shape), dt)), name)

    def sbn(self, name, n, shape, dt):
        return [self.sb("%s%d" % (name, i), shape, dt) for i in range(n)]

    def nxt(self, lst):
        k = id(lst)
        i = self.rot.get(k, 0)
        self.rot[k] = i + 1
        return lst[i % len(lst)]

    def din(self, name, shape, dt=F32):
        return self.nc.dram_tensor(name, list(shape), dt, kind="ExternalInput").ap()

    def dout(self, name, shape, dt=F32):
        return self.nc.dram_tensor(name, list(shape), dt, kind="ExternalOutput").ap()

    def dscr(self, name, shape, dt=BF16):
        kind = "ExternalOutput" if (os.environ.get("KDEBUG") and name in os.environ["KDEBUG"].split(",")) else "Internal"
        return self.nc.dram_tensor(name, list(shape), dt, kind=kind).ap()

    def mm(self, out, lhsT, rhs, start, stop, reads, writes):
        self.P.op("pe", lambda e: e.matmul(out, lhsT, rhs, start=start, stop=stop),
                  reads=reads, writes=writes, same_ok=True)

    def tr(self, out, in_, ident, reads, writes):
        self.P.op("pe", lambda e: e.transpose(out, in_, ident), reads=reads, writes=writes, same_ok=True)

    def act(self, out, in_, func, reads, writes, scale=None, bias=None):
        kw = {}
        if scale is not None:
            kw["scale"] = scale
        if bias is not None:
            kw["bias"] = bias
        self.P.op("act", lambda e: e.activation(out=out, in_=in_, func=func, **kw), reads=reads, writes=writes)

    def tt(self, eng, out, in0, in1, op, reads, writes):
        self.P.op(eng, lambda e: e.tensor_tensor(out=out, in0=in0, in1=in1, op=op), reads=reads, writes=writes)

    def ts(self, eng, out, in0, s1, s2, op0, op1, reads, writes):
        if op1 is None:
            self.P.op(eng, lambda e: e.tensor_scalar(out=out, in0=in0, scalar1=s1, scalar2=None, op0=op0),
                      reads=reads, writes=writes)
        else:
            self.P.op(eng, lambda e: e.tensor_scalar(out=out, in0=in0, scalar1=s1, scalar2=s2, op0=op0, op1=op1),
                      reads=reads, writes=writes)

    def stt(self, out, in0, scalar, in1, op0, op1, reads, writes):
        self.P.op("dve", lambda e: e.scalar_tensor_tensor(out=out, in0=in0, scalar=scalar, in1=in1, op0=op0, op1=op1),
                  reads=reads, writes=writes)

    def cp(self, eng, out, in_, reads, writes):
        if eng == "act":
            self.P.op("act", lambda e: e.copy(out=out, in_=in_), reads=reads, writes=writes)
        else:
            self.P.op(eng, lambda e: e.tensor_copy(out=out, in_=in_), reads=reads, writes=writes)

    def recip(self, out, in_, reads, writes):
        if FAST_RECIP:
            self.P.op("dve", lambda e: e.reciprocal_approx_fast(out, in_), reads=reads, writes=writes)
        else:
            self.P.op("dve", lambda e: e.reciprocal(out=out, in_=in_), reads=reads, writes=writes)

    def memset(self, eng, ap, val, writes):
        self.P.op(eng, lambda e: e.memset(ap, val), writes=writes)

    def ld(self, out, in_, owner, reads=(), writes=(), q="sp"):
        if q == "sp" and STORE_Q != "sp" and any(r is owner for r in reads):
            q = STORE_Q
        if q == "pool":
            key = id(owner)
            if key not in self.swown:
                self.swown[key] = Res("sw_" + owner.name)
                self.swown[key].norecycle = True
            owner = self.swown[key]
        self.P.dma(q, lambda e: e.dma_start(out=out, in_=in_), owner, reads=reads, writes=writes)

    def build(self):
        nc, P = self.nc, self.P
        stage = self.stage
        xT_d = self.din("xT", [1024, NTOK])
        cT_d = self.din("cT", [128, 16])
        ck_d = self.din("ck", [256, 640])
        cv_d = self.din("cv", [256, 640])
        st_d = self.din("state", [16, 128, 128])
        Lw = []
        for l in range(2):
            d = {}
            d["mod_w"] = self.din("l%d_mod_w" % l, [1024, 6144])
            d["mod_b"] = self.din("l%d_mod_bT" % l, [128, 48])
            d["n1"] = self.din("l%d_n1T" % l, [128, 8])
            d["n2"] = self.din("l%d_n2T" % l, [128, 8])
            d["w_in"] = self.din("l%d_w_in" % l, [1024, 2304 if l == 0 else 4128])
            d["w_out"] = self.din("l%d_w_out" % l, [1024, 1024])
            d["w1"] = self.din("l%d_w1" % l, [1024, 4096])
            d["w2"] = self.din("l%d_w2" % l, [4096, 1024])
            Lw.append(d)
        gains_d = self.din("gains", [128, 4])
        gk_tm_d = self.din("gk_tm", [128, 640])
        cos_d = self.din("cosT", [128, 2048])
        sin_d = self.din("sinT", [128, 2048])
        perm_d = self.din("perm", [128, 128])
        bias_d = self.din("bias_exp", [8, 128, 2048])
        ident_d = self.din("ident", [128, 128])
        l1c_d = self.din("l1c", [128, 1024])
        l1m_d = self.din("l1m", [128, 384])
        convw_d = self.din("convw", [128, 72])
        alog_d = self.din("alog_b", [128, 16])
        dtb_d = self.din("dtb_b", [128, 16])
        onorm_d = self.din("onorm", [128, 1])

        yT_d = self.dout("yT", [1024, NTOK])
        nk_d = self.dout("nk", [NPR, 640])
        nv_d = self.dout("nv", [NPR, 640])
        ns_d = self.dout("ns", [4, 16, 128, 128])

        Wb = []
        for l in range(2):
            d = {}
            d["w_in"] = self.dscr("wb%d_in" % l, [1024, 2304 if l == 0 else 4128])
            d["w_out"] = self.dscr("wb%d_out" % l, [1024, 1024])
            d["w1"] = self.dscr("wb%d_w1" % l, [1024, 4096])
            d["w2"] = self.dscr("wb%d_w2" % l, [4096, 1024])
            Wb.append(d)
        QKT = self.dscr("QKT", [13, 128, NTOK])
        V0 = self.dscr("V0", [NTOK, 640])
        KC = self.dscr("KC", [5, 128, 256])
        OT = self.dscr("OT", [8, 128, NTOK])
        rWb = [{k: Res("rWb%d%s" % (l, k)) for k in Wb[l]} for l in range(2)]
        rQKT = [[Res("rQKT") for _ in range(NB)] for _ in range(13)]
        rV0 = [Res("rV0") for _ in range(NB)]
        rKC = Res("rKC")
        rOT = [[Res("rOT") for _ in range(NB)] for _ in range(8)]

        X1 = self.dscr("X1", [1024, NTOK], F32)
        rX1 = [Res("rX1_%d" % b_) for b_ in range(NB)]
        rXin = [Res("rXin%d" % b_) for b_ in range(NB)]
        rY = [Res("rY%d" % b_) for b_ in range(NB)]

        ones_bf = self.sb("ones_bf", [128, 128], BF16)
        bd_bf = self.sb("bd_bf", [128, 128], BF16)
        perm_bf = self.sb("perm_bf", [128, 128], BF16)
        ident_f = self.sb("ident_f", [128, 128], F32)
        stage_f = self.sb("stage_f", [128, 128], F32)
        gains = self.sb("gains", [128, 4], F32)
        gk_tm = self.sb("gk_tm", [128, 640], F32)
        cT = self.sb("cT", [128, 8, 2], F32)
        scT = self.sb("scT", [128, 8, 2], F32)
        modT = self.sb("modT", [128, 48, 2], F32)
        modb = self.sb("modb", [128, 48], F32)
        n1 = self.sb("n1", [128, 8], F32)
        n2 = self.sb("n2", [128, 8], F32)
        G1 = self.sb("G1", [128, 8, 2], F32)
        G2 = self.sb("G2", [128, 8, 2], F32)
        epsc = self.sb("epsc", [128, 1], F32)

        ps = [T(self.st.enter_context(nc.psum_tensor("ps%d" % i, [128, 512], F32)), "ps%d" % i) for i in range(8)]
        for p_ in ps:
            p_.r.excl = True
        psA = ps[0:4]
        psB = ps[4:6]
        psC = ps[6:8]
        psBC = ps[4:8]
        psB1 = ps[5:8]

        xb = self.sbn("xb", 2, [128, 8, 512], F32)
        wbuf = self.sbn("wbuf", 3, [128, 4096], BF16)
        sqb = self.sb("sqb", [128, 8, 512], BF16)
        hT = self.sbn("hT", 2, [128, 8, 512], BF16)
        Rb = self.sb("Rb", [128, 512], F32)
        rtmp = self.sbn("rtmp", 5, [128, 512], F32)
        ftmp = self.sbn("ftmp", 4, [128, 512], F32)
        btmp = self.sbn("btmp", 4, [128, 512], BF16)
        ostg = self.sbn("ostg", 3, [128, 512], BF16)
        arena = self.st.enter_context(nc.sbuf_tensor("s_arena", [128, 20480], F32))

        def aview(off_b, shape, dt, name):
            n = 1
            for d_ in shape[1:]:
                n *= d_
            esz = 4 if dt == F32 else 2
            nb = n * esz
            assert off_b % 4 == 0 and off_b + nb <= 81920, (name, off_b, nb)
            ap = arena[:, off_b // 4:(off_b + nb + 3) // 4]
            if dt != F32:
                ap = ap.bitcast(dt)
            if len(shape) == 3:
                ap = ap.rearrange("p (a b) -> p a b", b=shape[2])
            elif len(shape) == 4:
                ap = ap.rearrange("p (a b c) -> p a b c", b=shape[2], c=shape[3])
            return T(ap, name), off_b + ((nb + 31) // 32) * 32

        barrier = P.barrier

        self.memset("dve", ones_bf[:, :], 1.0, [ones_bf.r])
        self.memset("dve", bd_bf[:, :], 0.0, [bd_bf.r])
        self.memset("dve", bd_bf[0:64, 0:64], 1.0, [bd_bf.r])
        self.memset("dve", bd_bf[64:128, 64:128], 1.0, [bd_bf.r])
        self.memset("dve", epsc[:, :], EPS, [epsc.r])
        self.ld(stage_f[:, :], perm_d, stage_f.r, writes=[stage_f.r])
        self.cp("dve", perm_bf[:, :], stage_f[:, :], [stage_f.r], [perm_bf.r])
        self.ld(ident_f[:, :], ident_d, ident_f.r, writes=[ident_f.r])
        self.ld(gains[:, :], gains_d, gains.r, writes=[gains.r])
        self.ld(gk_tm[:, :], gk_tm_d, gk_tm.r, writes=[gk_tm.r])
        self.ld(cT[:, :, :], cT_d.rearrange("p (c s) -> p c s", s=2), cT.r, writes=[cT.r])
        self.act(scT[:, :, :], cT[:, :, :], AF.Silu, [cT.r], [scT.r])

        for l in range(2):
            for k in ("w_in", "w_out", "w1", "w2"):
                cr = Res("cast%d%s" % (l, k))
                src, dst = Lw[l][k], Wb[l][k]
                nrow = src.shape[0]
                for r0 in range(0, nrow, 256):
                    P.dma("pool", (lambda s_, d_: (lambda e: e.dma_start(out=d_, in_=s_)))(src[r0:r0 + 256, :], dst[r0:r0 + 256, :]),
                          cr, writes=[rWb[l][k]])
                    P.nobar.add(cr.dsem)

        def xdram(Xd):
            return Xd.rearrange("(c p) t -> p c t", p=128)

        def load_x(Xd, rXd, b):
            x = self.nxt(xb)
            self.ld(x[:, :, :], xdram(Xd)[:, :, b * 512:(b + 1) * 512], x.r, reads=[rXd[b]], writes=[x.r])
            return x

        def store_x(x, Xd, rXd, b):
            self.ld(xdram(Xd)[:, :, b * 512:(b + 1) * 512], x[:, :, :], x.r, reads=[x.r], writes=[rXd[b]])

        def modulation(l):
            mw0, o = aview(0, [128, 8, 256], F32, "mw0")
            mw1, o = aview(o, [128, 8, 256], F32, "mw1")
            mwbuf = [mw0, mw1]
            self.ld(modb[:, :], Lw[l]["mod_b"], modb.r, writes=[modb.r])
            self.ld(n1[:, :], Lw[l]["n1"], n1.r, writes=[n1.r])
            self.ld(n2[:, :], Lw[l]["n2"], n2.r, writes=[n2.r])
            pm = psC[0]
            mw = Lw[l]["mod_w"].rearrange("(c p) n -> p c n", p=128)
            for g in range(24):
                wt = self.nxt(mwbuf)
                self.ld(wt[:, :, :], mw[:, :, g * 256:(g + 1) * 256], wt.r, writes=[wt.r])
                for j in range(2):
                    ch = g * 2 + j
                    for kc in range(8):
                        self.mm(pm[:, ch * 2:ch * 2 + 2], wt[:, kc, j * 128:(j + 1) * 128], scT[:, kc, :],
                                kc == 0, kc == 7, [wt.r, scT.r], [pm.r])
            pmv = pm[:, 0:96].rearrange("p (c s) -> p c s", s=2)
            for s_ in range(2):
                self.tt("dve", modT[:, :, s_], pmv[:, :, s_], modb[:, :], ALU.add, [pm.r, modb.r], [modT.r])
            for s_ in range(2):
                self.stt(G1[:, :, s_], modT[:, 8:16, s_], 1.0, n1[:, :], ALU.add, ALU.mult, [modT.r, n1.r], [G1.r])
                self.stt(G2[:, :, s_], modT[:, 32:40, s_], 1.0, n2[:, :], ALU.add, ALU.mult, [modT.r, n2.r], [G2.r])
            barrier()

        def norm_mod(x, b, G, shift_base, pa=None):
            s_ = 0 if b < 2 else 1
            self.act(sqb[:, :, :], x[:, :, :], AF.Square, [x.r], [sqb.r])
            if pa is None:
                pa = self.nxt(psB)
            for kc in range(8):
                self.mm(pa[:, :], ones_bf[:, :], sqb[:, kc, :], kc == 0, kc == 7, [ones_bf.r, sqb.r], [pa.r])
            rt = self.nxt(rtmp)
            self.act(rt[:, :], pa[:, :], AF.Ln, [pa.r, epsc.r], [rt.r], scale=1.0 / 1024.0, bias=epsc[:, 0:1])
            self.act(Rb[:, :], rt[:, :], AF.Exp, [rt.r], [Rb.r], scale=-0.5)
            h = self.nxt(hT)
            for kc in range(8):
                ft = self.nxt(ftmp)
                self.tt("dve", ft[:, :], x[:, kc, :], Rb[:, :], ALU.mult, [x.r, Rb.r], [ft.r])
                self.act(h[:, kc, :], ft[:, :], AF.Identity, [ft.r, G.r, modT.r], [h.r],
                         scale=G[:, kc, s_:s_ + 1], bias=modT[:, shift_base + kc, s_:s_ + 1])
            return h

        def load_w(Wd, rW, KC_, n0, ncols):
            wt = self.nxt(wbuf)
            v = wt.t[:, 0:KC_ * ncols].rearrange("p (c n) -> p c n", n=ncols)
            self.ld(v, Wd.rearrange("(c p) n -> p c n", p=128)[:, :, n0:n0 + ncols], wt.r, reads=[rW], writes=[wt.r])
            return wt, v

        def resid_mlp(l, x, b):
            s_ = 0 if b < 2 else 1
            hid, _ = aview(0, [128, 32, 512], BF16, "hid")
            h2 = norm_mod(x, b, G2, 24)
            for g in range(8):
                wt, wv = load_w(Wb[l]["w1"], rWb[l]["w1"], 8, g * 512, 512)
                for j in range(4):
                    pm = self.nxt(psA)
                    for kc in range(8):
                        self.mm(pm[:, :], wv[:, kc, j * 128:(j + 1) * 128], h2[:, kc, :], kc == 0, kc == 7, [wt.r, h2.r], [pm.r])
                    rl = self.nxt(ftmp)
                    self.act(rl[:, :], pm[:, :], AF.Relu, [pm.r], [rl.r])
                    self.tt("pool", hid[:, g * 4 + j, :], rl[:, :], rl[:, :], ALU.mult, [rl.r], [hid.r])
            for n in range(8):
                wt = self.nxt(wbuf)
                wv = wt.t[:, 0:4096].rearrange("p (c n) -> p c n", n=128)
                self.ld(wv, Wb[l]["w2"].rearrange("(c p) n -> p c n", p=128)[:, :, n * 128:(n + 1) * 128], wt.r,
                        reads=[rWb[l]["w2"]], writes=[wt.r])
                pm = self.nxt(psA)
                for hc in range(32):
                    self.mm(pm[:, :], wv[:, hc, :], hid[:, hc, :], hc == 0, hc == 31, [wt.r, hid.r], [pm.r])
                self.stt(x[:, n, :], pm[:, :], modT[:, 40 + n, s_:s_ + 1], x[:, n, :], ALU.mult, ALU.add,
                         [pm.r, modT.r, x.r], [x.r])

        def out_proj_resid(l, x, b, OTd, rOTd, ot=None):
            s_ = 0 if b < 2 else 1
            if ot is None:
                ot = self.nxt(hT)
                self.ld(ot[:, :, :], OTd.rearrange("c p t -> p c t")[:, :, b * 512:(b + 1) * 512], ot.r,
                        reads=[rOTd[c_][b] for c_ in range(8)], writes=[ot.r])
            for g in range(2):
                wt, wv = load_w(Wb[l]["w_out"], rWb[l]["w_out"], 8, g * 512, 512)
                for j in range(4):
                    n = g * 4 + j
                    pm = self.nxt(psA)
                    for kc in range(8):
                        self.mm(pm[:, :], wv[:, kc, j * 128:(j + 1) * 128], ot[:, kc, :], kc == 0, kc == 7, [wt.r, ot.r], [pm.r])
                    self.stt(x[:, n, :], pm[:, :], modT[:, 16 + n, s_:s_ + 1], x[:, n, :], ALU.mult, ALU.add,
                             [pm.r, modT.r, x.r], [x.r])

        if stage >= 2:
            modulation(0)

        o = 0
        cosb, o = aview(o, [128, 512], F32, "cosb")
        sinb, o = aview(o, [128, 512], F32, "sinb")
        vst0, o = aview(o, [128, 640], BF16, "vst0")
        vst1, o = aview(o, [128, 640], BF16, "vst1")
        vst = [vst0, vst1]
        nv0, o = aview(o, [128, 640], F32, "nv0")
        nv1, o = aview(o, [128, 640], F32, "nv1")
        nvst = [nv0, nv1]
        nk0, o = aview(o, [128, 640], F32, "nk0")
        nk1, o = aview(o, [128, 640], F32, "nk1")
        nkst = [nk0, nk1]
        ksq, o = aview(o, [128, 640], F32, "ksq")
        kss, o = aview(o, [128, 10], F32, "kss")
        krs, o = aview(o, [128, 10], F32, "krs")

        def l0_fm_chunk(b, h, wv, j, qi, gi, rope):
            sl = slice(b * 512, (b + 1) * 512)
            pm = self.nxt(psA)
            for kc in range(8):
                self.mm(pm[:, :], wv[0][:, kc, j * 128:(j + 1) * 128], h[:, kc, :], kc == 0, kc == 7,
                        [wv[1].r, h.r], [pm.r])
            yield
            sq = self.nxt(btmp)
            self.act(sq[:, :], pm[:, :], AF.Square, [pm.r], [sq.r])
            pn = self.nxt(psB)
            self.mm(pn[:, :], bd_bf[:, :], sq[:, :], True, True, [bd_bf.r, sq.r], [pn.r])
            rt = self.nxt(rtmp)
            self.act(rt[:, :], pn[:, :], AF.Ln, [pn.r, epsc.r], [rt.r], scale=1.0 / 64.0, bias=epsc[:, 0:1])
            rr = self.nxt(rtmp)
            self.act(rr[:, :], rt[:, :], AF.Exp, [rt.r], [rr.r], scale=-0.5)
            qn = self.nxt(ftmp)
            self.tt("dve", qn[:, :], pm[:, :], rr[:, :], ALU.mult, [pm.r, rr.r], [qn.r])
            ob = self.nxt(ostg)
            if not rope:
                self.act(ob[:, :], qn[:, :], AF.Identity, [qn.r, gains.r], [ob.r], scale=gains[:, gi:gi + 1])
            else:
                qg = self.nxt(ftmp)
                self.ts("dve", qg[:, :], qn[:, :], gains[:, gi:gi + 1], None, ALU.mult, None, [qn.r, gains.r], [qg.r])
                qb_ = self.nxt(btmp)
                self.cp("act", qb_[:, :], qg[:, :], [qg.r], [qb_.r])
                pr = self.nxt(psB)
                self.mm(pr[:, :], perm_bf[:, :], qb_[:, :], True, True, [perm_bf.r, qb_.r], [pr.r])
                a = self.nxt(ftmp)
                self.tt("dve", a[:, :], qg[:, :], cosb[:, :], ALU.mult, [qg.r, cosb.r], [a.r])
                b2 = self.nxt(rtmp)
                self.tt("dve", b2[:, :], pr[:, :], sinb[:, :], ALU.mult, [pr.r, sinb.r], [b2.r])
                self.tt("pool", ob[:, :], a[:, :], b2[:, :], ALU.add, [a.r, b2.r], [ob.r])
            self.ld(QKT[qi, :, sl], ob[:, :], ob.r, reads=[ob.r], writes=[rQKT[qi][b]])

        def k_norm_tm(pk, ncols, c0, tok0):
            nh = ncols // 64
            self.act(ksq[:, 0:ncols], pk[:, 0:ncols], AF.Square, [pk.r], [ksq.r])
            self.P.op("dve", lambda e: e.tensor_reduce(out=kss[:, 0:nh], in_=ksq[:, 0:ncols].rearrange("p (h d) -> p h d", d=64),
                                                        axis=AX.X, op=ALU.add), reads=[ksq.r], writes=[kss.r])
            self.act(krs[:, 0:nh], kss[:, 0:nh], AF.Sqrt, [kss.r, epsc.r], [krs.r], scale=1.0 / 64.0, bias=epsc[:, 0:1])
            self.recip(kss[:, 0:nh], krs[:, 0:nh], [krs.r], [kss.r])
            nk = self.nxt(nkst)
            for hh in range(nh):
                self.stt(nk[:, c0 + hh * 64:c0 + (hh + 1) * 64], pk[:, hh * 64:(hh + 1) * 64], kss[:, hh:hh + 1],
                         gk_tm[:, c0 + hh * 64:c0 + (hh + 1) * 64], ALU.mult, ALU.mult, [pk.r, kss.r, gk_tm.r], [nk.r])
            self.ld(nk_d[tok0:tok0 + 128, c0:c0 + ncols], nk[:, c0:c0 + ncols], nk.r, reads=[nk.r], writes=[Res("nko")])

        KNB = int(os.environ.get('KNB', NB))
        for b in range(KNB if stage >= 3 else 0):
            samp = b >= 2
            x = load_x(xT_d, rXin, b)
            h = norm_mod(x, b, G1, 0)
            if samp:
                t0 = (b - 2) * 512
                self.ld(cosb[:, :], cos_d[:, t0:t0 + 512], cosb.r, writes=[cosb.r])
                self.ld(sinb[:, :], sin_d[:, t0:t0 + 512], sinb.r, writes=[sinb.r])
            wt, wv = load_w(Wb[0]["w_in"], rWb[0]["w_in"], 8, 0, 512)
            self.interleave((l0_fm_chunk(b, h, (wv, wt), j, j, 0, samp) for j in range(4)), int(os.environ.get('KL0W', 3)))
            wt1, wv1 = load_w(Wb[0]["w_in"], rWb[0]["w_in"], 8, 512, 512)
            self.interleave(iter([l0_fm_chunk(b, h, (wv1, wt1), 0, 4, 1, samp), l0_fm_chunk(b, h, (wv1, wt1), 2, 5, 2, False),
                                  l0_fm_chunk(b, h, (wv1, wt1), 3, 6, 2, False)]), int(os.environ.get('KL0W', 3)))
            for t in range(4):
                tsl = slice(t * 128, (t + 1) * 128)
                tok0 = b * 512 + t * 128
                pv = self.nxt(psA)
                for kc in range(8):
                    self.mm(pv[:, 0:128], h[:, kc, tsl], wv1[:, kc, 128:256], kc == 0, kc == 7, [h.r, wt1.r], [pv.r])
                vs = self.nxt(vst)
                if samp:
                    self.cp("act", vs[:, 0:128], pv[:, 0:128], [pv.r], [vs.r])
                else:
                    nv = self.nxt(nvst)
                    self.cp("dve", nv[:, 0:128], pv[:, 0:128], [pv.r], [nv.r])
                    self.ld(nv_d[tok0:tok0 + 128, 0:128], nv[:, 0:128], nv.r, reads=[nv.r], writes=[Res("nvo")])
                    self.cp("act", vs[:, 0:128], nv[:, 0:128], [nv.r], [vs.r])
                self.ld(V0[tok0:tok0 + 128, 0:128], vs[:, 0:128], vs.r, reads=[vs.r], writes=[rV0[b]])
                if not samp:
                    pk = self.nxt(psA)
                    for kc in range(8):
                        self.mm(pk[:, 0:128], h[:, kc, tsl], wv1[:, kc, 0:128], kc == 0, kc == 7, [h.r, wt1.r], [pk.r])
                    k_norm_tm(pk, 128, 0, tok0)
            wt2, wv2 = load_w(Wb[0]["w_in"], rWb[0]["w_in"], 8, 1024, 512)
            self.interleave(iter([l0_fm_chunk(b, h, (wv2, wt2), 0, 7, 2, False), l0_fm_chunk(b, h, (wv2, wt2), 1, 8, 2, False),
                                  l0_fm_chunk(b, h, (wv2, wt2), 2, 9, 3, False), l0_fm_chunk(b, h, (wv2, wt2), 3, 10, 3, False)]), int(os.environ.get('KL0W', 3)))
            wt3, wv3 = load_w(Wb[0]["w_in"], rWb[0]["w_in"], 8, 1536, 512)
            self.interleave(iter([l0_fm_chunk(b, h, (wv3, wt3), 0, 11, 3, False), l0_fm_chunk(b, h, (wv3, wt3), 1, 12, 3, False)]), int(os.environ.get('KL0W', 3)))
            if not samp:
                for t in range(4):
                    tsl = slice(t * 128, (t + 1) * 128)
                    tok0 = b * 512 + t * 128
                    pk2 = self.nxt(psA)
                    for kc in range(8):
                        self.mm(pk2[:, 0:256], h[:, kc, tsl], wv2[:, kc, 256:512], kc == 0, kc == 7, [h.r, wt2.r], [pk2.r])
                    for kc in range(8):
                        self.mm(pk2[:, 256:512], h[:, kc, tsl], wv3[:, kc, 0:256], kc == 0, kc == 7, [h.r, wt3.r], [pk2.r])
                    k_norm_tm(pk2, 512, 128, tok0)
            wt4, wv4 = load_w(Wb[0]["w_in"], rWb[0]["w_in"], 8, 2048, 256)
            for t in range(4):
                tsl = slice(t * 128, (t + 1) * 128)
                tok0 = b * 512 + t * 128
                pv2 = self.nxt(psA)
                for kc in range(8):
                    self.mm(pv2[:, 0:256], h[:, kc, tsl], wv3[:, kc, 256:512], kc == 0, kc == 7, [h.r, wt3.r], [pv2.r])
                for kc in range(8):
                    self.mm(pv2[:, 256:512], h[:, kc, tsl], wv4[:, kc, 0:256], kc == 0, kc == 7, [h.r, wt4.r], [pv2.r])
                vs = self.nxt(vst)
                if samp:
                    self.cp("act", vs[:, 128:640], pv2[:, :], [pv2.r], [vs.r])
                else:
                    nv = self.nxt(nvst)
                    self.cp("dve", nv[:, 128:640], pv2[:, :], [pv2.r], [nv.r])
                    self.ld(nv_d[tok0:tok0 + 128, 128:640], nv[:, 128:640], nv.r, reads=[nv.r], writes=[Res("nvo")])
                    self.cp("act", vs[:, 128:640], nv[:, 128:640], [nv.r], [vs.r])
                self.ld(V0[tok0:tok0 + 128, 128:640], vs[:, 128:640], vs.r, reads=[vs.r], writes=[rV0[b]])
        barrier()

        if stage >= 4:
            o = 0
            ck_sb, o = aview(o, [128, 2, 640], F32, "ck_sb")
            kct, o = aview(o, [128, 5, 256], BF16, "kct")
            self.ld(ck_sb[:, :, :], ck_d.rearrange("(t p) c -> p t c", p=128), ck_sb.r, writes=[ck_sb.r])
            for ci in range(5):
                pt = self.nxt(psA)
                for t in range(2):
                    self.tr(pt[:, t * 128:(t + 1) * 128], ck_sb[:, t, ci * 128:(ci + 1) * 128], ident_f[:, :],
                            [ck_sb.r, ident_f.r], [pt.r])
                self.cp("dve", kct[:, ci, :], pt[:, 0:256], [pt.r], [kct.r])
            self.ld(KC.rearrange("c p t -> p c t"), kct[:, :, :], kct.r, reads=[kct.r], writes=[rKC])
            barrier()

        def attn_unit(q_ap, k_fn, v_fn, nkc, nq, half, out_ap, rds, out_r, po=None, po_off=0, norm=True):
            psx = self.nxt(psA)
            for kc in range(nkc):
                self.mm(psx[:, kc * nq:(kc + 1) * nq], k_fn(kc), q_ap, True, True, rds, [psx.r])
            pT = self.nxt(btmp)
            self.act(pT[:, 0:nkc * nq], psx[:, 0:nkc * nq], AF.Exp, [psx.r], [pT.r], scale=0.125)
            if po is None:
                po = self.nxt(psB)
            for kc in range(nkc):
                self.mm(po[:, po_off:po_off + nq], v_fn(kc), pT[:, kc * nq:(kc + 1) * nq], kc == 0, kc == nkc - 1,
                        rds + [pT.r], [po.r])
            if norm:
                normalize(po, po_off, nq, half, out_ap, out_r)
            return po

        def normalize(po, po_off, nq, half, out_ap, out_r):
            rc = self.nxt(rtmp)
            if half == 0:
                self.recip(rc[0:64, 0:nq], po[64:128, po_off:po_off + nq], [po.r], [rc.r])
                self.tt("dve", out_ap, po[0:64, po_off:po_off + nq], rc[0:64, 0:nq], ALU.mult, [po.r, rc.r], [out_r])
            else:
                self.recip(rc[64:128, 0:nq], po[0:64, po_off:po_off + nq], [po.r], [rc.r])
                self.tt("dve", out_ap, po[64:128, po_off:po_off + nq], rc[64:128, 0:nq], ALU.mult, [po.r, rc.r], [out_r])

        if stage >= 4:
            o = 0
            qk2 = []
            for i in range(2):
                t_, o = aview(o, [128, 13, 256], BF16, "qk%d" % i)
                qk2.append(t_)
            ka2 = []
            for i in range(2):
                t_, o = aview(o, [128, 2, 256], BF16, "ka2_%d" % i)
                ka2.append(t_)
            vaE, vaO = [], []
            for i in range(2):
                t_, o = aview(o, [128, 2, 10, 128], BF16, "vaE%d" % i)
                vaE.append(t_)
                t_, o = aview(o, [128, 2, 10, 128], BF16, "vaO%d" % i)
                vaO.append(t_)
            ostp = []
            for i in range(2):
                t_, o = aview(o, [128, 8, 256], BF16, "ostp%d" % i)
                ostp.append(t_)
            for i in range(2):
                self.memset("pool", vaE[i][:, :, :, 64:128], 1.0, [vaE[i].r])
                self.memset("pool", vaO[i][:, :, :, 0:64], 1.0, [vaO[i].r])
            QKTv = QKT.rearrange("c p t -> p c t")
            for s_ in range(4):
                tb = s_ * 256
                b = s_ // 2
                qk = qk2[s_ % 2]
                k2 = ka2[s_ % 2]
                vE, vO = vaE[s_ % 2], vaO[s_ % 2]
                ost = ostp[s_ % 2]
                self.ld(qk[:, :, :], QKTv[:, :, tb:tb + 256], qk.r, reads=[rQKT[c_][b] for c_ in range(13)], writes=[qk.r])
                for g in range(2):
                    for hf in range(2):
                        self.ld(k2[hf * 64:(hf + 1) * 64, g, :], QKT[4, g * 64:(g + 1) * 64, tb:tb + 256], k2.r,
                                reads=[rQKT[4][b]], writes=[k2.r])
                for t in range(2):
                    src = V0[tb + t * 128:tb + (t + 1) * 128, :].rearrange("p (h d) -> p h d", d=64)
                    self.ld(vE[:, t, :, 0:64], src, vE.r, reads=[rV0[b]], writes=[vE.r])
                    self.ld(vO[:, t, :, 64:128], src, vO.r, reads=[rV0[b]], writes=[vO.r])
                for hd in range(16):
                    isA = hd < 8
                    h_ = hd if isA else hd - 8
                    hf = h_ % 2
                    psl = slice(hf * 64, (hf + 1) * 64)
                    va = vE if hf == 0 else vO
                    if isA:
                        g = h_ // 4
                        q_ap = qk[psl, h_ // 2, :]
                        k_fn = (lambda kc, g=g, psl=psl: k2[psl, g, kc * 128:(kc + 1) * 128])
                        v_fn = (lambda kc, g=g, va=va: va[:, kc, g, :])
                        rds = [qk.r, k2.r, va.r]
                    else:
                        q_ap = qk[psl, 5 + h_ // 2, :]
                        k_fn = (lambda kc, h_=h_, psl=psl: qk[psl, 9 + h_ // 2, kc * 128:(kc + 1) * 128])
                        v_fn = (lambda kc, h_=h_, va=va: va[:, kc, 2 + h_, :])
                        rds = [qk.r, va.r]
                    och = (h_ // 2) if isA else 4 + h_ // 2
                    attn_unit(q_ap, k_fn, v_fn, 2, 256, hf, ost[psl, och, :], rds, ost.r)
                self.ld(OT.rearrange("c p t -> p c t")[:, :, tb:tb + 256], ost[:, :, :], ost.r, reads=[ost.r],
                        writes=[rOT[c_][b] for c_ in range(8)])
            barrier()

        if stage >= 5:
            o = 0
            ka2s, o = aview(o, [128, 2, 2304], BF16, "ka2s")
            vEA, o = aview(o, [128, 18, 2, 128], BF16, "vEA")
            vOA, o = aview(o, [128, 18, 2, 128], BF16, "vOA")
            qa2 = []
            osa = []
            for i in range(2):
                t_, o = aview(o, [128, 2048], BF16, "qa%d" % i)
                qa2.append(t_)
                t_, o = aview(o, [128, 2048], BF16, "osa%d" % i)
                osa.append(t_)
            self.memset("pool", vEA[:, :, :, 64:128], 1.0, [vEA.r])
            self.memset("pool", vOA[:, :, :, 0:64], 1.0, [vOA.r])
            allq = [rQKT[4][b_] for b_ in range(2, 6)]
            for g in range(2):
                for hf in range(2):
                    self.ld(ka2s[hf * 64:(hf + 1) * 64, g, 0:2048], QKT[4, g * 64:(g + 1) * 64, 1024:3072], ka2s.r,
                            reads=allq, writes=[ka2s.r])
                    self.ld(ka2s[hf * 64:(hf + 1) * 64, g, 2048:2304], KC[0, g * 64:(g + 1) * 64, :], ka2s.r,
                            reads=[rKC], writes=[ka2s.r])
                srcv = V0[1024:3072, g * 64:(g + 1) * 64].rearrange("(t p) d -> p t d", p=128)
                self.ld(vEA[:, 0:16, g, 0:64], srcv, vEA.r, reads=rV0[2:6], writes=[vEA.r])
                self.ld(vOA[:, 0:16, g, 64:128], srcv, vOA.r, reads=rV0[2:6], writes=[vOA.r])
                srcc = cv_d[:, g * 64:(g + 1) * 64].rearrange("(t p) d -> p t d", p=128)
                self.ld(vEA[:, 16:18, g, 0:64], srcc, vEA.r, writes=[vEA.r], q="pool")
                self.ld(vOA[:, 16:18, g, 64:128], srcc, vOA.r, writes=[vOA.r], q="pool")
            for hp in range(4):
                qa = qa2[hp % 2]
                os_ = osa[hp % 2]
                self.ld(qa[:, :], QKT[hp, :, 1024:3072], qa.r, reads=[rQKT[hp][b_] for b_ in range(2, 6)], writes=[qa.r])
                for hf in range(2):
                    h_ = hp * 2 + hf
                    g = h_ // 4
                    psl = slice(hf * 64, (hf + 1) * 64)
                    va = vEA if hf == 0 else vOA
                    for qb_i in range(4):
                        po = self.nxt(psB)

                        def s_mm(kc, g=g, psl=psl, qb_i=qb_i, qa=qa):
                            psx = self.nxt(psA)
                            self.mm(psx[:, :], ka2s[psl, g, kc * 128:(kc + 1) * 128], qa[psl, qb_i * 512:(qb_i + 1) * 512],
                                    True, True, [ka2s.r, qa.r], [psx.r])
                            return psx
                        pend = [s_mm(0), s_mm(1)]
                        for kc in range(18):
                            psx = pend.pop(0)
                            if kc + 2 < 18:
                                pend.append(s_mm(kc + 2))
                            pT = self.nxt(btmp)
                            self.act(pT[:, :], psx[:, :], AF.Exp, [psx.r], [pT.r], scale=0.125)
                            self.mm(po[:, :], va[:, kc, g, :], pT[:, :], kc == 0, kc == 17, [va.r, pT.r], [po.r])
                        normalize(po, 0, 512, hf, os_[psl, qb_i * 512:(qb_i + 1) * 512], os_.r)
                self.ld(OT[hp, :, 1024:3072], os_[:, :], os_.r, reads=[os_.r], writes=[rOT[hp][b_] for b_ in range(2, 6)])
            barrier()

        if stage >= 6:
            o = 0
            qb2, kb2, osb, bia = [], [], [], []
            for i in range(2):
                t_, o = aview(o, [128, 2048], BF16, "qb%d" % i)
                qb2.append(t_)
                t_, o = aview(o, [128, 2304], BF16, "kb%d" % i)
                kb2.append(t_)
                t_, o = aview(o, [128, 2048], BF16, "osb%d" % i)
                osb.append(t_)
            bia_t, o = aview(o, [128, 2048], F32, "bia")
            vsets = []
            for hf in range(2):
                e_, o = aview(o, [128, 16, 128], BF16, "vbE%d" % hf)
                d_, o = aview(o, [128, 15, 128], BF16, "vbOd%d" % hf)
                c_, o = aview(o, [128, 2, 128], BF16, "vbC%d" % hf)
                vsets.append((e_, d_, c_))
                onesl = slice(64, 128) if hf == 0 else slice(0, 64)
                for t_ in (e_, d_, c_):
                    self.memset("pool", t_[:, :, onesl], 1.0, [t_.r])
            for hp in range(4):
                qb_t = qb2[hp % 2]
                kb_t = kb2[hp % 2]
                os_ = osb[hp % 2]
                rq = [rQKT[5 + hp][b_] for b_ in range(2, 6)]
                rk = [rQKT[9 + hp][b_] for b_ in range(2, 6)]
                self.ld(qb_t[:, :], QKT[5 + hp, :, 1024:3072], qb_t.r, reads=rq, writes=[qb_t.r])
                self.ld(kb_t[:, 0:2048], QKT[9 + hp, :, 1024:3072], kb_t.r, reads=rk, writes=[kb_t.r])
                self.ld(kb_t[:, 2048:2304], KC[1 + hp, :, :], kb_t.r, reads=[rKC], writes=[kb_t.r])
                for hf in range(2):
                    h_ = hp * 2 + hf
                    psl = slice(hf * 64, (hf + 1) * 64)
                    vsl = slice(0, 64) if hf == 0 else slice(64, 128)
                    vE_, vD_, vC_ = vsets[hf]
                    c0 = 128 + h_ * 64
                    self.ld(vE_[:, :, vsl], V0[1024:3072, c0:c0 + 64].rearrange("(t p) d -> p t d", p=128), vE_.r,
                            reads=rV0[2:6], writes=[vE_.r])
                    self.ld(vD_[:, :, vsl], V0[1088:3008, c0:c0 + 64].rearrange("(t p) d -> p t d", p=128), vD_.r,
                            reads=rV0[2:6], writes=[vD_.r])
                    self.ld(vC_[:, :, vsl], cv_d[:, c0:c0 + 64].rearrange("(t p) d -> p t d", p=128), vC_.r,
                            writes=[vC_.r], q="pool")
                    self.ld(bia_t[:, :], bias_d[h_], bia_t.r, writes=[bia_t.r])
                    po = None

                    def b_scores(r, psl=psl, qb_t=qb_t, kb_t=kb_t):
                        rs = min(max(r - 4, 0), 24)
                        psx = self.nxt(psA)
                        qv = qb_t[psl, r * 64:(r + 1) * 64]
                        for i in range(4):
                            k0 = (rs + 2 * i) * 64
                            self.mm(psx[:, i * 64:(i + 1) * 64], kb_t[psl, k0:k0 + 128], qv, True, True, [kb_t.r, qb_t.r], [psx.r])
                        for c_ in range(2):
                            self.mm(psx[:, 256 + c_ * 64:256 + (c_ + 1) * 64], kb_t[psl, 2048 + c_ * 128:2048 + (c_ + 1) * 128], qv,
                                    True, True, [kb_t.r, qb_t.r], [psx.r])
                        return psx
                    pend = [b_scores(0), b_scores(1)]
                    for r in range(32):
                        rs = min(max(r - 4, 0), 24)
                        cl = r - rs
                        psx = pend.pop(0)
                        if r + 2 < 32:
                            pend.append(b_scores(r + 2))
                        sb_ = self.nxt(ftmp)
                        self.stt(sb_[:, 0:256], psx[:, 0:256], 0.125, bia_t[:, cl * 256:(cl + 1) * 256], ALU.mult, ALU.add,
                                 [psx.r, bia_t.r], [sb_.r])
                        pT = self.nxt(btmp)
                        self.act(pT[:, 0:256], sb_[:, 0:256], AF.Exp, [sb_.r], [pT.r])
                        self.act(pT[:, 256:384], psx[:, 256:384], AF.Exp, [psx.r], [pT.r], scale=0.125)
                        if r % 8 == 0:
                            po = self.nxt(psB)
                        off = (r % 8) * 64
                        for i in range(4):
                            rr_ = rs + 2 * i
                            vt = vE_[:, rr_ // 2, :] if rs % 2 == 0 else vD_[:, (rr_ - 1) // 2, :]
                            vr = vE_.r if rs % 2 == 0 else vD_.r
                            self.mm(po[:, off:off + 64], vt, pT[:, i * 64:(i + 1) * 64], i == 0, False, [vr, pT.r], [po.r])
                        for c_ in range(2):
                            self.mm(po[:, off:off + 64], vC_[:, c_, :], pT[:, 256 + c_ * 64:256 + (c_ + 1) * 64], False, c_ == 1,
                                    [vC_.r, pT.r], [po.r])
                        if r % 8 == 7:
                            r0 = r - 7
                            normalize(po, 0, 512, hf, os_[psl, r0 * 64:(r0 + 8) * 64], os_.r)
                self.ld(OT[4 + hp, :, 1024:3072], os_[:, :], os_.r, reads=[os_.r], writes=[rOT[4 + hp][b_] for b_ in range(2, 6)])
            barrier()

        L1 = stage >= 20
        if stage >= 7:
            for b in range(NB):
                x = load_x(xT_d, rXin, b)
                out_proj_resid(0, x, b, OT, rOT)
                if stage >= 8:
                    resid_mlp(0, x, b)
                store_x(x, X1 if L1 else yT_d, rX1 if L1 else rY, b)
            barrier()
        else:
            for b in range(NB):
                x = load_x(xT_d, rXin, b)
                store_x(x, yT_d, rY, b)

        if L1:
            NCH = 48
            QKV1 = self.dscr("QKV1", [24, 128, NTOK])
            ZT = self.dscr("ZT", [8, 128, NTOK])
            QN = self.dscr("QN", [8, 128, NTOK])
            KN = self.dscr("KN", [8, 128, NTOK])
            KTM = self.dscr("KTM", [NTOK, 8, 128])
            VTM = self.dscr("VTM", [NTOK, 8, 128])
            TBS = self.dscr("TBS", [NCH, 128, 8, 64])
            ATT = self.dscr("ATT", [NCH, 128, 8, 64])
            QG = self.dscr("QG", [2, NCH, 128, 8, 64])
            OTF = self.dscr("OTF", [2, 8, 128, NTOK])
            rQKV1 = [[Res("rQKV1") for _ in range(NB)] for _ in range(24)]
            rZT = [[Res("rZT") for _ in range(NB)] for _ in range(8)]
            rQN = [Res("rQN%d" % i) for i in range(8)]
            rKN = [Res("rKN%d" % i) for i in range(8)]
            rKTM, rVTM = Res("rKTM"), Res("rVTM")
            rTBS = [Res("rTBS") for _ in range(NCH)]
            rATT = [Res("rATT") for _ in range(NCH)]
            rQG = [[Res("rQG") for _ in range(NCH)] for _ in range(2)]
            rOTF = [[Res("rOTF") for _ in range(NCH)] for _ in range(2)]

            l1c = self.sb("l1c", [128, 1024], F32)
            self.ld(l1c[:, :], l1c_d, l1c.r, writes=[l1c.r])
            tri2 = l1c[:, 0:64]
            triT2 = l1c[:, 64:128]
            ident2 = l1c[:, 128:192]
            nmincl = l1c[:, 192:256]
            mstrict = l1c[:, 256:320]
            nmstrT = l1c[:, 320:384]
            bdones = l1c[:, 384:512]
            sel0 = l1c[:, 512:640]
            sel1 = l1c[:, 640:768]
            onesf = l1c[:, 768:896]
            identf = l1c[:, 896:1024]
            l1m = self.sb("l1m", [128, 6, 64], F32)
            self.ld(l1m[:, :, :], l1m_d.rearrange("p (l q) -> p l q", q=64), l1m.r, writes=[l1m.r])
            convw = self.sb("convw", [128, 24, 3], F32)
            self.ld(convw[:, :, :], convw_d.rearrange("p (c j) -> p c j", j=3), convw.r, writes=[convw.r])
            nexpA = self.sb("nexpA", [128, 16], F32)
            dtb = self.sb("dtb", [128, 16], F32)
            onorm = self.sb("onorm", [128, 1], F32)
            onec = self.sb("onec", [128, 1], F32)
            self.memset("dve", onec[:, :], 1.0, [onec.r])
            self.ld(nexpA[:, :], alog_d, nexpA.r, writes=[nexpA.r])
            self.ld(dtb[:, :], dtb_d, dtb.r, writes=[dtb.r])
            self.ld(onorm[:, :], onorm_d, onorm.r, writes=[onorm.r])
            self.act(nexpA[:, :], nexpA[:, :], AF.Exp, [nexpA.r], [nexpA.r])
            self.ts("dve", nexpA[:, :], nexpA[:, :], -1.0, None, ALU.mult, None, [nexpA.r], [nexpA.r])
            la_tm = self.sb("la_tm", [128, 24, 16], F32)
            be_tm = self.sb("be_tm", [128, 24, 16], F32)
            negeg = self.sb("negeg", [128, NCH, 8], F32)
            ksc = self.sb("ksc", [128, NCH, 8], F32)
            egl = self.sb("egl", [128, NCH, 16], F32)

            modulation(1)

            def a1_block(b):
                x = load_x(X1, rX1, b)
                h = norm_mod(x, b, G1, 0, pa=ps[4])
                sl = slice(b * 512, (b + 1) * 512)
                for grp in range(8):
                    wt, wv = load_w(Wb[1]["w_in"], rWb[1]["w_in"], 8, grp * 512, 512)
                    for j in range(4):
                        c = grp * 4 + j
                        pm = self.nxt(psA)
                        for kc in range(8):
                            self.mm(pm[:, :], wv[:, kc, j * 128:(j + 1) * 128], h[:, kc, :], kc == 0, kc == 7, [wt.r, h.r], [pm.r])
                        ob = self.nxt(ostg)
                        if c < 24:
                            self.cp("act", ob[:, :], pm[:, :], [pm.r], [ob.r])
                            self.ld(QKV1[c, :, sl], ob[:, :], ob.r, reads=[ob.r], writes=[rQKV1[c][b]])
                        else:
                            self.act(ob[:, :], pm[:, :], AF.Silu, [pm.r], [ob.r])
                            self.ld(ZT[c - 24, :, sl], ob[:, :], ob.r, reads=[ob.r], writes=[rZT[c - 24][b]])
                        yield
                wt, wv = load_w(Wb[1]["w_in"], rWb[1]["w_in"], 8, 4096, 32)
                for t in range(4):
                    tt_ = b * 4 + t
                    tsl = slice(t * 128, (t + 1) * 128)
                    pab = self.nxt(psA)
                    for kc in range(8):
                        self.mm(pab[:, 0:32], h[:, kc, tsl], wv[:, kc, 0:32], kc == 0, kc == 7, [h.r, wt.r], [pab.r])
                    t1 = self.nxt(rtmp)
                    self.tt("dve", t1[:, 0:16], pab[:, 0:16], dtb[:, :], ALU.add, [pab.r, dtb.r], [t1.r])
                    self.act(t1[:, 16:32], t1[:, 0:16], AF.Exp, [t1.r], [t1.r])
                    self.act(t1[:, 32:48], t1[:, 16:32], AF.Ln, [t1.r, onec.r], [t1.r], bias=onec[:, 0:1])
                    self.tt("dve", la_tm[:, tt_, :], t1[:, 32:48], nexpA[:, :], ALU.mult, [t1.r, nexpA.r], [la_tm.r])
                    self.act(be_tm[:, tt_, :], pab[:, 16:32], AF.Sigmoid, [pab.r], [be_tm.r])
                    yield

            if stage >= 21:
                o = 0
                diagW, o = aview(o, [128, 72, 128], BF16, "diagW")
                raws, sfps, tmbs, kns = [], [], [], []
                for i in range(5):
                    t_, o = aview(o, [128, 514], BF16, "raw%d" % i)
                    raws.append(t_)
                    t_, o = aview(o, [128, 512], F32, "sfp%d" % i)
                    sfps.append(t_)
                    t_, o = aview(o, [128, 4, 128], BF16, "tmb%d" % i)
                    tmbs.append(t_)
                    t_, o = aview(o, [128, 512], F32, "kn%d" % i)
                    kns.append(t_)
                for c in range(24):
                    for j in range(3):
                        self.ts("dve", diagW[:, c * 3 + j, :], identf, convw[:, c, j:j + 1], None, ALU.mult, None,
                                [l1c.r, convw.r], [diagW.r])
                pieces = [(0, 256, 0, 256), (256, 256, 256, 256), (512, 256, 512, 256), (768, 256, 768, 256)]
                pieces += [(1024 + i * 512, 512, 1024, 2048) for i in range(4)]
                def b1_unit(p0, L, s0, Tq, c):
                        bl = p0 // 512
                        raw = self.nxt(raws)
                        sfp = self.nxt(sfps)
                        lo = max(p0 - 1, s0)
                        hi = min(p0 + L + 1, s0 + Tq)
                        rdeps = [rQKV1[c][bb] for bb in range(max(bl - 1, 0), min(bl + 2, NB))]
                        if lo > p0 - 1:
                            self.memset("pool", raw[:, 0:1], 0.0, [raw.r])
                        if hi < p0 + L + 1:
                            self.memset("pool", raw[:, L + 1:L + 2], 0.0, [raw.r])
                        self.ld(raw[:, lo - (p0 - 1):hi - (p0 - 1)], QKV1[c, :, lo:hi], raw.r, reads=rdeps, writes=[raw.r])
                        pc_ = self.nxt(psA)
                        for j in range(3):
                            self.mm(pc_[:, 0:L], diagW[:, c * 3 + j, :], raw[:, j:j + L], j == 0, j == 2, [diagW.r, raw.r], [pc_.r])
                        self.act(sfp[:, 0:L], pc_[:, 0:L], AF.Silu, [pc_.r], [sfp.r])
                        yield
                        src_tm = sfp
                        if c < 16:
                            sq = self.nxt(btmp)
                            self.tt("pool", sq[:, 0:L], sfp[:, 0:L], sfp[:, 0:L], ALU.mult, [sfp.r], [sq.r])
                            pn = self.nxt(psB1)
                            self.mm(pn[:, 0:L], ones_bf[:, :], sq[:, 0:L], True, True, [ones_bf.r, sq.r], [pn.r])
                            rt = self.nxt(rtmp)
                            self.act(rt[:, 0:L], pn[:, 0:L], AF.Ln, [pn.r, epsc.r], [rt.r], bias=epsc[:, 0:1])
                            rr = self.nxt(rtmp)
                            self.act(rr[:, 0:L], rt[:, 0:L], AF.Exp, [rt.r], [rr.r], scale=-0.5)
                            ob = self.nxt(ostg)
                            if c < 8:
                                self.stt(ob[:, 0:L], sfp[:, 0:L], 128.0 ** -0.5, rr[:, 0:L], ALU.mult, ALU.mult, [sfp.r, rr.r], [ob.r])
                                self.ld(QN[c, :, p0:p0 + L], ob[:, 0:L], ob.r, reads=[ob.r], writes=[rQN[c]])
                            else:
                                kn = self.nxt(kns)
                                self.tt("dve", kn[:, 0:L], sfp[:, 0:L], rr[:, 0:L], ALU.mult, [sfp.r, rr.r], [kn.r])
                                self.cp("pool", ob[:, 0:L], kn[:, 0:L], [kn.r], [ob.r])
                                self.ld(KN[c - 8, :, p0:p0 + L], ob[:, 0:L], ob.r, reads=[ob.r], writes=[rKN[c - 8]])
                                src_tm = kn
                        yield
                        if c >= 8:
                            hh = (c - 8) % 8
                            dst, rdst = (KTM, rKTM) if c < 16 else (VTM, rVTM)
                            pt = self.nxt(psA)
                            nt = L // 128
                            for i in range(nt):
                                self.tr(pt[:, i * 128:(i + 1) * 128], src_tm[:, i * 128:(i + 1) * 128], identf,
                                        [src_tm.r, l1c.r], [pt.r])
                            tmb = self.nxt(tmbs)
                            self.cp("act", tmb[:, 0:nt, :], pt[:, 0:L].rearrange("p (i d) -> p i d", d=128), [pt.r], [tmb.r])
                            self.ld(dst[p0:p0 + L, hh, :].rearrange("(i p) d -> p i d", p=128), tmb[:, 0:nt, :], tmb.r,
                                    reads=[tmb.r], writes=[rdst])
                def units(pidx):
                    return (b1_unit(pieces[pi][0], pieces[pi][1], pieces[pi][2], pieces[pi][3], c) for pi in pidx for c in range(24))

                self.interleave2(a1_block(0), iter([]), 3)
                self.interleave2(a1_block(1), iter([]), 3)
                self.interleave2(a1_block(2), units([0, 1]), 4)
                self.interleave2(a1_block(3), units([2, 3]), 4)
                self.interleave2(a1_block(4), units([4]), 4)
                self.interleave2(a1_block(5), units([5]), 4)
                self.interleave(units([6, 7]), 4)
                barrier()

            if stage >= 22:
                o = 0
                NSET = 3
                kTp, qTp = [], []
                for i in range(1):
                    t_, o = aview(o, [128, 8, 512], BF16, "kTp%d" % i)
                    kTp.append(t_)
                    t_, o = aview(o, [128, 8, 512], BF16, "qTp%d" % i)
                    qTp.append(t_)
                sets = []
                for i in range(NSET):
                    d_ = {}
                    for nm in ("b0", "b1", "b2", "b3", "b4"):
                        d_[nm], o = aview(o, [128, 8, 64], F32, "%s_%d" % (nm, i))
                    d_["eg"] = d_["b2"]
                    for nm in ("P0", "Q0", "T", "U", "X", "Xp", "attb", "tbb", "qg0", "qg1"):
                        d_[nm], o = aview(o, [128, 8, 64], BF16, "%s_%d" % (nm, i))
                    d_["g2"], o = aview(o, [128, 8], F32, "g2_%d" % i)
                    d_["be2"], o = aview(o, [128, 8], F32, "be2_%d" % i)
                    d_["sm"], o = aview(o, [128, 16], F32, "sm_%d" % i)
                    sets.append(d_)

                def bc_h(ap2):
                    return ap2.unsqueeze(2).to_broadcast([128, 8, 64])

                def bc_m(ap2):
                    return ap2.unsqueeze(1).to_broadcast([128, 8, 64])

                def v3(bank):
                    return bank[:, :].rearrange("p (h i) -> p h i", i=64)

                DS = (slice(0, 64), slice(64, 128))
                for pc in range(6):
                    kT = self.nxt(kTp)
                    qT = self.nxt(qTp)
                    self.ld(kT[:, :, :], KN.rearrange("h p t -> p h t")[:, :, pc * 512:(pc + 1) * 512], kT.r, reads=rKN, writes=[kT.r])
                    self.ld(qT[:, :, :], QN.rearrange("h p t -> p h t")[:, :, pc * 512:(pc + 1) * 512], qT.r, reads=rQN, writes=[qT.r])
                    def chunk_gen(ci, pc=pc, kT=kT, qT=qT):
                        cg = pc * 8 + ci
                        W = sets[ci % NSET] if ci < 6 else sets[ci - 6]
                        g2, be2, sm = W["g2"], W["be2"], W["sm"]
                        tt_ = cg // 2
                        hb = (cg % 2) * 64
                        hsl = slice(hb, hb + 64)
                        csl = slice(ci * 64, (ci + 1) * 64)
                        la_c = la_tm[hsl, tt_, :]
                        pg = self.nxt(ps)
                        self.mm(pg[0:64, 0:8], tri2[hsl, :], la_c[:, 0:8], True, True, [l1c.r, la_tm.r], [pg.r])
                        self.mm(pg[64:128, 0:8], triT2[hsl, :], la_c[:, 8:16], True, True, [l1c.r, la_tm.r], [pg.r])
                        self.mm(pg[:, 16:32], onesf[hsl, :], la_c, True, True, [l1c.r, la_tm.r], [pg.r])
                        self.cp("dve", g2[:, :], pg[:, 0:8], [pg.r], [g2.r])
                        self.act(egl[:, cg, :], pg[:, 16:32], AF.Exp, [pg.r], [egl.r])
                        self.tt("dve", sm[0:64, 0:8], pg[0:64, 16:24], g2[0:64, :], ALU.subtract, [pg.r, g2.r], [sm.r])
                        self.tt("dve", sm[64:128, 0:8], pg[64:128, 24:32], g2[64:128, :], ALU.subtract, [pg.r, g2.r], [sm.r])
                        self.act(ksc[:, cg, :], sm[:, 0:8], AF.Exp, [sm.r], [ksc.r])
                        self.act(sm[:, 8:16], g2[:, :], AF.Exp, [g2.r], [sm.r])
                        self.ts("pool", negeg[:, cg, :], sm[:, 8:16], -1.0, None, ALU.mult, None, [sm.r], [negeg.r])
                        self.cp("act", be2[0:64, :], be_tm[hsl, tt_, 0:8], [be_tm.r], [be2.r])
                        self.cp("act", be2[64:128, :], be_tm[hsl, tt_, 8:16], [be_tm.r], [be2.r])
                        yield
                        dG, dB = W["b0"], W["b1"]
                        self.tt("dve", dG[:, :, :], bc_m(ident2), bc_h(g2[:, :]), ALU.mult, [l1c.r, g2.r], [dG.r])
                        self.tt("dve", dB[:, :, :], bc_m(ident2), bc_h(be2[:, :]), ALU.mult, [l1c.r, be2.r], [dB.r])
                        dGf = dG[:, :, :].rearrange("p h i -> p (h i)")
                        dBf = dB[:, :, :].rearrange("p h i -> p (h i)")
                        for d in range(2):
                            pe_ = self.nxt(ps)
                            self.mm(pe_[:, :], sel0 if d == 0 else sel1, dGf, True, True, [l1c.r, dG.r], [pe_.r])
                            eg = W["eg"]
                            self.act(eg[:, :, :], v3(pe_), AF.Exp, [pe_.r], [eg.r])
                            qg_ = W["qg%d" % d]
                            self.tt("pool", qg_[:, :, :], qT[:, :, csl], eg[:, :, :], ALU.mult, [qT.r, eg.r], [qg_.r])
                            self.ld(QG[d, cg], qg_[:, :, :], qg_.r, reads=[qg_.r], writes=[rQG[d][cg]])
                        yield
                        pGr = self.nxt(ps)
                        pBr = self.nxt(ps)
                        self.mm(pGr[:, :], bdones, dGf, True, True, [l1c.r, dG.r], [pGr.r])
                        self.mm(pBr[:, :], bdones, dBf, True, True, [l1c.r, dB.r], [pBr.r])
                        E, E1, DTi = W["b2"], W["b3"], W["b4"]
                        self.tt("dve", E[:, :, :], v3(pGr), bc_h(g2[:, :]), ALU.subtract, [pGr.r, g2.r], [E.r])
                        self.tt("pool", E1[:, :, :], E[:, :, :], bc_m(nmincl), ALU.add, [E.r, l1c.r], [E1.r])
                        self.act(DTi[:, :, :], E1[:, :, :], AF.Exp, [E1.r], [DTi.r])
                        E2 = W["b3"]
                        self.tt("pool", E2[:, :, :], bc_m(nmstrT), E[:, :, :], ALU.subtract, [E.r, l1c.r, DTi.r], [E2.r])
                        Dm = W["b0"]
                        self.act(Dm[:, :, :], E2[:, :, :], AF.Exp, [E2.r], [Dm.r])
                        self.tt("pool", Dm[:, :, :], Dm[:, :, :], bc_h(be2[:, :]), ALU.mult, [Dm.r, be2.r], [Dm.r])
                        BrM = W["b1"]
                        self.tt("dve", BrM[:, :, :], v3(pBr), bc_m(mstrict), ALU.mult, [pBr.r, l1c.r], [BrM.r])
                        self.tt("pool", BrM[:, :, :], BrM[:, :, :], DTi[:, :, :], ALU.mult, [BrM.r, DTi.r], [BrM.r])
                        pkk = self.nxt(ps)
                        pqk = self.nxt(ps)
                        for hh in range(8):
                            for d in range(2):
                                self.mm(pkk[DS[d], hh * 64:(hh + 1) * 64], kT[:, hh, csl], kT[:, hh, csl], True, True, [kT.r], [pkk.r])
                        for hh in range(8):
                            for d in range(2):
                                self.mm(pqk[DS[d], hh * 64:(hh + 1) * 64], kT[:, hh, csl], qT[:, hh, csl], True, True, [kT.r, qT.r], [pqk.r])
                        P0, Q0, Tm, Um, Xm, Xpm = W["P0"], W["Q0"], W["T"], W["U"], W["X"], W["Xp"]
                        attb, tbb = W["attb"], W["tbb"]
                        self.tt("dve", P0[:, :, :], v3(pkk), Dm[:, :, :], ALU.mult, [pkk.r, Dm.r], [P0.r])
                        self.tt("dve", Q0[:, :, :], v3(pkk), BrM[:, :, :], ALU.mult, [pkk.r, BrM.r], [Q0.r])
                        self.tt("dve", attb[:, :, :], v3(pqk), DTi[:, :, :], ALU.mult, [pqk.r, DTi.r], [attb.r])
                        self.ld(ATT[cg], attb[:, :, :], attb.r, reads=[attb.r], writes=[rATT[cg]])

                        def mk(li):
                            return bc_m(l1m[:, li, :])
                        self.tt("pool", Xm[:, :, :], P0[:, :, :], mk(0), ALU.mult, [P0.r, l1m.r], [Xm.r])
                        self.tt("pool", Tm[:, :, :], bc_m(ident2), Xm[:, :, :], ALU.subtract, [l1c.r, Xm.r], [Tm.r])
                        self.tt("dve", Xpm[:, :, :], Q0[:, :, :], mk(0), ALU.mult, [Q0.r, l1m.r], [Xpm.r])
                        self.tt("dve", Um[:, :, :], bc_m(ident2), Xpm[:, :, :], ALU.subtract, [l1c.r, Xpm.r], [Um.r])
                        yield
                        for li in range(1, 6):
                            last = (li == 5)
                            if not last:
                                pX = self.nxt(ps)
                                for hh in range(8):
                                    for d in range(2):
                                        self.mm(pX[DS[d], hh * 64:(hh + 1) * 64], Q0[DS[d], hh, :], Tm[DS[d], hh, :], True, True,
                                                [Q0.r, Tm.r], [pX.r])
                                self.tt("dve", Xm[:, :, :], v3(pX), mk(li), ALU.mult, [pX.r, l1m.r], [Xm.r])
                            pXp = self.nxt(ps)
                            for hh in range(8):
                                for d in range(2):
                                    self.mm(pXp[DS[d], hh * 64:(hh + 1) * 64], P0[DS[d], hh, :], Um[DS[d], hh, :], True, True,
                                            [P0.r, Um.r], [pXp.r])
                            self.tt("dve", Xpm[:, :, :], v3(pXp), mk(li), ALU.mult, [pXp.r, l1m.r], [Xpm.r])
                            yield
                            if not last:
                                pY = self.nxt(ps)
                                for hh in range(8):
                                    for d in range(2):
                                        self.mm(pY[DS[d], hh * 64:(hh + 1) * 64], Um[DS[d], hh, :], Xm[DS[d], hh, :], True, True,
                                                [Um.r, Xm.r], [pY.r])
                            pYp = self.nxt(ps)
                            for hh in range(8):
                                for d in range(2):
                                    self.mm(pYp[DS[d], hh * 64:(hh + 1) * 64], Tm[DS[d], hh, :], Xpm[DS[d], hh, :], True, True,
                                            [Tm.r, Xpm.r], [pYp.r])
                            if not last:
                                self.tt("dve", Tm[:, :, :], Tm[:, :, :], v3(pY), ALU.subtract, [Tm.r, pY.r], [Tm.r])
                            self.tt("dve", Um[:, :, :], Um[:, :, :], v3(pYp), ALU.subtract, [Um.r, pYp.r], [Um.r])
                            yield
                        self.tt("pool", tbb[:, :, :], Um[:, :, :], bc_h(be2[:, :]), ALU.mult, [Um.r, be2.r], [tbb.r])
                        self.ld(TBS[cg], tbb[:, :, :], tbb.r, reads=[tbb.r], writes=[rTBS[cg]])
                    for grp_ in ((0, 1, 2), (3, 4, 5), (6, 7)):
                        gens = [chunk_gen(ci) for ci in grp_]
                        while gens:
                            for g_ in list(gens):
                                try:
                                    next(g_)
                                except StopIteration:
                                    gens.remove(g_)
                barrier()

            if stage >= 23:
                o = 0
                Sf, o = aview(o, [128, 16, 128], F32, "Sf")
                Sb_, o = aview(o, [128, 16, 128], BF16, "Sb")
                NR = 3
                kqs, qgs, ktms, vtms, tbts, atts, osts = [], [], [], [], [], [], []
                for i in range(NR):
                    t_, o = aview(o, [128, 2, 8, 64], BF16, "kq%d" % i)
                    kqs.append(t_)
                    t_, o = aview(o, [128, 2, 8, 64], BF16, "qg%d" % i)
                    qgs.append(t_)
                    t_, o = aview(o, [128, 8, 128], BF16, "ktm%d" % i)
                    ktms.append(t_)
                    t_, o = aview(o, [128, 8, 128], BF16, "vtm%d" % i)
                    vtms.append(t_)
                    t_, o = aview(o, [128, 8, 64], BF16, "tbt%d" % i)
                    tbts.append(t_)
                    t_, o = aview(o, [128, 8, 64], BF16, "att%d" % i)
                    atts.append(t_)
                    t_, o = aview(o, [128, 2, 8, 64], BF16, "ost%d" % i)
                    osts.append(t_)
                rps, vns, vsns, osfs = [], [], [], []
                for i in range(2):
                    t_, o = aview(o, [128, 8, 128], BF16, "rp%d" % i)
                    rps.append(t_)
                    t_, o = aview(o, [128, 8, 128], BF16, "vn%d" % i)
                    vns.append(t_)
                    t_, o = aview(o, [128, 8, 128], BF16, "vsn%d" % i)
                    vsns.append(t_)
                    t_, o = aview(o, [128, 8, 64], F32, "osf%d" % i)
                    osfs.append(t_)
                KNv = KN.rearrange("h p t -> p h t")
                SfR = [Res("SfR%d" % u) for u in range(16)]
                SbR = [Res("SbR%d" % u) for u in range(16)]
                rpR = {id(t_): [Res("rpR") for _ in range(16)] for t_ in rps}
                vnR = {id(t_): [Res("vnR") for _ in range(2)] for t_ in vns}
                vsnR = {id(t_): [Res("vsnR") for _ in range(4)] for t_ in vsns}
                seqs = [(0, 4), (4, 4), (8, 4), (12, 4), (16, 32)]
                for si, (cb, N) in enumerate(seqs):
                    if si < 4:
                        self.memset("dve", Sf[:, :, :], 0.0, SfR)
                    else:
                        self.ld(Sf[:, :, :], st_d.rearrange("u p v -> p u v"), Sf.r, writes=SfR)
                    self.cp("act", Sb_[:, :, :], Sf[:, :, :], SfR, SbR)
                    for s_ in range(N):
                        cc = (cb + s_, cb + N - 1 - s_)
                        kq = self.nxt(kqs)
                        qg = self.nxt(qgs)
                        ktm = self.nxt(ktms)
                        vtm = self.nxt(vtms)
                        tbt = self.nxt(tbts)
                        att = self.nxt(atts)
                        ost = self.nxt(osts)
                        rp = self.nxt(rps)
                        vn = self.nxt(vns)
                        vsn = self.nxt(vsns)
                        for d in range(2):
                            c_ = cc[d]
                            tk = slice(c_ * 64, (c_ + 1) * 64)
                            ds_ = slice(d * 64, (d + 1) * 64)
                            self.ld(kq[:, d, :, :], KNv[:, :, tk], kq.r, reads=rKN, writes=[kq.r])
                            self.ld(qg[:, d, :, :], QG[d, c_], qg.r, reads=[rQG[d][c_]], writes=[qg.r])
                            self.ld(ktm[ds_, :, :], KTM[tk, :, :], ktm.r, reads=[rKTM], writes=[ktm.r])
                            self.ld(vtm[ds_, :, :], VTM[tk, :, :], vtm.r, reads=[rVTM], writes=[vtm.r])
                            self.ld(tbt[ds_, :, :], TBS[c_, ds_, :, :], tbt.r, reads=[rTBS[c_]], writes=[tbt.r])
                            self.ld(att[ds_, :, :], ATT[c_, ds_, :, :], att.r, reads=[rATT[c_]], writes=[att.r])
                        DS = (slice(0, 64), slice(64, 128))
                        for g4 in range(2):
                            for d in range(2):
                                for h4 in range(4):
                                    hh = g4 * 4 + h4
                                    self.mm(psA[g4][DS[d], h4 * 128:(h4 + 1) * 128], kq[:, d, hh, :], Sb_[:, d * 8 + hh, :], True, True,
                                            [kq.r, SbR[d * 8 + hh]], [psA[g4].r])
                        for g4 in range(2):
                            for d in range(2):
                                for h4 in range(4):
                                    hh = g4 * 4 + h4
                                    self.stt(rp[DS[d], hh, :], psA[g4][DS[d], h4 * 128:(h4 + 1) * 128], negeg[DS[d], cc[d], hh:hh + 1],
                                             vtm[DS[d], hh, :], ALU.mult, ALU.add, [psA[g4].r, negeg.r, vtm.r], [rpR[id(rp)][d * 8 + hh]])
                        for g4 in range(2):
                            for d in range(2):
                                for h4 in range(4):
                                    hh = g4 * 4 + h4
                                    self.mm(psA[2 + g4][DS[d], h4 * 128:(h4 + 1) * 128], tbt[DS[d], hh, :], rp[DS[d], hh, :], True, True,
                                            [tbt.r, rpR[id(rp)][d * 8 + hh]], [psA[2 + g4].r])
                        for g4 in range(2):
                            hs4 = slice(g4 * 4, g4 * 4 + 4)
                            pv3 = psA[2 + g4][:, :].rearrange("p (h v) -> p h v", v=128)
                            self.cp("act", vn[:, hs4, :], pv3, [psA[2 + g4].r], [vnR[id(vn)][g4]])
                            for d in range(2):
                                self.tt("dve", vsn[DS[d], hs4, :], pv3[DS[d]],
                                        ksc[DS[d], cc[d], hs4].unsqueeze(2).to_broadcast([64, 4, 128]), ALU.mult,
                                        [psA[2 + g4].r, ksc.r], [vsnR[id(vsn)][g4 * 2 + d]])
                        for d in range(2):
                            poS, poA = (psB[0], psB[1]) if d == 0 else (psC[0], psC[1])
                            for hh in range(8):
                                self.mm(poS[:, hh * 64:(hh + 1) * 64], Sb_[:, d * 8 + hh, :], qg[:, d, hh, :], True, True, [SbR[d * 8 + hh], qg.r], [poS.r])
                            for hh in range(8):
                                self.mm(poA[:, hh * 64:(hh + 1) * 64], vn[DS[d], hh, :], att[DS[d], hh, :], True, True, [vnR[id(vn)][hh // 4], att.r], [poA.r])
                        for u in range(16):
                            d, hh = u // 8, u % 8
                            bk = psA[u // 4]
                            self.mm(bk[:, (u % 4) * 128:(u % 4 + 1) * 128], ktm[DS[d], hh, :], vsn[DS[d], hh, :], True, True,
                                    [ktm.r, vsnR[id(vsn)][(hh // 4) * 2 + d]], [bk.r])
                        for d in range(2):
                            poS, poA = (psB[0], psB[1]) if d == 0 else (psC[0], psC[1])
                            osf = self.nxt(osfs)
                            self.cp("act", osf[:, :, :], poS[:, :].rearrange("p (h i) -> p h i", i=64), [poS.r], [osf.r])
                            self.tt("dve", ost[:, d, :, :], poA[:, :].rearrange("p (h i) -> p h i", i=64), osf[:, :, :], ALU.add,
                                    [poA.r, osf.r], [ost.r])
                            tk = slice(cc[d] * 64, (cc[d] + 1) * 64)
                            self.ld(OTF[d].rearrange("h p t -> p h t")[:, :, tk], ost[:, d, :, :], ost.r, reads=[ost.r],
                                    writes=[rOTF[d][cc[d]]])
                        for u in range(16):
                            d = u // 8
                            bk = psA[u // 4]
                            self.stt(Sf[:, u, :], Sf[:, u, :], egl[:, cc[d], u:u + 1], bk[:, (u % 4) * 128:(u % 4 + 1) * 128], ALU.mult, ALU.add,
                                     [SfR[u], egl.r, bk.r], [SfR[u]])
                        for k4 in range(4):
                            self.cp("act", Sb_[:, k4 * 4:(k4 + 1) * 4, :], Sf[:, k4 * 4:(k4 + 1) * 4, :], SfR[k4 * 4:(k4 + 1) * 4], SbR[k4 * 4:(k4 + 1) * 4])
                    if si < 4:
                        self.ld(ns_d[si].rearrange("u p v -> p u v"), Sf[:, :, :], Sf.r, reads=SfR, writes=[Res("nso")])
                barrier()

            if stage >= 24:
                o = 0
                hid_end = 32768
                o = hid_end
                of_, o = aview(o, [128, 8, 512], BF16, "of")
                ob_, o = aview(o, [128, 8, 512], BF16, "ob")
                zs_, o = aview(o, [128, 8, 512], BF16, "zs")
                ot2, o = aview(o, [128, 8, 512], BF16, "ot2")
                for b in range(NB):
                    sl = slice(b * 512, (b + 1) * 512)
                    crange = range(b * 8, b * 8 + 8)
                    self.ld(of_[:, :, :], OTF[0].rearrange("h p t -> p h t")[:, :, sl], of_.r, reads=[rOTF[0][c_] for c_ in crange], writes=[of_.r])
                    self.ld(ob_[:, :, :], OTF[1].rearrange("h p t -> p h t")[:, :, sl], ob_.r, reads=[rOTF[1][c_] for c_ in crange], writes=[ob_.r])
                    self.ld(zs_[:, :, :], ZT.rearrange("h p t -> p h t")[:, :, sl], zs_.r, reads=[rZT[hh][b] for hh in range(8)], writes=[zs_.r])
                    for hh in range(8):
                        oh = self.nxt(ftmp)
                        self.tt("dve", oh[:, :], of_[:, hh, :], ob_[:, hh, :], ALU.add, [of_.r, ob_.r], [oh.r])
                        sq = self.nxt(btmp)
                        self.act(sq[:, :], oh[:, :], AF.Square, [oh.r], [sq.r])
                        pn = self.nxt(psB)
                        self.mm(pn[:, :], ones_bf[:, :], sq[:, :], True, True, [ones_bf.r, sq.r], [pn.r])
                        rt = self.nxt(rtmp)
                        self.act(rt[:, :], pn[:, :], AF.Ln, [pn.r, epsc.r], [rt.r], scale=1.0 / 128.0, bias=epsc[:, 0:1])
                        rr = self.nxt(rtmp)
                        self.act(rr[:, :], rt[:, :], AF.Exp, [rt.r], [rr.r], scale=-0.5)
                        yh = self.nxt(ftmp)
                        self.tt("dve", yh[:, :], oh[:, :], rr[:, :], ALU.mult, [oh.r, rr.r], [yh.r])
                        self.stt(ot2[:, hh, :], yh[:, :], onorm[:, 0:1], zs_[:, hh, :], ALU.mult, ALU.mult,
                                 [yh.r, onorm.r, zs_.r], [ot2.r])
                    x = load_x(X1, rX1, b)
                    out_proj_resid(1, x, b, None, None, ot=ot2)
                    resid_mlp(1, x, b)
                    store_x(x, yT_d, rY, b)
                barrier()

        self.final_wait()
        P.emit(self.st)
        return nc

    def final_wait(self):
        P = self.P
        toks = []
        for eng in ENGS:
            for (fn, waits, inc) in P.ops[eng]:
                pass
        final = {}
        for eng in ENGS:
            for (fn, waits, inc) in P.ops[eng]:
                if inc is not None:
                    final[inc[0]] = final.get(inc[0], 0) + inc[1]
        waits = [(k, v) for k, v in final.items() if k != "sp"]
        P.ops["sp"].append((None, waits, None))


_CACHE = {}


def _host_consts():
    GRID_W, HD = 64, 64
    t = np.arange(2048)
    pos = np.stack([t // GRID_W, t % GRID_W], -1).astype(np.float32)
    axis_dim = HD // 2
    inv_freq = (1.0 / (10000.0 ** (np.arange(0, axis_dim, 2, dtype=np.float32) / axis_dim))).astype(np.float32)
    ang = pos[:, :, None] * inv_freq
    cos = np.cos(ang).astype(np.float32)
    sin = np.sin(ang).astype(np.float32)
    cosT = np.zeros((128, 2048), np.float32)
    sinT = np.zeros((128, 2048), np.float32)
    perm = np.zeros((128, 128), np.float32)
    for p in range(128):
        d = p % 64
        ax = d // 32
        j = d % 16
        cosT[p] = cos[:, ax, j]
        sinT[p] = sin[:, ax, j]
        if (d % 32) < 16:
            perm[p + 16, p] = -1.0
        else:
            perm[p - 16, p] = 1.0
    ident = np.eye(128, dtype=np.float32)
    return cosT, sinT, perm, ident


def _l1_consts():
    c = np.zeros((128, 1024), np.float32)
    p = np.arange(128)[:, None]
    t = p % 64
    d = p // 64
    i = np.arange(64)[None, :]
    c[:, 0:64] = (t <= i)
    c[:, 64:128] = (t >= i)
    c[:, 128:192] = (t == i)
    incl = np.where(d == 0, i >= t, i <= t)
    c[:, 192:256] = np.where(incl, 0.0, NEG)
    c[:, 256:320] = np.where(d == 0, i > t, i < t)
    strT = np.where(d == 0, i < t, i > t)
    c[:, 320:384] = np.where(strT, 0.0, NEG)
    q = np.arange(128)[None, :]
    c[:, 384:512] = (p // 64 == q // 64)
    c[:, 512:640] = (p < 64) * np.ones((1, 128))
    c[:, 640:768] = (p >= 64) * np.ones((1, 128))
    c[:, 768:896] = 1.0
    c[:, 896:1024] = (p == q)
    return c


def _l1_masks():
    m = np.zeros((128, 6, 64), np.float32)
    i = (np.arange(128) % 64)[:, None]
    j = np.arange(64)[None, :]
    for li, sz in enumerate((1, 2, 4, 8, 16, 32)):
        m[:, li, :] = ((i // (2 * sz)) == (j // (2 * sz))) & ((i // sz) != (j // sz))
    return m.reshape(128, 384)


def _bias_expand(rel_bias):
    H = 8
    e = np.arange(2)[:, None, None, None, None]
    kc = np.arange(64)[None, :, None, None, None]
    cl = np.arange(8)[None, None, :, None, None]
    i = np.arange(4)[None, None, None, :, None]
    qc = np.arange(64)[None, None, None, None, :]
    dr = -cl + 2 * i + e
    cs = np.clip(qc - 8, 0, 48)
    valid = (kc >= cs) & (kc < cs + 16)
    valid = np.broadcast_to(valid, (2, 64, 8, 4, 64))
    ri = np.broadcast_to(dr + 7, (2, 64, 8, 4, 64))
    ci = np.clip(np.broadcast_to(kc - qc + 15, (2, 64, 8, 4, 64)), 0, 30)
    out = np.empty((H, 2, 64, 8, 4, 64), np.float32)
    for h in range(H):
        g = rel_bias[h][ri, ci]
        out[h] = np.where(valid, g, np.float32(NEG))
    return np.ascontiguousarray(out.reshape(H, 128, 2048))


def kernel(x_prompt, x_sample, c, cache_l0_k, cache_l0_v, state_l1, c_ctx,
           l0_mod_w, l0_mod_b, l0_norm1, l0_w_in, l0_q_norm_a, l0_k_norm_a, l0_q_norm_b, l0_k_norm_b,
           l0_rel_bias, l0_w_out, l0_norm2, l0_mlp_w1, l0_mlp_w2,
           l1_mod_w, l1_mod_b, l1_norm1, l1_w_in, l1_conv_w, l1_a_log, l1_dt_bias, l1_out_norm,
           l1_w_out, l1_norm2, l1_mlp_w1, l1_mlp_w2, _stage=99):
    f = lambda a: np.ascontiguousarray(np.asarray(a, dtype=np.float32))
    key = ("nc", _stage)
    if key not in _CACHE:
        _CACHE[key] = Builder(_stage).build()
    nc = _CACHE[key]
    cosT, sinT, perm, ident = _host_consts()
    colT = lambda v, n: f(np.asarray(v).reshape(n, 128).T)
    t2 = lambda v: np.tile(np.asarray(v, np.float32), 2)
    gains = f(np.stack([t2(l0_q_norm_a), t2(l0_k_norm_a), t2(l0_q_norm_b), t2(l0_k_norm_b)], -1))
    gk_row = np.concatenate([np.tile(np.asarray(l0_k_norm_a, np.float32), 2), np.tile(np.asarray(l0_k_norm_b, np.float32), 8)])
    gk_tm = f(np.broadcast_to(gk_row[None, :], (128, 640)))
    bias_exp = _bias_expand(np.asarray(l0_rel_bias, np.float32))
    shared = {
        "l0_mod_w": f(l0_mod_w), "l0_mod_bT": colT(l0_mod_b, 48), "l0_n1T": colT(l0_norm1, 8), "l0_n2T": colT(l0_norm2, 8),
        "l0_w_in": f(l0_w_in), "l0_w_out": f(l0_w_out), "l0_w1": f(l0_mlp_w1), "l0_w2": f(l0_mlp_w2),
        "l1_mod_w": f(l1_mod_w), "l1_mod_bT": colT(l1_mod_b, 48), "l1_n1T": colT(l1_norm1, 8), "l1_n2T": colT(l1_norm2, 8),
        "l1_w_in": f(l1_w_in), "l1_w_out": f(l1_w_out), "l1_w1": f(l1_mlp_w1), "l1_w2": f(l1_mlp_w2),
        "l1c": _l1_consts(), "l1m": _l1_masks(),
        "convw": f(np.asarray(l1_conv_w, np.float32).reshape(3, 24, 128).transpose(2, 1, 0).reshape(128, 72)),
        "alog_b": f(np.broadcast_to(np.asarray(l1_a_log, np.float32).reshape(1, 16), (128, 16))),
        "dtb_b": f(np.broadcast_to(np.asarray(l1_dt_bias, np.float32).reshape(1, 16), (128, 16))),
        "onorm": f(np.asarray(l1_out_norm, np.float32).reshape(128, 1)),
        "gains": gains, "gk_tm": gk_tm, "cosT": cosT, "sinT": sinT, "perm": perm, "bias_exp": bias_exp, "ident": ident,
    }
    xp = np.asarray(x_prompt, np.float32)
    xs = np.asarray(x_sample, np.float32)
    in_maps = []
    for i in range(8):
        xi = np.concatenate([xp[4 * i:4 * i + 4].reshape(NPR, 1024), xs[i]], 0)
        cpair = np.stack([np.asarray(c_ctx, np.float32), np.asarray(c, np.float32)[i]], -1)
        cTi = cpair.reshape(8, 128, 2).transpose(1, 0, 2).reshape(128, 16)
        m = dict(shared)
        m["xT"] = f(xi.T)
        m["cT"] = f(cTi)
        m["ck"] = f(np.asarray(cache_l0_k, np.float32)[i].reshape(256, 640))
        m["cv"] = f(np.asarray(cache_l0_v, np.float32)[i].reshape(256, 640))
        m["state"] = f(np.asarray(state_l1, np.float32)[i].reshape(16, 128, 128))
        in_maps.append(m)
    res = run_bass_kernel_spmd(nc, in_maps, core_ids=list(range(8)))
    yp = np.empty((32, 256, 1024), np.float32)
    ys = np.empty((8, 2048, 1024), np.float32)
    nk = np.empty((32, 256, 10, 64), np.float32)
    nv = np.empty((32, 256, 10, 64), np.float32)
    ns = np.empty((32, 2, 8, 128, 128), np.float32)
    for i in range(8):
        r = res.results[i]
        y = np.asarray(r["yT"]).T
        yp[4 * i:4 * i + 4] = y[:NPR].reshape(4, 256, 1024)
        ys[i] = y[NPR:]
        nk[4 * i:4 * i + 4] = np.asarray(r["nk"]).reshape(4, 256, 10, 64)
        nv[4 * i:4 * i + 4] = np.asarray(r["nv"]).reshape(4, 256, 10, 64)
        ns[4 * i:4 * i + 4] = np.asarray(r["ns"]).reshape(4, 2, 8, 128, 128)
    return (yp, ys, nk, nv, ns)
```

```python
from contextlib import ExitStack
import os
import numpy as np
import concourse.bass as bass
import concourse.mybir as mybir
from concourse.bass_utils import run_bass_kernel_spmd

F32 = mybir.dt.float32
BF16 = mybir.dt.bfloat16
AF = mybir.ActivationFunctionType
ALU = mybir.AluOpType
AX = mybir.AxisListType
ENGS = ("pe", "act", "dve", "pool", "sp")

NTOK = 3072
NPR = 1024
NB = 6
EPS = 1e-6
NEG = -30000.0
STORE_Q = os.environ.get('KSTQ', 'act')
FAST_RECIP = False
SAME_ENGINE_INORDER = bool(int(os.environ.get('KSAME', '1')))


class Res:
    __slots__ = ("name", "last_w", "readers", "dsem", "dcount", "excl", "norecycle")

    def __init__(self, name, excl=False):
        self.name = name
        self.excl = excl
        self.norecycle = False
        self.last_w = None
        self.readers = []
        self.dsem = None
        self.dcount = 0


class Prog:
    def __init__(self, nc):
        self.nc = nc
        self.ops = {e: [] for e in ENGS}
        self.cnt = {e: 0 for e in ENGS}
        self.waited = {e: {} for e in ENGS}
        self.semkeys = list(ENGS)
        self.owners = []
        self.free = []
        self.nobar = set()

    def _dep_tokens(self, reads, writes):
        toks = []
        for r in reads:
            if r.last_w is not None:
                toks.append(r.last_w)
            if r.excl:
                toks.extend(r.readers)
        for w in writes:
            if w.last_w is not None:
                toks.append(w.last_w)
            toks.extend(w.readers)
        return toks

    def _add_waits(self, eng, toks, skip=None):
        need = {}
        for (k, v) in toks:
            if skip is not None and k == skip:
                continue
            if v > need.get(k, 0):
                need[k] = v
        waits = []
        wd = self.waited[eng]
        for k, v in need.items():
            if wd.get(k, 0) >= v:
                continue
            wd[k] = v
            waits.append((k, v))
        return waits

    def op(self, eng, fn, reads=(), writes=(), same_ok=False):
        toks = self._dep_tokens(reads, writes)
        if SAME_ENGINE_INORDER and eng != "pool":
            same_ok = True
        waits = self._add_waits(eng, toks, skip=(eng if same_ok else None))
        self.cnt[eng] += 1
        tok = (eng, self.cnt[eng])
        self.ops[eng].append((fn, waits, (eng, 1)))
        for r in reads:
            r.readers.append(tok)
        for w in writes:
            w.last_w = tok
            w.readers = []
        return tok

    def dma(self, eng, fn, owner, reads=(), writes=()):
        if owner.dsem is None:
            if self.free and not owner.norecycle and eng != "pool":
                owner.dsem, owner.dcount = self.free.pop()
            else:
                owner.dsem = "d%d" % len(self.semkeys)
                owner.dcount = 0
                self.semkeys.append(owner.dsem)
            self.owners.append(owner)
        toks = self._dep_tokens(reads, writes)
        waits = self._add_waits(eng, toks)
        owner.dcount += 16
        tok = (owner.dsem, owner.dcount)
        self.ops[eng].append((fn, waits, (owner.dsem, 16)))
        for r in reads:
            r.readers.append(tok)
        for w in writes:
            w.last_w = tok
            w.readers = []
        return tok

    def barrier(self):
        final = {}
        for eng in ENGS:
            for (fn, waits, inc) in self.ops[eng]:
                if inc is not None:
                    final[inc[0]] = final.get(inc[0], 0) + inc[1]
        items = [(k, v) for k, v in final.items() if k not in self.nobar]
        for eng in ENGS:
            waits = self._add_waits(eng, items)
            self.ops[eng].append((None, waits, None))
        keep = []
        for o in self.owners:
            if o.dsem in self.nobar or o.norecycle:
                keep.append(o)
            else:
                self.free.append((o.dsem, o.dcount))
                o.dsem = None
        self.owners = keep

    def finish_wait(self, eng, resources):
        toks = [r.last_w for r in resources if r.last_w is not None]
        waits = self._add_waits(eng, toks)
        self.ops[eng].append((None, waits, None))

    def emit(self, stack):
        nc = self.nc
        sems = {}
        for k in self.semkeys:
            sems[k] = stack.enter_context(nc.semaphore(k))
        block = stack.enter_context(nc.Block())
        handles = {"pe": block.tensor, "act": block.scalar, "dve": block.vector,
                   "pool": block.gpsimd, "sp": block.sync}
        for e in ENGS:
            ops = self.ops[e]

            def body(eh, ops=ops):
                for fn, waits, inc in ops:
                    for (k, v) in waits:
                        eh.wait_ge(sems[k], v)
                    if fn is not None:
                        ins = fn(eh)
                        ins.then_inc(sems[inc[0]], inc[1])
            handles[e](body)


class T:
    def __init__(self, t, name):
        self.t = t
        self.r = Res(name)

    def __getitem__(self, k):
        return self.t[k]


class Builder:
    def __init__(self, stage):
        self.stage = stage
        self.nc = bass.Bass("TRN2", target_bir_lowering=False)
        self.P = Prog(self.nc)
        self.st = ExitStack()
        self.rot = {}
        self.swown = {}

    @staticmethod
    def interleave(gens, width):
        active = []
        it = iter(gens)
        more = True
        while True:
            while more and len(active) < width:
                try:
                    active.append(next(it))
                except StopIteration:
                    more = False
            if not active:
                break
            for g_ in list(active):
                try:
                    next(g_)
                except StopIteration:
                    active.remove(g_)

    @staticmethod
    def interleave2(main, gens, width):
        active = []
        it = iter(gens)
        more = True
        main_alive = main is not None
        while True:
            while more and len(active) < width:
                try:
                    active.append(next(it))
                except StopIteration:
                    more = False
            if not active and not main_alive:
                break
            if main_alive:
                try:
                    next(main)
                except StopIteration:
                    main_alive = False
            for g_ in list(active):
                try:
                    next(g_)
                except StopIteration:
                    active.remove(g_)

    def sb(self, name, shape, dt):
        return T(self.st.enter_context(self.nc.sbuf_tensor("s_" + name, list(shape), dt)), name)

    def sbn(self, name, n, shape, dt):
        return [self.sb("%s%d" % (name, i), shape, dt) for i in range(n)]

    def nxt(self, lst):
        k = id(lst)
        i = self.rot.get(k, 0)
        self.rot[k] = i + 1
        return lst[i % len(lst)]

    def din(self, name, shape, dt=F32):
        return self.nc.dram_tensor(name, list(shape), dt, kind="ExternalInput").ap()

    def dout(self, name, shape, dt=F32):
        return self.nc.dram_tensor(name, list(shape), dt, kind="ExternalOutput").ap()

    def dscr(self, name, shape, dt=BF16):
        kind = "ExternalOutput" if (os.environ.get("KDEBUG") and name in os.environ["KDEBUG"].split(",")) else "Internal"
        return self.nc.dram_tensor(name, list(shape), dt, kind=kind).ap()

    def mm(self, out, lhsT, rhs, start, stop, reads, writes):
        self.P.op("pe", lambda e: e.matmul(out, lhsT, rhs, start=start, stop=stop),
                  reads=reads, writes=writes, same_ok=True)

    def tr(self, out, in_, ident, reads, writes):
        self.P.op("pe", lambda e: e.transpose(out, in_, ident), reads=reads, writes=writes, same_ok=True)

    def act(self, out, in_, func, reads, writes, scale=None, bias=None):
        kw = {}
        if scale is not None:
            kw["scale"] = scale
        if bias is not None:
            kw["bias"] = bias
        self.P.op("act", lambda e: e.activation(out=out, in_=in_, func=func, **kw), reads=reads, writes=writes)

    def tt(self, eng, out, in0, in1, op, reads, writes):
        self.P.op(eng, lambda e: e.tensor_tensor(out=out, in0=in0, in1=in1, op=op), reads=reads, writes=writes)

    def ts(self, eng, out, in0, s1, s2, op0, op1, reads, writes):
        if op1 is None:
            self.P.op(eng, lambda e: e.tensor_scalar(out=out, in0=in0, scalar1=s1, scalar2=None, op0=op0),
                      reads=reads, writes=writes)
        else:
            self.P.op(eng, lambda e: e.tensor_scalar(out=out, in0=in0, scalar1=s1, scalar2=s2, op0=op0, op1=op1),
                      reads=reads, writes=writes)

    def stt(self, out, in0, scalar, in1, op0, op1, reads, writes):
        self.P.op("dve", lambda e: e.scalar_tensor_tensor(out=out, in0=in0, scalar=scalar, in1=in1, op0=op0, op1=op1),
                  reads=reads, writes=writes)

    def cp(self, eng, out, in_, reads, writes):
        if eng == "act":
            self.P.op("act", lambda e: e.copy(out=out, in_=in_), reads=reads, writes=writes)
        else:
            self.P.op(eng, lambda e: e.tensor_copy(out=out, in_=in_), reads=reads, writes=writes)

    def recip(self, out, in_, reads, writes):
        if FAST_RECIP:
            self.P.op("dve", lambda e: e.reciprocal_approx_fast(out, in_), reads=reads, writes=writes)
        else:
            self.P.op("dve", lambda e: e.reciprocal(out=out, in_=in_), reads=reads, writes=writes)

    def memset(self, eng, ap, val, writes):
        self.P.op(eng, lambda e: e.memset(ap, val), writes=writes)

    def ld(self, out, in_, owner, reads=(), writes=(), q="sp"):
        if q == "sp" and STORE_Q != "sp" and any(r is owner for r in reads):
            q = STORE_Q
        if q == "pool":
            key = id(owner)
            if key not in self.swown:
                self.swown[key] = Res("sw_" + owner.name)
                self.swown[key].norecycle = True
            owner = self.swown[key]
        self.P.dma(q, lambda e: e.dma_start(out=out, in_=in_), owner, reads=reads, writes=writes)

    def build(self):
        nc, P = self.nc, self.P
        stage = self.stage
        xT_d = self.din("xT", [1024, NTOK])
        cT_d = self.din("cT", [128, 16])
        ck_d = self.din("ck", [256, 640])
        cv_d = self.din("cv", [256, 640])
        st_d = self.din("state", [16, 128, 128])
        Lw = []
        for l in range(2):
            d = {}
            d["mod_w"] = self.din("l%d_mod_w" % l, [1024, 6144])
            d["mod_b"] = self.din("l%d_mod_bT" % l, [128, 48])
            d["n1"] = self.din("l%d_n1T" % l, [128, 8])
            d["n2"] = self.din("l%d_n2T" % l, [128, 8])
            d["w_in"] = self.din("l%d_w_in" % l, [1024, 2304 if l == 0 else 4128])
            d["w_out"] = self.din("l%d_w_out" % l, [1024, 1024])
            d["w1"] = self.din("l%d_w1" % l, [1024, 4096])
            d["w2"] = self.din("l%d_w2" % l, [4096, 1024])
            Lw.append(d)
        gains_d = self.din("gains", [128, 4])
        gk_tm_d = self.din("gk_tm", [128, 640])
        cos_d = self.din("cosT", [128, 2048])
        sin_d = self.din("sinT", [128, 2048])
        perm_d = self.din("perm", [128, 128])
        bias_d = self.din("bias_exp", [8, 128, 2048])
        ident_d = self.din("ident", [128, 128])
        l1c_d = self.din("l1c", [128, 1024])
        l1m_d = self.din("l1m", [128, 384])
        convw_d = self.din("convw", [128, 72])
        alog_d = self.din("alog_b", [128, 16])
        dtb_d = self.din("dtb_b", [128, 16])
        onorm_d = self.din("onorm", [128, 1])

        yT_d = self.dout("yT", [1024, NTOK])
        nk_d = self.dout("nk", [NPR, 640])
        nv_d = self.dout("nv", [NPR, 640])
        ns_d = self.dout("ns", [4, 16, 128, 128])

        Wb = []
        for l in range(2):
            d = {}
            d["w_in"] = self.dscr("wb%d_in" % l, [1024, 2304 if l == 0 else 4128])
            d["w_out"] = self.dscr("wb%d_out" % l, [1024, 1024])
            d["w1"] = self.dscr("wb%d_w1" % l, [1024, 4096])
            d["w2"] = self.dscr("wb%d_w2" % l, [4096, 1024])
            Wb.append(d)
        QKT = self.dscr("QKT", [13, 128, NTOK])
        V0 = self.dscr("V0", [NTOK, 640])
        KC = self.dscr("KC", [5, 128, 256])
        OT = self.dscr("OT", [8, 128, NTOK])
        rWb = [{k: Res("rWb%d%s" % (l, k)) for k in Wb[l]} for l in range(2)]
        rQKT = [[Res("rQKT") for _ in range(NB)] for _ in range(13)]
        rV0 = [Res("rV0") for _ in range(NB)]
        rKC = Res("rKC")
        rOT = [[Res("rOT") for _ in range(NB)] for _ in range(8)]

        X1 = self.dscr("X1", [1024, NTOK], F32)
        rX1 = [Res("rX1_%d" % b_) for b_ in range(NB)]
        rXin = [Res("rXin%d" % b_) for b_ in range(NB)]
        rY = [Res("rY%d" % b_) for b_ in range(NB)]

        ones_bf = self.sb("ones_bf", [128, 128], BF16)
        bd_bf = self.sb("bd_bf", [128, 128], BF16)
        perm_bf = self.sb("perm_bf", [128, 128], BF16)
        ident_f = self.sb("ident_f", [128, 128], F32)
        stage_f = self.sb("stage_f", [128, 128], F32)
        gains = self.sb("gains", [128, 4], F32)
        gk_tm = self.sb("gk_tm", [128, 640], F32)
        cT = self.sb("cT", [128, 8, 2], F32)
        scT = self.sb("scT", [128, 8, 2], F32)
        modT = self.sb("modT", [128, 48, 2], F32)
        modb = self.sb("modb", [128, 48], F32)
        n1 = self.sb("n1", [128, 8], F32)
        n2 = self.sb("n2", [128, 8], F32)
        G1 = self.sb("G1", [128, 8, 2], F32)
        G2 = self.sb("G2", [128, 8, 2], F32)
        epsc = self.sb("epsc", [128, 1], F32)

        ps = [T(self.st.enter_context(nc.psum_tensor("ps%d" % i, [128, 512], F32)), "ps%d" % i) for i in range(8)]
        for p_ in ps:
            p_.r.excl = True
        psA = ps[0:4]
        psB = ps[4:6]
        psC = ps[6:8]
        psBC = ps[4:8]
        psB1 = ps[5:8]

        xb = self.sbn("xb", 2, [128, 8, 512], F32)
        wbuf = self.sbn("wbuf", 3, [128, 4096], BF16)
        sqb = self.sb("sqb", [128, 8, 512], BF16)
        hT = self.sbn("hT", 2, [128, 8, 512], BF16)
        Rb = self.sb("Rb", [128, 512], F32)
        rtmp = self.sbn("rtmp", 5, [128, 512], F32)
        ftmp = self.sbn("ftmp", 4, [128, 512], F32)
        btmp = self.sbn("btmp", 4, [128, 512], BF16)
        ostg = self.sbn("ostg", 3, [128, 512], BF16)
        arena = self.st.enter_context(nc.sbuf_tensor("s_arena", [128, 20480], F32))

        def aview(off_b, shape, dt, name):
            n = 1
            for d_ in shape[1:]:
                n *= d_
            esz = 4 if dt == F32 else 2
            nb = n * esz
            assert off_b % 4 == 0 and off_b + nb <= 81920, (name, off_b, nb)
            ap = arena[:, off_b // 4:(off_b + nb + 3) // 4]
            if dt != F32:
                ap = ap.bitcast(dt)
            if len(shape) == 3:
                ap = ap.rearrange("p (a b) -> p a b", b=shape[2])
            elif len(shape) == 4:
                ap = ap.rearrange("p (a b c) -> p a b c", b=shape[2], c=shape[3])
            return T(ap, name), off_b + ((nb + 31) // 32) * 32

        barrier = P.barrier

        self.memset("dve", ones_bf[:, :], 1.0, [ones_bf.r])
        self.memset("dve", bd_bf[:, :], 0.0, [bd_bf.r])
        self.memset("dve", bd_bf[0:64, 0:64], 1.0, [bd_bf.r])
        self.memset("dve", bd_bf[64:128, 64:128], 1.0, [bd_bf.r])
        self.memset("dve", epsc[:, :], EPS, [epsc.r])
        self.ld(stage_f[:, :], perm_d, stage_f.r, writes=[stage_f.r])
        self.cp("dve", perm_bf[:, :], stage_f[:, :], [stage_f.r], [perm_bf.r])
        self.ld(ident_f[:, :], ident_d, ident_f.r, writes=[ident_f.r])
        self.ld(gains[:, :], gains_d, gains.r, writes=[gains.r])
        self.ld(gk_tm[:, :], gk_tm_d, gk_tm.r, writes=[gk_tm.r])
        self.ld(cT[:, :, :], cT_d.rearrange("p (c s) -> p c s", s=2), cT.r, writes=[cT.r])
        self.act(scT[:, :, :], cT[:, :, :], AF.Silu, [cT.r], [scT.r])

        for l in range(2):
            for k in ("w_in", "w_out", "w1", "w2"):
                cr = Res("cast%d%s" % (l, k))
                src, dst = Lw[l][k], Wb[l][k]
                nrow = src.shape[0]
                for r0 in range(0, nrow, 256):
                    P.dma("pool", (lambda s_, d_: (lambda e: e.dma_start(out=d_, in_=s_)))(src[r0:r0 + 256, :], dst[r0:r0 + 256, :]),
                          cr, writes=[rWb[l][k]])
                    P.nobar.add(cr.dsem)

        def xdram(Xd):
            return Xd.rearrange("(c p) t -> p c t", p=128)

        def load_x(Xd, rXd, b):
            x = self.nxt(xb)
            self.ld(x[:, :, :], xdram(Xd)[:, :, b * 512:(b + 1) * 512], x.r, reads=[rXd[b]], writes=[x.r])
            return x

        def store_x(x, Xd, rXd, b):
            self.ld(xdram(Xd)[:, :, b * 512:(b + 1) * 512], x[:, :, :], x.r, reads=[x.r], writes=[rXd[b]])

        def modulation(l):
            mw0, o = aview(0, [128, 8, 256], F32, "mw0")
            mw1, o = aview(o, [128, 8, 256], F32, "mw1")
            mwbuf = [mw0, mw1]
            self.ld(modb[:, :], Lw[l]["mod_b"], modb.r, writes=[modb.r])
            self.ld(n1[:, :], Lw[l]["n1"], n1.r, writes=[n1.r])
            self.ld(n2[:, :], Lw[l]["n2"], n2.r, writes=[n2.r])
            pm = psC[0]
            mw = Lw[l]["mod_w"].rearrange("(c p) n -> p c n", p=128)
            for g in range(24):
                wt = self.nxt(mwbuf)
                self.ld(wt[:, :, :], mw[:, :, g * 256:(g + 1) * 256], wt.r, writes=[wt.r])
                for j in range(2):
                    ch = g * 2 + j
                    for kc in range(8):
                        self.mm(pm[:, ch * 2:ch * 2 + 2], wt[:, kc, j * 128:(j + 1) * 128], scT[:, kc, :],
                                kc == 0, kc == 7, [wt.r, scT.r], [pm.r])
            pmv = pm[:, 0:96].rearrange("p (c s) -> p c s", s=2)
            for s_ in range(2):
                self.tt("dve", modT[:, :, s_], pmv[:, :, s_], modb[:, :], ALU.add, [pm.r, modb.r], [modT.r])
            for s_ in range(2):
                self.stt(G1[:, :, s_], modT[:, 8:16, s_], 1.0, n1[:, :], ALU.add, ALU.mult, [modT.r, n1.r], [G1.r])
                self.stt(G2[:, :, s_], modT[:, 32:40, s_], 1.0, n2[:, :], ALU.add, ALU.mult, [modT.r, n2.r], [G2.r])
            barrier()

        def norm_mod(x, b, G, shift_base, pa=None):
            s_ = 0 if b < 2 else 1
            self.act(sqb[:, :, :], x[:, :, :], AF.Square, [x.r], [sqb.r])
            if pa is None:
                pa = self.nxt(psB)
            for kc in range(8):
                self.mm(pa[:, :], ones_bf[:, :], sqb[:, kc, :], kc == 0, kc == 7, [ones_bf.r, sqb.r], [pa.r])
            rt = self.nxt(rtmp)
            self.act(rt[:, :], pa[:, :], AF.Ln, [pa.r, epsc.r], [rt.r], scale=1.0 / 1024.0, bias=epsc[:, 0:1])
            self.act(Rb[:, :], rt[:, :], AF.Exp, [rt.r], [Rb.r], scale=-0.5)
            h = self.nxt(hT)
            for kc in range(8):
                ft = self.nxt(ftmp)
                self.tt("dve", ft[:, :], x[:, kc, :], Rb[:, :], ALU.mult, [x.r, Rb.r], [ft.r])
                self.act(h[:, kc, :], ft[:, :], AF.Identity, [ft.r, G.r, modT.r], [h.r],
                         scale=G[:, kc, s_:s_ + 1], bias=modT[:, shift_base + kc, s_:s_ + 1])
            return h

        def load_w(Wd, rW, KC_, n0, ncols):
            wt = self.nxt(wbuf)
            v = wt.t[:, 0:KC_ * ncols].rearrange("p (c n) -> p c n", n=ncols)
            self.ld(v, Wd.rearrange("(c p) n -> p c n", p=128)[:, :, n0:n0 + ncols], wt.r, reads=[rW], writes=[wt.r])
            return wt, v

        def resid_mlp(l, x, b):
            s_ = 0 if b < 2 else 1
            hid, _ = aview(0, [128, 32, 512], BF16, "hid")
            h2 = norm_mod(x, b, G2, 24)
            for g in range(8):
                wt, wv = load_w(Wb[l]["w1"], rWb[l]["w1"], 8, g * 512, 512)
                for j in range(4):
                    pm = self.nxt(psA)
                    for kc in range(8):
                        self.mm(pm[:, :], wv[:, kc, j * 128:(j + 1) * 128], h2[:, kc, :], kc == 0, kc == 7, [wt.r, h2.r], [pm.r])
                    rl = self.nxt(ftmp)
                    self.act(rl[:, :], pm[:, :], AF.Relu, [pm.r], [rl.r])
                    self.tt("pool", hid[:, g * 4 + j, :], rl[:, :], rl[:, :], ALU.mult, [rl.r], [hid.r])
            for n in range(8):
                wt = self.nxt(wbuf)
                wv = wt.t[:, 0:4096].rearrange("p (c n) -> p c n", n=128)
                self.ld(wv, Wb[l]["w2"].rearrange("(c p) n -> p c n", p=128)[:, :, n * 128:(n + 1) * 128], wt.r,
                        reads=[rWb[l]["w2"]], writes=[wt.r])
                pm = self.nxt(psA)
                for hc in range(32):
                    self.mm(pm[:, :], wv[:, hc, :], hid[:, hc, :], hc == 0, hc == 31, [wt.r, hid.r], [pm.r])
                self.stt(x[:, n, :], pm[:, :], modT[:, 40 + n, s_:s_ + 1], x[:, n, :], ALU.mult, ALU.add,
                         [pm.r, modT.r, x.r], [x.r])

        def out_proj_resid(l, x, b, OTd, rOTd, ot=None):
            s_ = 0 if b < 2 else 1
            if ot is None:
                ot = self.nxt(hT)
                self.ld(ot[:, :, :], OTd.rearrange("c p t -> p c t")[:, :, b * 512:(b + 1) * 512], ot.r,
                        reads=[rOTd[c_][b] for c_ in range(8)], writes=[ot.r])
            for g in range(2):
                wt, wv = load_w(Wb[l]["w_out"], rWb[l]["w_out"], 8, g * 512, 512)
                for j in range(4):
                    n = g * 4 + j
                    pm = self.nxt(psA)
                    for kc in range(8):
                        self.mm(pm[:, :], wv[:, kc, j * 128:(j + 1) * 128], ot[:, kc, :], kc == 0, kc == 7, [wt.r, ot.r], [pm.r])
                    self.stt(x[:, n, :], pm[:, :], modT[:, 16 + n, s_:s_ + 1], x[:, n, :], ALU.mult, ALU.add,
                             [pm.r, modT.r, x.r], [x.r])

        if stage >= 2:
            modulation(0)

        o = 0
        cosb, o = aview(o, [128, 512], F32, "cosb")
        sinb, o = aview(o, [128, 512], F32, "sinb")
        vst0, o = aview(o, [128, 640], BF16, "vst0")
        vst1, o = aview(o, [128, 640], BF16, "vst1")
        vst = [vst0, vst1]
        nv0, o = aview(o, [128, 640], F32, "nv0")
        nv1, o = aview(o, [128, 640], F32, "nv1")
        nvst = [nv0, nv1]
        nk0, o = aview(o, [128, 640], F32, "nk0")
        nk1, o = aview(o, [128, 640], F32, "nk1")
        nkst = [nk0, nk1]
        ksq, o = aview(o, [128, 640], F32, "ksq")
        kss, o = aview(o, [128, 10], F32, "kss")
        krs, o = aview(o, [128, 10], F32, "krs")

        def l0_fm_chunk(b, h, wv, j, qi, gi, rope):
            sl = slice(b * 512, (b + 1) * 512)
            pm = self.nxt(psA)
            for kc in range(8):
                self.mm(pm[:, :], wv[0][:, kc, j * 128:(j + 1) * 128], h[:, kc, :], kc == 0, kc == 7,
                        [wv[1].r, h.r], [pm.r])
            yield
            sq = self.nxt(btmp)
            self.act(sq[:, :], pm[:, :], AF.Square, [pm.r], [sq.r])
            pn = self.nxt(psB)
            self.mm(pn[:, :], bd_bf[:, :], sq[:, :], True, True, [bd_bf.r, sq.r], [pn.r])
            rt = self.nxt(rtmp)
            self.act(rt[:, :], pn[:, :], AF.Ln, [pn.r, epsc.r], [rt.r], scale=1.0 / 64.0, bias=epsc[:, 0:1])
            rr = self.nxt(rtmp)
            self.act(rr[:, :], rt[:, :], AF.Exp, [rt.r], [rr.r], scale=-0.5)
            qn = self.nxt(ftmp)
            self.tt("dve", qn[:, :], pm[:, :], rr[:, :], ALU.mult, [pm.r, rr.r], [qn.r])
            ob = self.nxt(ostg)
            if not rope:
                self.act(ob[:, :], qn[:, :], AF.Identity, [qn.r, gains.r], [ob.r], scale=gains[:, gi:gi + 1])
            else:
                qg = self.nxt(ftmp)
                self.ts("dve", qg[:, :], qn[:, :], gains[:, gi:gi + 1], None, ALU.mult, None, [qn.r, gains.r], [qg.r])
                qb_ = self.nxt(btmp)
                self.cp("act", qb_[:, :], qg[:, :], [qg.r], [qb_.r])
                pr = self.nxt(psB)
                self.mm(pr[:, :], perm_bf[:, :], qb_[:, :], True, True, [perm_bf.r, qb_.r], [pr.r])
                a = self.nxt(ftmp)
                self.tt("dve", a[:, :], qg[:, :], cosb[:, :], ALU.mult, [qg.r, cosb.r], [a.r])
                b2 = self.nxt(rtmp)
                self.tt("dve", b2[:, :], pr[:, :], sinb[:, :], ALU.mult, [pr.r, sinb.r], [b2.r])
                self.tt("pool", ob[:, :], a[:, :], b2[:, :], ALU.add, [a.r, b2.r], [ob.r])
            self.ld(QKT[qi, :, sl], ob[:, :], ob.r, reads=[ob.r], writes=[rQKT[qi][b]])

        def k_norm_tm(pk, ncols, c0, tok0):
            nh = ncols // 64
            self.act(ksq[:, 0:ncols], pk[:, 0:ncols], AF.Square, [pk.r], [ksq.r])
            self.P.op("dve", lambda e: e.tensor_reduce(out=kss[:, 0:nh], in_=ksq[:, 0:ncols].rearrange("p (h d) -> p h d", d=64),
                                                        axis=AX.X, op=ALU.add), reads=[ksq.r], writes=[kss.r])
            self.act(krs[:, 0:nh], kss[:, 0:nh], AF.Sqrt, [kss.r, epsc.r], [krs.r], scale=1.0 / 64.0, bias=epsc[:, 0:1])
            self.recip(kss[:, 0:nh], krs[:, 0:nh], [krs.r], [kss.r])
            nk = self.nxt(nkst)
            for hh in range(nh):
                self.stt(nk[:, c0 + hh * 64:c0 + (hh + 1) * 64], pk[:, hh * 64:(hh + 1) * 64], kss[:, hh:hh + 1],
                         gk_tm[:, c0 + hh * 64:c0 + (hh + 1) * 64], ALU.mult, ALU.mult, [pk.r, kss.r, gk_tm.r], [nk.r])
            self.ld(nk_d[tok0:tok0 + 128, c0:c0 + ncols], nk[:, c0:c0 + ncols], nk.r, reads=[nk.r], writes=[Res("nko")])

        KNB = int(os.environ.get('KNB', NB))
        for b in range(KNB if stage >= 3 else 0):
            samp = b >= 2
            x = load_x(xT_d, rXin, b)
            h = norm_mod(x, b, G1, 0)
            if samp:
                t0 = (b - 2) * 512
                self.ld(cosb[:, :], cos_d[:, t0:t0 + 512], cosb.r, writes=[cosb.r])
                self.ld(sinb[:, :], sin_d[:, t0:t0 + 512], sinb.r, writes=[sinb.r])
            wt, wv = load_w(Wb[0]["w_in"], rWb[0]["w_in"], 8, 0, 512)
            self.interleave((l0_fm_chunk(b, h, (wv, wt), j, j, 0, samp) for j in range(4)), int(os.environ.get('KL0W', 3)))
            wt1, wv1 = load_w(Wb[0]["w_in"], rWb[0]["w_in"], 8, 512, 512)
            self.interleave(iter([l0_fm_chunk(b, h, (wv1, wt1), 0, 4, 1, samp), l0_fm_chunk(b, h, (wv1, wt1), 2, 5, 2, False),
                                  l0_fm_chunk(b, h, (wv1, wt1), 3, 6, 2, False)]), int(os.environ.get('KL0W', 3)))
            for t in range(4):
                tsl = slice(t * 128, (t + 1) * 128)
                tok0 = b * 512 + t * 128
                pv = self.nxt(psA)
                for kc in range(8):
                    self.mm(pv[:, 0:128], h[:, kc, tsl], wv1[:, kc, 128:256], kc == 0, kc == 7, [h.r, wt1.r], [pv.r])
                vs = self.nxt(vst)
                if samp:
                    self.cp("act", vs[:, 0:128], pv[:, 0:128], [pv.r], [vs.r])
                else:
                    nv = self.nxt(nvst)
                    self.cp("dve", nv[:, 0:128], pv[:, 0:128], [pv.r], [nv.r])
                    self.ld(nv_d[tok0:tok0 + 128, 0:128], nv[:, 0:128], nv.r, reads=[nv.r], writes=[Res("nvo")])
                    self.cp("act", vs[:, 0:128], nv[:, 0:128], [nv.r], [vs.r])
                self.ld(V0[tok0:tok0 + 128, 0:128], vs[:, 0:128], vs.r, reads=[vs.r], writes=[rV0[b]])
                if not samp:
                    pk = self.nxt(psA)
                    for kc in range(8):
                        self.mm(pk[:, 0:128], h[:, kc, tsl], wv1[:, kc, 0:128], kc == 0, kc == 7, [h.r, wt1.r], [pk.r])
                    k_norm_tm(pk, 128, 0, tok0)
            wt2, wv2 = load_w(Wb[0]["w_in"], rWb[0]["w_in"], 8, 1024, 512)
            self.interleave(iter([l0_fm_chunk(b, h, (wv2, wt2), 0, 7, 2, False), l0_fm_chunk(b, h, (wv2, wt2), 1, 8, 2, False),
                                  l0_fm_chunk(b, h, (wv2, wt2), 2, 9, 3, False), l0_fm_chunk(b, h, (wv2, wt2), 3, 10, 3, False)]), int(os.environ.get('KL0W', 3)))
            wt3, wv3 = load_w(Wb[0]["w_in"], rWb[0]["w_in"], 8, 1536, 512)
            self.interleave(iter([l0_fm_chunk(b, h, (wv3, wt3), 0, 11, 3, False), l0_fm_chunk(b, h, (wv3, wt3), 1, 12, 3, False)]), int(os.environ.get('KL0W', 3)))
            if not samp:
                for t in range(4):
                    tsl = slice(t * 128, (t + 1) * 128)
                    tok0 = b * 512 + t * 128
                    pk2 = self.nxt(psA)
                    for kc in range(8):
                        self.mm(pk2[:, 0:256], h[:, kc, tsl], wv2[:, kc, 256:512], kc == 0, kc == 7, [h.r, wt2.r], [pk2.r])
                    for kc in range(8):
                        self.mm(pk2[:, 256:512], h[:, kc, tsl], wv3[:, kc, 0:256], kc == 0, kc == 7, [h.r, wt3.r], [pk2.r])
                    k_norm_tm(pk2, 512, 128, tok0)
            wt4, wv4 = load_w(Wb[0]["w_in"], rWb[0]["w_in"], 8, 2048, 256)
            for t in range(4):
                tsl = slice(t * 128, (t + 1) * 128)
                tok0 = b * 512 + t * 128
                pv2 = self.nxt(psA)
                for kc in range(8):
                    self.mm(pv2[:, 0:256], h[:, kc, tsl], wv3[:, kc, 256:512], kc == 0, kc == 7, [h.r, wt3.r], [pv2.r])
                for kc in range(8):
                    self.mm(pv2[:, 256:512], h[:, kc, tsl], wv4[:, kc, 0:256], kc == 0, kc == 7, [h.r, wt4.r], [pv2.r])
                vs = self.nxt(vst)
                if samp:
                    self.cp("act", vs[:, 128:640], pv2[:, :], [pv2.r], [vs.r])
                else:
                    nv = self.nxt(nvst)
                    self.cp("dve", nv[:, 128:640], pv2[:, :], [pv2.r], [nv.r])
                    self.ld(nv_d[tok0:tok0 + 128, 128:640], nv[:, 128:640], nv.r, reads=[nv.r], writes=[Res("nvo")])
                    self.cp("act", vs[:, 128:640], nv[:, 128:640], [nv.r], [vs.r])
                self.ld(V0[tok0:tok0 + 128, 128:640], vs[:, 128:640], vs.r, reads=[vs.r], writes=[rV0[b]])
        barrier()

        if stage >= 4:
            o = 0
            ck_sb, o = aview(o, [128, 2, 640], F32, "ck_sb")
            kct, o = aview(o, [128, 5, 256], BF16, "kct")
            self.ld(ck_sb[:, :, :], ck_d.rearrange("(t p) c -> p t c", p=128), ck_sb.r, writes=[ck_sb.r])
            for ci in range(5):
                pt = self.nxt(psA)
                for t in range(2):
                    self.tr(pt[:, t * 128:(t + 1) * 128], ck_sb[:, t, ci * 128:(ci + 1) * 128], ident_f[:, :],
                            [ck_sb.r, ident_f.r], [pt.r])
                self.cp("dve", kct[:, ci, :], pt[:, 0:256], [pt.r], [kct.r])
            self.ld(KC.rearrange("c p t -> p c t"), kct[:, :, :], kct.r, reads=[kct.r], writes=[rKC])
            barrier()

        def attn_unit(q_ap, k_fn, v_fn, nkc, nq, half, out_ap, rds, out_r, po=None, po_off=0, norm=True):
            psx = self.nxt(psA)
            for kc in range(nkc):
                self.mm(psx[:, kc * nq:(kc + 1) * nq], k_fn(kc), q_ap, True, True, rds, [psx.r])
            pT = self.nxt(btmp)
            self.act(pT[:, 0:nkc * nq], psx[:, 0:nkc * nq], AF.Exp, [psx.r], [pT.r], scale=0.125)
            if po is None:
                po = self.nxt(psB)
            for kc in range(nkc):
                self.mm(po[:, po_off:po_off + nq], v_fn(kc), pT[:, kc * nq:(kc + 1) * nq], kc == 0, kc == nkc - 1,
                        rds + [pT.r], [po.r])
            if norm:
                normalize(po, po_off, nq, half, out_ap, out_r)
            return po

        def normalize(po, po_off, nq, half, out_ap, out_r):
            rc = self.nxt(rtmp)
            if half == 0:
                self.recip(rc[0:64, 0:nq], po[64:128, po_off:po_off + nq], [po.r], [rc.r])
                self.tt("dve", out_ap, po[0:64, po_off:po_off + nq], rc[0:64, 0:nq], ALU.mult, [po.r, rc.r], [out_r])
            else:
                self.recip(rc[64:128, 0:nq], po[0:64, po_off:po_off + nq], [po.r], [rc.r])
                self.tt("dve", out_ap, po[64:128, po_off:po_off + nq], rc[64:128, 0:nq], ALU.mult, [po.r, rc.r], [out_r])

        if stage >= 4:
            o = 0
            qk2 = []
            for i in range(2):
                t_, o = aview(o, [128, 13, 256], BF16, "qk%d" % i)
                qk2.append(t_)
            ka2 = []
            for i in range(2):
                t_, o = aview(o, [128, 2, 256], BF16, "ka2_%d" % i)
                ka2.append(t_)
            vaE, vaO = [], []
            for i in range(2):
                t_, o = aview(o, [128, 2, 10, 128], BF16, "vaE%d" % i)
                vaE.append(t_)
                t_, o = aview(o, [128, 2, 10, 128], BF16, "vaO%d" % i)
                vaO.append(t_)
            ostp = []
            for i in range(2):
                t_, o = aview(o, [128, 8, 256], BF16, "ostp%d" % i)
                ostp.append(t_)
            for i in range(2):
                self.memset("pool", vaE[i][:, :, :, 64:128], 1.0, [vaE[i].r])
                self.memset("pool", vaO[i][:, :, :, 0:64], 1.0, [vaO[i].r])
            QKTv = QKT.rearrange("c p t -> p c t")
            for s_ in range(4):
                tb = s_ * 256
                b = s_ // 2
                qk = qk2[s_ % 2]
                k2 = ka2[s_ % 2]
                vE, vO = vaE[s_ % 2], vaO[s_ % 2]
                ost = ostp[s_ % 2]
                self.ld(qk[:, :, :], QKTv[:, :, tb:tb + 256], qk.r, reads=[rQKT[c_][b] for c_ in range(13)], writes=[qk.r])
                for g in range(2):
                    for hf in range(2):
                        self.ld(k2[hf * 64:(hf + 1) * 64, g, :], QKT[4, g * 64:(g + 1) * 64, tb:tb + 256], k2.r,
                                reads=[rQKT[4][b]], writes=[k2.r])
                for t in range(2):
                    src = V0[tb + t * 128:tb + (t + 1) * 128, :].rearrange("p (h d) -> p h d", d=64)
                    self.ld(vE[:, t, :, 0:64], src, vE.r, reads=[rV0[b]], writes=[vE.r])
                    self.ld(vO[:, t, :, 64:128], src, vO.r, reads=[rV0[b]], writes=[vO.r])
                for hd in range(16):
                    isA = hd < 8
                    h_ = hd if isA else hd - 8
                    hf = h_ % 2
                    psl = slice(hf * 64, (hf + 1) * 64)
                    va = vE if hf == 0 else vO
                    if isA:
                        g = h_ // 4
                        q_ap = qk[psl, h_ // 2, :]
                        k_fn = (lambda kc, g=g, psl=psl: k2[psl, g, kc * 128:(kc + 1) * 128])
                        v_fn = (lambda kc, g=g, va=va: va[:, kc, g, :])
                        rds = [qk.r, k2.r, va.r]
                    else:
                        q_ap = qk[psl, 5 + h_ // 2, :]
                        k_fn = (lambda kc, h_=h_, psl=psl: qk[psl, 9 + h_ // 2, kc * 128:(kc + 1) * 128])
                        v_fn = (lambda kc, h_=h_, va=va: va[:, kc, 2 + h_, :])
                        rds = [qk.r, va.r]
                    och = (h_ // 2) if isA else 4 + h_ // 2
                    attn_unit(q_ap, k_fn, v_fn, 2, 256, hf, ost[psl, och, :], rds, ost.r)
                self.ld(OT.rearrange("c p t -> p c t")[:, :, tb:tb + 256], ost[:, :, :], ost.r, reads=[ost.r],
                        writes=[rOT[c_][b] for c_ in range(8)])
            barrier()

        if stage >= 5:
            o = 0
            ka2s, o = aview(o, [128, 2, 2304], BF16, "ka2s")
            vEA, o = aview(o, [128, 18, 2, 128], BF16, "vEA")
            vOA, o = aview(o, [128, 18, 2, 128], BF16, "vOA")
            qa2 = []
            osa = []
            for i in range(2):
                t_, o = aview(o, [128, 2048], BF16, "qa%d" % i)
                qa2.append(t_)
                t_, o = aview(o, [128, 2048], BF16, "osa%d" % i)
                osa.append(t_)
            self.memset("pool", vEA[:, :, :, 64:128], 1.0, [vEA.r])
            self.memset("pool", vOA[:, :, :, 0:64], 1.0, [vOA.r])
            allq = [rQKT[4][b_] for b_ in range(2, 6)]
            for g in range(2):
                for hf in range(2):
                    self.ld(ka2s[hf * 64:(hf + 1) * 64, g, 0:2048], QKT[4, g * 64:(g + 1) * 64, 1024:3072], ka2s.r,
                            reads=allq, writes=[ka2s.r])
                    self.ld(ka2s[hf * 64:(hf + 1) * 64, g, 2048:2304], KC[0, g * 64:(g + 1) * 64, :], ka2s.r,
                            reads=[rKC], writes=[ka2s.r])
                srcv = V0[1024:3072, g * 64:(g + 1) * 64].rearrange("(t p) d -> p t d", p=128)
                self.ld(vEA[:, 0:16, g, 0:64], srcv, vEA.r, reads=rV0[2:6], writes=[vEA.r])
                self.ld(vOA[:, 0:16, g, 64:128], srcv, vOA.r, reads=rV0[2:6], writes=[vOA.r])
                srcc = cv_d[:, g * 64:(g + 1) * 64].rearrange("(t p) d -> p t d", p=128)
                self.ld(vEA[:, 16:18, g, 0:64], srcc, vEA.r, writes=[vEA.r], q="pool")
                self.ld(vOA[:, 16:18, g, 64:128], srcc, vOA.r, writes=[vOA.r], q="pool")
            for hp in range(4):
                qa = qa2[hp % 2]
                os_ = osa[hp % 2]
                self.ld(qa[:, :], QKT[hp, :, 1024:3072], qa.r, reads=[rQKT[hp][b_] for b_ in range(2, 6)], writes=[qa.r])
                for hf in range(2):
                    h_ = hp * 2 + hf
                    g = h_ // 4
                    psl = slice(hf * 64, (hf + 1) * 64)
                    va = vEA if hf == 0 else vOA
                    for qb_i in range(4):
                        po = self.nxt(psB)

                        def s_mm(kc, g=g, psl=psl, qb_i=qb_i, qa=qa):
                            psx = self.nxt(psA)
                            self.mm(psx[:, :], ka2s[psl, g, kc * 128:(kc + 1) * 128], qa[psl, qb_i * 512:(qb_i + 1) * 512],
                                    True, True, [ka2s.r, qa.r], [psx.r])
                            return psx
                        pend = [s_mm(0), s_mm(1)]
                        for kc in range(18):
                            psx = pend.pop(0)
                            if kc + 2 < 18:
                                pend.append(s_mm(kc + 2))
                            pT = self.nxt(btmp)
                            self.act(pT[:, :], psx[:, :], AF.Exp, [psx.r], [pT.r], scale=0.125)
                            self.mm(po[:, :], va[:, kc, g, :], pT[:, :], kc == 0, kc == 17, [va.r, pT.r], [po.r])
                        normalize(po, 0, 512, hf, os_[psl, qb_i * 512:(qb_i + 1) * 512], os_.r)
                self.ld(OT[hp, :, 1024:3072], os_[:, :], os_.r, reads=[os_.r], writes=[rOT[hp][b_] for b_ in range(2, 6)])
            barrier()

        if stage >= 6:
            o = 0
            qb2, kb2, osb, bia = [], [], [], []
            for i in range(2):
                t_, o = aview(o, [128, 2048], BF16, "qb%d" % i)
                qb2.append(t_)
                t_, o = aview(o, [128, 2304], BF16, "kb%d" % i)
                kb2.append(t_)
                t_, o = aview(o, [128, 2048], BF16, "osb%d" % i)
                osb.append(t_)
            bia_t, o = aview(o, [128, 2048], F32, "bia")
            vsets = []
            for hf in range(2):
                e_, o = aview(o, [128, 16, 128], BF16, "vbE%d" % hf)
                d_, o = aview(o, [128, 15, 128], BF16, "vbOd%d" % hf)
                c_, o = aview(o, [128, 2, 128], BF16, "vbC%d" % hf)
                vsets.append((e_, d_, c_))
                onesl = slice(64, 128) if hf == 0 else slice(0, 64)
                for t_ in (e_, d_, c_):
                    self.memset("pool", t_[:, :, onesl], 1.0, [t_.r])
            for hp in range(4):
                qb_t = qb2[hp % 2]
                kb_t = kb2[hp % 2]
                os_ = osb[hp % 2]
                rq = [rQKT[5 + hp][b_] for b_ in range(2, 6)]
                rk = [rQKT[9 + hp][b_] for b_ in range(2, 6)]
                self.ld(qb_t[:, :], QKT[5 + hp, :, 1024:3072], qb_t.r, reads=rq, writes=[qb_t.r])
                self.ld(kb_t[:, 0:2048], QKT[9 + hp, :, 1024:3072], kb_t.r, reads=rk, writes=[kb_t.r])
                self.ld(kb_t[:, 2048:2304], KC[1 + hp, :, :], kb_t.r, reads=[rKC], writes=[kb_t.r])
                for hf in range(2):
                    h_ = hp * 2 + hf
                    psl = slice(hf * 64, (hf + 1) * 64)
                    vsl = slice(0, 64) if hf == 0 else slice(64, 128)
                    vE_, vD_, vC_ = vsets[hf]
                    c0 = 128 + h_ * 64
                    self.ld(vE_[:, :, vsl], V0[1024:3072, c0:c0 + 64].rearrange("(t p) d -> p t d", p=128), vE_.r,
                            reads=rV0[2:6], writes=[vE_.r])
                    self.ld(vD_[:, :, vsl], V0[1088:3008, c0:c0 + 64].rearrange("(t p) d -> p t d", p=128), vD_.r,
                            reads=rV0[2:6], writes=[vD_.r])
                    self.ld(vC_[:, :, vsl], cv_d[:, c0:c0 + 64].rearrange("(t p) d -> p t d", p=128), vC_.r,
                            writes=[vC_.r], q="pool")
                    self.ld(bia_t[:, :], bias_d[h_], bia_t.r, writes=[bia_t.r])
                    po = None

                    def b_scores(r, psl=psl, qb_t=qb_t, kb_t=kb_t):
                        rs = min(max(r - 4, 0), 24)
                        psx = self.nxt(psA)
                        qv = qb_t[psl, r * 64:(r + 1) * 64]
                        for i in range(4):
                            k0 = (rs + 2 * i) * 64
                            self.mm(psx[:, i * 64:(i + 1) * 64], kb_t[psl, k0:k0 + 128], qv, True, True, [kb_t.r, qb_t.r], [psx.r])
                        for c_ in range(2):
                            self.mm(psx[:, 256 + c_ * 64:256 + (c_ + 1) * 64], kb_t[psl, 2048 + c_ * 128:2048 + (c_ + 1) * 128], qv,
                                    True, True, [kb_t.r, qb_t.r], [psx.r])
                        return psx
                    pend = [b_scores(0), b_scores(1)]
                    for r in range(32):
                        rs = min(max(r - 4, 0), 24)
                        cl = r - rs
                        psx = pend.pop(0)
                        if r + 2 < 32:
                            pend.append(b_scores(r + 2))
                        sb_ = self.nxt(ftmp)
                        self.stt(sb_[:, 0:256], psx[:, 0:256], 0.125, bia_t[:, cl * 256:(cl + 1) * 256], ALU.mult, ALU.add,
                                 [psx.r, bia_t.r], [sb_.r])
                        pT = self.nxt(btmp)
                        self.act(pT[:, 0:256], sb_[:, 0:256], AF.Exp, [sb_.r], [pT.r])
                        self.act(pT[:, 256:384], psx[:, 256:384], AF.Exp, [psx.r], [pT.r], scale=0.125)
                        if r % 8 == 0:
                            po = self.nxt(psB)
                        off = (r % 8) * 64
                        for i in range(4):
                            rr_ = rs + 2 * i
                            vt = vE_[:, rr_ // 2, :] if rs % 2 == 0 else vD_[:, (rr_ - 1) // 2, :]
                            vr = vE_.r if rs % 2 == 0 else vD_.r
                            self.mm(po[:, off:off + 64], vt, pT[:, i * 64:(i + 1) * 64], i == 0, False, [vr, pT.r], [po.r])
                        for c_ in range(2):
                            self.mm(po[:, off:off + 64], vC_[:, c_, :], pT[:, 256 + c_ * 64:256 + (c_ + 1) * 64], False, c_ == 1,
                                    [vC_.r, pT.r], [po.r])
                        if r % 8 == 7:
                            r0 = r - 7
                            normalize(po, 0, 512, hf, os_[psl, r0 * 64:(r0 + 8) * 64], os_.r)
                self.ld(OT[4 + hp, :, 1024:3072], os_[:, :], os_.r, reads=[os_.r], writes=[rOT[4 + hp][b_] for b_ in range(2, 6)])
            barrier()

        L1 = stage >= 20
        if stage >= 7:
            for b in range(NB):
                x = load_x(xT_d, rXin, b)
                out_proj_resid(0, x, b, OT, rOT)
                if stage >= 8:
                    resid_mlp(0, x, b)
                store_x(x, X1 if L1 else yT_d, rX1 if L1 else rY, b)
            barrier()
        else:
            for b in range(NB):
                x = load_x(xT_d, rXin, b)
                store_x(x, yT_d, rY, b)

        if L1:
            NCH = 48
            QKV1 = self.dscr("QKV1", [24, 128, NTOK])
            ZT = self.dscr("ZT", [8, 128, NTOK])
            QN = self.dscr("QN", [8, 128, NTOK])
            KN = self.dscr("KN", [8, 128, NTOK])
            KTM = self.dscr("KTM", [NTOK, 8, 128])
            VTM = self.dscr("VTM", [NTOK, 8, 128])
            TBS = self.dscr("TBS", [NCH, 128, 8, 64])
            ATT = self.dscr("ATT", [NCH, 128, 8, 64])
            QG = self.dscr("QG", [2, NCH, 128, 8, 64])
            OTF = self.dscr("OTF", [2, 8, 128, NTOK])
            rQKV1 = [[Res("rQKV1") for _ in range(NB)] for _ in range(24)]
            rZT = [[Res("rZT") for _ in range(NB)] for _ in range(8)]
            rQN = [Res("rQN%d" % i) for i in range(8)]
            rKN = [Res("rKN%d" % i) for i in range(8)]
            rKTM, rVTM = Res("rKTM"), Res("rVTM")
            rTBS = [Res("rTBS") for _ in range(NCH)]
            rATT = [Res("rATT") for _ in range(NCH)]
            rQG = [[Res("rQG") for _ in range(NCH)] for _ in range(2)]
            rOTF = [[Res("rOTF") for _ in range(NCH)] for _ in range(2)]

            l1c = self.sb("l1c", [128, 1024], F32)
            self.ld(l1c[:, :], l1c_d, l1c.r, writes=[l1c.r])
            tri2 = l1c[:, 0:64]
            triT2 = l1c[:, 64:128]
            ident2 = l1c[:, 128:192]
            nmincl = l1c[:, 192:256]
            mstrict = l1c[:, 256:320]
            nmstrT = l1c[:, 320:384]
            bdones = l1c[:, 384:512]
            sel0 = l1c[:, 512:640]
            sel1 = l1c[:, 640:768]
            onesf = l1c[:, 768:896]
            identf = l1c[:, 896:1024]
            l1m = self.sb("l1m", [128, 6, 64], F32)
            self.ld(l1m[:, :, :], l1m_d.rearrange("p (l q) -> p l q", q=64), l1m.r, writes=[l1m.r])
            convw = self.sb("convw", [128, 24, 3], F32)
            self.ld(convw[:, :, :], convw_d.rearrange("p (c j) -> p c j", j=3), convw.r, writes=[convw.r])
            nexpA = self.sb("nexpA", [128, 16], F32)
            dtb = self.sb("dtb", [128, 16], F32)
            onorm = self.sb("onorm", [128, 1], F32)
            onec = self.sb("onec", [128, 1], F32)
            self.memset("dve", onec[:, :], 1.0, [onec.r])
            self.ld(nexpA[:, :], alog_d, nexpA.r, writes=[nexpA.r])
            self.ld(dtb[:, :], dtb_d, dtb.r, writes=[dtb.r])
            self.ld(onorm[:, :], onorm_d, onorm.r, writes=[onorm.r])
            self.act(nexpA[:, :], nexpA[:, :], AF.Exp, [nexpA.r], [nexpA.r])
            self.ts("dve", nexpA[:, :], nexpA[:, :], -1.0, None, ALU.mult, None, [nexpA.r], [nexpA.r])
            la_tm = self.sb("la_tm", [128, 24, 16], F32)
            be_tm = self.sb("be_tm", [128, 24, 16], F32)
            negeg = self.sb("negeg", [128, NCH, 8], F32)
            ksc = self.sb("ksc", [128, NCH, 8], F32)
            egl = self.sb("egl", [128, NCH, 16], F32)

            modulation(1)

            def a1_block(b):
                x = load_x(X1, rX1, b)
                h = norm_mod(x, b, G1, 0, pa=ps[4])
                sl = slice(b * 512, (b + 1) * 512)
                for grp in range(8):
                    wt, wv = load_w(Wb[1]["w_in"], rWb[1]["w_in"], 8, grp * 512, 512)
                    for j in range(4):
                        c = grp * 4 + j
                        pm = self.nxt(psA)
                        for kc in range(8):
                            self.mm(pm[:, :], wv[:, kc, j * 128:(j + 1) * 128], h[:, kc, :], kc == 0, kc == 7, [wt.r, h.r], [pm.r])
                        ob = self.nxt(ostg)
                        if c < 24:
                            self.cp("act", ob[:, :], pm[:, :], [pm.r], [ob.r])
                            self.ld(QKV1[c, :, sl], ob[:, :], ob.r, reads=[ob.r], writes=[rQKV1[c][b]])
                        else:
                            self.act(ob[:, :], pm[:, :], AF.Silu, [pm.r], [ob.r])
                            self.ld(ZT[c - 24, :, sl], ob[:, :], ob.r, reads=[ob.r], writes=[rZT[c - 24][b]])
                        yield
                wt, wv = load_w(Wb[1]["w_in"], rWb[1]["w_in"], 8, 4096, 32)
                for t in range(4):
                    tt_ = b * 4 + t
                    tsl = slice(t * 128, (t + 1) * 128)
                    pab = self.nxt(psA)
                    for kc in range(8):
                        self.mm(pab[:, 0:32], h[:, kc, tsl], wv[:, kc, 0:32], kc == 0, kc == 7, [h.r, wt.r], [pab.r])
                    t1 = self.nxt(rtmp)
                    self.tt("dve", t1[:, 0:16], pab[:, 0:16], dtb[:, :], ALU.add, [pab.r, dtb.r], [t1.r])
                    self.act(t1[:, 16:32], t1[:, 0:16], AF.Exp, [t1.r], [t1.r])
                    self.act(t1[:, 32:48], t1[:, 16:32], AF.Ln, [t1.r, onec.r], [t1.r], bias=onec[:, 0:1])
                    self.tt("dve", la_tm[:, tt_, :], t1[:, 32:48], nexpA[:, :], ALU.mult, [t1.r, nexpA.r], [la_tm.r])
                    self.act(be_tm[:, tt_, :], pab[:, 16:32], AF.Sigmoid, [pab.r], [be_tm.r])
                    yield

            if stage >= 21:
                o = 0
                diagW, o = aview(o, [128, 72, 128], BF16, "diagW")
                raws, sfps, tmbs, kns = [], [], [], []
                for i in range(5):
                    t_, o = aview(o, [128, 514], BF16, "raw%d" % i)
                    raws.append(t_)
                    t_, o = aview(o, [128, 512], F32, "sfp%d" % i)
                    sfps.append(t_)
                    t_, o = aview(o, [128, 4, 128], BF16, "tmb%d" % i)
                    tmbs.append(t_)
                    t_, o = aview(o, [128, 512], F32, "kn%d" % i)
                    kns.append(t_)
                for c in range(24):
                    for j in range(3):
                        self.ts("dve", diagW[:, c * 3 + j, :], identf, convw[:, c, j:j + 1], None, ALU.mult, None,
                                [l1c.r, convw.r], [diagW.r])
                pieces = [(0, 256, 0, 256), (256, 256, 256, 256), (512, 256, 512, 256), (768, 256, 768, 256)]
                pieces += [(1024 + i * 512, 512, 1024, 2048) for i in range(4)]
                def b1_unit(p0, L, s0, Tq, c):
                        bl = p0 // 512
                        raw = self.nxt(raws)
                        sfp = self.nxt(sfps)
                        lo = max(p0 - 1, s0)
                        hi = min(p0 + L + 1, s0 + Tq)
                        rdeps = [rQKV1[c][bb] for bb in range(max(bl - 1, 0), min(bl + 2, NB))]
                        if lo > p0 - 1:
                            self.memset("pool", raw[:, 0:1], 0.0, [raw.r])
                        if hi < p0 + L + 1:
                            self.memset("pool", raw[:, L + 1:L + 2], 0.0, [raw.r])
                        self.ld(raw[:, lo - (p0 - 1):hi - (p0 - 1)], QKV1[c, :, lo:hi], raw.r, reads=rdeps, writes=[raw.r])
                        pc_ = self.nxt(psA)
                        for j in range(3):
                            self.mm(pc_[:, 0:L], diagW[:, c * 3 + j, :], raw[:, j:j + L], j == 0, j == 2, [diagW.r, raw.r], [pc_.r])
                        self.act(sfp[:, 0:L], pc_[:, 0:L], AF.Silu, [pc_.r], [sfp.r])
                        yield
                        src_tm = sfp
                        if c < 16:
                            sq = self.nxt(btmp)
                            self.tt("pool", sq[:, 0:L], sfp[:, 0:L], sfp[:, 0:L], ALU.mult, [sfp.r], [sq.r])
                            pn = self.nxt(psB1)
                            self.mm(pn[:, 0:L], ones_bf[:, :], sq[:, 0:L], True, True, [ones_bf.r, sq.r], [pn.r])
                            rt = self.nxt(rtmp)
                            self.act(rt[:, 0:L], pn[:, 0:L], AF.Ln, [pn.r, epsc.r], [rt.r], bias=epsc[:, 0:1])
                            rr = self.nxt(rtmp)
                            self.act(rr[:, 0:L], rt[:, 0:L], AF.Exp, [rt.r], [rr.r], scale=-0.5)
                            ob = self.nxt(ostg)
                            if c < 8:
                                self.stt(ob[:, 0:L], sfp[:, 0:L], 128.0 ** -0.5, rr[:, 0:L], ALU.mult, ALU.mult, [sfp.r, rr.r], [ob.r])
                                self.ld(QN[c, :, p0:p0 + L], ob[:, 0:L], ob.r, reads=[ob.r], writes=[rQN[c]])
                            else:
                                kn = self.nxt(kns)
                                self.tt("dve", kn[:, 0:L], sfp[:, 0:L], rr[:, 0:L], ALU.mult, [sfp.r, rr.r], [kn.r])
                                self.cp("pool", ob[:, 0:L], kn[:, 0:L], [kn.r], [ob.r])
                                self.ld(KN[c - 8, :, p0:p0 + L], ob[:, 0:L], ob.r, reads=[ob.r], writes=[rKN[c - 8]])
                                src_tm = kn
                        yield
                        if c >= 8:
                            hh = (c - 8) % 8
                            dst, rdst = (KTM, rKTM) if c < 16 else (VTM, rVTM)
                            pt = self.nxt(psA)
                            nt = L // 128
                            for i in range(nt):
                                self.tr(pt[:, i * 128:(i + 1) * 128], src_tm[:, i * 128:(i + 1) * 128], identf,
                                        [src_tm.r, l1c.r], [pt.r])
                            tmb = self.nxt(tmbs)
                            self.cp("act", tmb[:, 0:nt, :], pt[:, 0:L].rearrange("p (i d) -> p i d", d=128), [pt.r], [tmb.r])
                            self.ld(dst[p0:p0 + L, hh, :].rearrange("(i p) d -> p i d", p=128), tmb[:, 0:nt, :], tmb.r,
                                    reads=[tmb.r], writes=[rdst])
                def units(pidx):
                    return (b1_unit(pieces[pi][0], pieces[pi][1], pieces[pi][2], pieces[pi][3], c) for pi in pidx for c in range(24))

                self.interleave2(a1_block(0), iter([]), 3)
                self.interleave2(a1_block(1), iter([]), 3)
                self.interleave2(a1_block(2), units([0, 1]), 4)
                self.interleave2(a1_block(3), units([2, 3]), 4)
                self.interleave2(a1_block(4), units([4]), 4)
                self.interleave2(a1_block(5), units([5]), 4)
                self.interleave(units([6, 7]), 4)
                barrier()

            if stage >= 22:
                o = 0
                NSET = 3
                kTp, qTp = [], []
                for i in range(1):
                    t_, o = aview(o, [128, 8, 512], BF16, "kTp%d" % i)
                    kTp.append(t_)
                    t_, o = aview(o, [128, 8, 512], BF16, "qTp%d" % i)
                    qTp.append(t_)
                sets = []
                for i in range(NSET):
                    d_ = {}
                    for nm in ("b0", "b1", "b2", "b3", "b4"):
                        d_[nm], o = aview(o, [128, 8, 64], F32, "%s_%d" % (nm, i))
                    d_["eg"] = d_["b2"]
                    for nm in ("P0", "Q0", "T", "U", "X", "Xp", "attb", "tbb", "qg0", "qg1"):
                        d_[nm], o = aview(o, [128, 8, 64], BF16, "%s_%d" % (nm, i))
                    d_["g2"], o = aview(o, [128, 8], F32, "g2_%d" % i)
                    d_["be2"], o = aview(o, [128, 8], F32, "be2_%d" % i)
                    d_["sm"], o = aview(o, [128, 16], F32, "sm_%d" % i)
                    sets.append(d_)

                def bc_h(ap2):
                    return ap2.unsqueeze(2).to_broadcast([128, 8, 64])

                def bc_m(ap2):
                    return ap2.unsqueeze(1).to_broadcast([128, 8, 64])

                def v3(bank):
                    return bank[:, :].rearrange("p (h i) -> p h i", i=64)

                DS = (slice(0, 64), slice(64, 128))
                for pc in range(6):
                    kT = self.nxt(kTp)
                    qT = self.nxt(qTp)
                    self.ld(kT[:, :, :], KN.rearrange("h p t -> p h t")[:, :, pc * 512:(pc + 1) * 512], kT.r, reads=rKN, writes=[kT.r])
                    self.ld(qT[:, :, :], QN.rearrange("h p t -> p h t")[:, :, pc * 512:(pc + 1) * 512], qT.r, reads=rQN, writes=[qT.r])
                    def chunk_gen(ci, pc=pc, kT=kT, qT=qT):
                        cg = pc * 8 + ci
                        W = sets[ci % NSET] if ci < 6 else sets[ci - 6]
                        g2, be2, sm = W["g2"], W["be2"], W["sm"]
                        tt_ = cg // 2
                        hb = (cg % 2) * 64
                        hsl = slice(hb, hb + 64)
                        csl = slice(ci * 64, (ci + 1) * 64)
                        la_c = la_tm[hsl, tt_, :]
                        pg = self.nxt(ps)
                        self.mm(pg[0:64, 0:8], tri2[hsl, :], la_c[:, 0:8], True, True, [l1c.r, la_tm.r], [pg.r])
                        self.mm(pg[64:128, 0:8], triT2[hsl, :], la_c[:, 8:16], True, True, [l1c.r, la_tm.r], [pg.r])
                        self.mm(pg[:, 16:32], onesf[hsl, :], la_c, True, True, [l1c.r, la_tm.r], [pg.r])
                        self.cp("dve", g2[:, :], pg[:, 0:8], [pg.r], [g2.r])
                        self.act(egl[:, cg, :], pg[:, 16:32], AF.Exp, [pg.r], [egl.r])
                        self.tt("dve", sm[0:64, 0:8], pg[0:64, 16:24], g2[0:64, :], ALU.subtract, [pg.r, g2.r], [sm.r])
                        self.tt("dve", sm[64:128, 0:8], pg[64:128, 24:32], g2[64:128, :], ALU.subtract, [pg.r, g2.r], [sm.r])
                        self.act(ksc[:, cg, :], sm[:, 0:8], AF.Exp, [sm.r], [ksc.r])
                        self.act(sm[:, 8:16], g2[:, :], AF.Exp, [g2.r], [sm.r])
                        self.ts("pool", negeg[:, cg, :], sm[:, 8:16], -1.0, None, ALU.mult, None, [sm.r], [negeg.r])
                        self.cp("act", be2[0:64, :], be_tm[hsl, tt_, 0:8], [be_tm.r], [be2.r])
                        self.cp("act", be2[64:128, :], be_tm[hsl, tt_, 8:16], [be_tm.r], [be2.r])
                        yield
                        dG, dB = W["b0"], W["b1"]
                        self.tt("dve", dG[:, :, :], bc_m(ident2), bc_h(g2[:, :]), ALU.mult, [l1c.r, g2.r], [dG.r])
                        self.tt("dve", dB[:, :, :], bc_m(ident2), bc_h(be2[:, :]), ALU.mult, [l1c.r, be2.r], [dB.r])
                        dGf = dG[:, :, :].rearrange("p h i -> p (h i)")
                        dBf = dB[:, :, :].rearrange("p h i -> p (h i)")
                        for d in range(2):
                            pe_ = self.nxt(ps)
                            self.mm(pe_[:, :], sel0 if d == 0 else sel1, dGf, True, True, [l1c.r, dG.r], [pe_.r])
                            eg = W["eg"]
                            self.act(eg[:, :, :], v3(pe_), AF.Exp, [pe_.r], [eg.r])
                            qg_ = W["qg%d" % d]
                            self.tt("pool", qg_[:, :, :], qT[:, :, csl], eg[:, :, :], ALU.mult, [qT.r, eg.r], [qg_.r])
                            self.ld(QG[d, cg], qg_[:, :, :], qg_.r, reads=[qg_.r], writes=[rQG[d][cg]])
                        yield
                        pGr = self.nxt(ps)
                        pBr = self.nxt(ps)
                        self.mm(pGr[:, :], bdones, dGf, True, True, [l1c.r, dG.r], [pGr.r])
                        self.mm(pBr[:, :], bdones, dBf, True, True, [l1c.r, dB.r], [pBr.r])
                        E, E1, DTi = W["b2"], W["b3"], W["b4"]
                        self.tt("dve", E[:, :, :], v3(pGr), bc_h(g2[:, :]), ALU.subtract, [pGr.r, g2.r], [E.r])
                        self.tt("pool", E1[:, :, :], E[:, :, :], bc_m(nmincl), ALU.add, [E.r, l1c.r], [E1.r])
                        self.act(DTi[:, :, :], E1[:, :, :], AF.Exp, [E1.r], [DTi.r])
                        E2 = W["b3"]
                        self.tt("pool", E2[:, :, :], bc_m(nmstrT), E[:, :, :], ALU.subtract, [E.r, l1c.r, DTi.r], [E2.r])
                        Dm = W["b0"]
                        self.act(Dm[:, :, :], E2[:, :, :], AF.Exp, [E2.r], [Dm.r])
                        self.tt("pool", Dm[:, :, :], Dm[:, :, :], bc_h(be2[:, :]), ALU.mult, [Dm.r, be2.r], [Dm.r])
                        BrM = W["b1"]
                        self.tt("dve", BrM[:, :, :], v3(pBr), bc_m(mstrict), ALU.mult, [pBr.r, l1c.r], [BrM.r])
                        self.tt("pool", BrM[:, :, :], BrM[:, :, :], DTi[:, :, :], ALU.mult, [BrM.r, DTi.r], [BrM.r])
                        pkk = self.nxt(ps)
                        pqk = self.nxt(ps)
                        for hh in range(8):
                            for d in range(2):
                                self.mm(pkk[DS[d], hh * 64:(hh + 1) * 64], kT[:, hh, csl], kT[:, hh, csl], True, True, [kT.r], [pkk.r])
                        for hh in range(8):
                            for d in range(2):
                                self.mm(pqk[DS[d], hh * 64:(hh + 1) * 64], kT[:, hh, csl], qT[:, hh, csl], True, True, [kT.r, qT.r], [pqk.r])
                        P0, Q0, Tm, Um, Xm, Xpm = W["P0"], W["Q0"], W["T"], W["U"], W["X"], W["Xp"]
                        attb, tbb = W["attb"], W["tbb"]
                        self.tt("dve", P0[:, :, :], v3(pkk), Dm[:, :, :], ALU.mult, [pkk.r, Dm.r], [P0.r])
                        self.tt("dve", Q0[:, :, :], v3(pkk), BrM[:, :, :], ALU.mult, [pkk.r, BrM.r], [Q0.r])
                        self.tt("dve", attb[:, :, :], v3(pqk), DTi[:, :, :], ALU.mult, [pqk.r, DTi.r], [attb.r])
                        self.ld(ATT[cg], attb[:, :, :], attb.r, reads=[attb.r], writes=[rATT[cg]])

                        def mk(li):
                            return bc_m(l1m[:, li, :])
                        self.tt("pool", Xm[:, :, :], P0[:, :, :], mk(0), ALU.mult, [P0.r, l1m.r], [Xm.r])
                        self.tt("pool", Tm[:, :, :], bc_m(ident2), Xm[:, :, :], ALU.subtract, [l1c.r, Xm.r], [Tm.r])
                        self.tt("dve", Xpm[:, :, :], Q0[:, :, :], mk(0), ALU.mult, [Q0.r, l1m.r], [Xpm.r])
                        self.tt("dve", Um[:, :, :], bc_m(ident2), Xpm[:, :, :], ALU.subtract, [l1c.r, Xpm.r], [Um.r])
                        yield
                        for li in range(1, 6):
                            last = (li == 5)
                            if not last:
                                pX = self.nxt(ps)
                                for hh in range(8):
                                    for d in range(2):
                                        self.mm(pX[DS[d], hh * 64:(hh + 1) * 64], Q0[DS[d], hh, :], Tm[DS[d], hh, :], True, True,
                                                [Q0.r, Tm.r], [pX.r])
                                self.tt("dve", Xm[:, :, :], v3(pX), mk(li), ALU.mult, [pX.r, l1m.r], [Xm.r])
                            pXp = self.nxt(ps)
                            for hh in range(8):
                                for d in range(2):
                                    self.mm(pXp[DS[d], hh * 64:(hh + 1) * 64], P0[DS[d], hh, :], Um[DS[d], hh, :], True, True,
                                            [P0.r, Um.r], [pXp.r])
                            self.tt("dve", Xpm[:, :, :], v3(pXp), mk(li), ALU.mult, [pXp.r, l1m.r], [Xpm.r])
                            yield
                            if not last:
                                pY = self.nxt(ps)
                                for hh in range(8):
                                    for d in range(2):
                                        self.mm(pY[DS[d], hh * 64:(hh + 1) * 64], Um[DS[d], hh, :], Xm[DS[d], hh, :], True, True,
                                                [Um.r, Xm.r], [pY.r])
                            pYp = self.nxt(ps)
                            for hh in range(8):
                                for d in range(2):
                                    self.mm(pYp[DS[d], hh * 64:(hh + 1) * 64], Tm[DS[d], hh, :], Xpm[DS[d], hh, :], True, True,
                                            [Tm.r, Xpm.r], [pYp.r])
                            if not last:
                                self.tt("dve", Tm[:, :, :], Tm[:, :, :], v3(pY), ALU.subtract, [Tm.r, pY.r], [Tm.r])
                            self.tt("dve", Um[:, :, :], Um[:, :, :], v3(pYp), ALU.subtract, [Um.r, pYp.r], [Um.r])
                            yield
                        self.tt("pool", tbb[:, :, :], Um[:, :, :], bc_h(be2[:, :]), ALU.mult, [Um.r, be2.r], [tbb.r])
                        self.ld(TBS[cg], tbb[:, :, :], tbb.r, reads=[tbb.r], writes=[rTBS[cg]])
                    for grp_ in ((0, 1, 2), (3, 4, 5), (6, 7)):
                        gens = [chunk_gen(ci) for ci in grp_]
                        while gens:
                            for g_ in list(gens):
                                try:
                                    next(g_)
                                except StopIteration:
                                    gens.remove(g_)
                barrier()

            if stage >= 23:
                o = 0
                Sf, o = aview(o, [128, 16, 128], F32, "Sf")
                Sb_, o = aview(o, [128, 16, 128], BF16, "Sb")
                NR = 3
                kqs, qgs, ktms, vtms, tbts, atts, osts = [], [], [], [], [], [], []
                for i in range(NR):
                    t_, o = aview(o, [128, 2, 8, 64], BF16, "kq%d" % i)
                    kqs.append(t_)
                    t_, o = aview(o, [128, 2, 8, 64], BF16, "qg%d" % i)
                    qgs.append(t_)
                    t_, o = aview(o, [128, 8, 128], BF16, "ktm%d" % i)
                    ktms.append(t_)
                    t_, o = aview(o, [128, 8, 128], BF16, "vtm%d" % i)
                    vtms.append(t_)
                    t_, o = aview(o, [128, 8, 64], BF16, "tbt%d" % i)
                    tbts.append(t_)
                    t_, o = aview(o, [128, 8, 64], BF16, "att%d" % i)
                    atts.append(t_)
                    t_, o = aview(o, [128, 2, 8, 64], BF16, "ost%d" % i)
                    osts.append(t_)
                rps, vns, vsns, osfs = [], [], [], []
                for i in range(2):
                    t_, o = aview(o, [128, 8, 128], BF16, "rp%d" % i)
                    rps.append(t_)
                    t_, o = aview(o, [128, 8, 128], BF16, "vn%d" % i)
                    vns.append(t_)
                    t_, o = aview(o, [128, 8, 128], BF16, "vsn%d" % i)
                    vsns.append(t_)
                    t_, o = aview(o, [128, 8, 64], F32, "osf%d" % i)
                    osfs.append(t_)
                KNv = KN.rearrange("h p t -> p h t")
                SfR = [Res("SfR%d" % u) for u in range(16)]
                SbR = [Res("SbR%d" % u) for u in range(16)]
                rpR = {id(t_): [Res("rpR") for _ in range(16)] for t_ in rps}
                vnR = {id(t_): [Res("vnR") for _ in range(2)] for t_ in vns}
                vsnR = {id(t_): [Res("vsnR") for _ in range(4)] for t_ in vsns}
                seqs = [(0, 4), (4, 4), (8, 4), (12, 4), (16, 32)]
                for si, (cb, N) in enumerate(seqs):
                    if si < 4:
                        self.memset("dve", Sf[:, :, :], 0.0, SfR)
                    else:
                        self.ld(Sf[:, :, :], st_d.rearrange("u p v -> p u v"), Sf.r, writes=SfR)
                    self.cp("act", Sb_[:, :, :], Sf[:, :, :], SfR, SbR)
                    for s_ in range(N):
                        cc = (cb + s_, cb + N - 1 - s_)
                        kq = self.nxt(kqs)
                        qg = self.nxt(qgs)
                        ktm = self.nxt(ktms)
                        vtm = self.nxt(vtms)
                        tbt = self.nxt(tbts)
                        att = self.nxt(atts)
                        ost = self.nxt(osts)
                        rp = self.nxt(rps)
                        vn = self.nxt(vns)
                        vsn = self.nxt(vsns)
                        for d in range(2):
                            c_ = cc[d]
                            tk = slice(c_ * 64, (c_ + 1) * 64)
                            ds_ = slice(d * 64, (d + 1) * 64)
                            self.ld(kq[:, d, :, :], KNv[:, :, tk], kq.r, reads=rKN, writes=[kq.r])
                            self.ld(qg[:, d, :, :], QG[d, c_], qg.r, reads=[rQG[d][c_]], writes=[qg.r])
                            self.ld(ktm[ds_, :, :], KTM[tk, :, :], ktm.r, reads=[rKTM], writes=[ktm.r])
                            self.ld(vtm[ds_, :, :], VTM[tk, :, :], vtm.r, reads=[rVTM], writes=[vtm.r])
                            self.ld(tbt[ds_, :, :], TBS[c_, ds_, :, :], tbt.r, reads=[rTBS[c_]], writes=[tbt.r])
                            self.ld(att[ds_, :, :], ATT[c_, ds_, :, :], att.r, reads=[rATT[c_]], writes=[att.r])
                        DS = (slice(0, 64), slice(64, 128))
                        for g4 in range(2):
                            for d in range(2):
                                for h4 in range(4):
                                    hh = g4 * 4 + h4
                                    self.mm(psA[g4][DS[d], h4 * 128:(h4 + 1) * 128], kq[:, d, hh, :], Sb_[:, d * 8 + hh, :], True, True,
                                            [kq.r, SbR[d * 8 + hh]], [psA[g4].r])
                        for g4 in range(2):
                            for d in range(2):
                                for h4 in range(4):
                                    hh = g4 * 4 + h4
                                    self.stt(rp[DS[d], hh, :], psA[g4][DS[d], h4 * 128:(h4 + 1) * 128], negeg[DS[d], cc[d], hh:hh + 1],
                                             vtm[DS[d], hh, :], ALU.mult, ALU.add, [psA[g4].r, negeg.r, vtm.r], [rpR[id(rp)][d * 8 + hh]])
                        for g4 in range(2):
                            for d in range(2):
                                for h4 in range(4):
                                    hh = g4 * 4 + h4
                                    self.mm(psA[2 + g4][DS[d], h4 * 128:(h4 + 1) * 128], tbt[DS[d], hh, :], rp[DS[d], hh, :], True, True,
                                            [tbt.r, rpR[id(rp)][d * 8 + hh]], [psA[2 + g4].r])
                        for g4 in range(2):
                            hs4 = slice(g4 * 4, g4 * 4 + 4)
                            pv3 = psA[2 + g4][:, :].rearrange("p (h v) -> p h v", v=128)
                            self.cp("act", vn[:, hs4, :], pv3, [psA[2 + g4].r], [vnR[id(vn)][g4]])
                            for d in range(2):
                                self.tt("dve", vsn[DS[d], hs4, :], pv3[DS[d]],
                                        ksc[DS[d], cc[d], hs4].unsqueeze(2).to_broadcast([64, 4, 128]), ALU.mult,
                                        [psA[2 + g4].r, ksc.r], [vsnR[id(vsn)][g4 * 2 + d]])
                        for d in range(2):
                            poS, poA = (psB[0], psB[1]) if d == 0 else (psC[0], psC[1])
                            for hh in range(8):
                                self.mm(poS[:, hh * 64:(hh + 1) * 64], Sb_[:, d * 8 + hh, :], qg[:, d, hh, :], True, True, [SbR[d * 8 + hh], qg.r], [poS.r])
                            for hh in range(8):
                                self.mm(poA[:, hh * 64:(hh + 1) * 64], vn[DS[d], hh, :], att[DS[d], hh, :], True, True, [vnR[id(vn)][hh // 4], att.r], [poA.r])
                        for u in range(16):
                            d, hh = u // 8, u % 8
                            bk = psA[u // 4]
                            self.mm(bk[:, (u % 4) * 128:(u % 4 + 1) * 128], ktm[DS[d], hh, :], vsn[DS[d], hh, :], True, True,
                                    [ktm.r, vsnR[id(vsn)][(hh // 4) * 2 + d]], [bk.r])
                        for d in range(2):
                            poS, poA = (psB[0], psB[1]) if d == 0 else (psC[0], psC[1])
                            osf = self.nxt(osfs)
                            self.cp("act", osf[:, :, :], poS[:, :].rearrange("p (h i) -> p h i", i=64), [poS.r], [osf.r])
                            self.tt("dve", ost[:, d, :, :], poA[:, :].rearrange("p (h i) -> p h i", i=64), osf[:, :, :], ALU.add,
                                    [poA.r, osf.r], [ost.r])
                            tk = slice(cc[d] * 64, (cc[d] + 1) * 64)
                            self.ld(OTF[d].rearrange("h p t -> p h t")[:, :, tk], ost[:, d, :, :], ost.r, reads=[ost.r],
                                    writes=[rOTF[d][cc[d]]])
                        for u in range(16):
                            d = u // 8
                            bk = psA[u // 4]
                            self.stt(Sf[:, u, :], Sf[:, u, :], egl[:, cc[d], u:u + 1], bk[:, (u % 4) * 128:(u % 4 + 1) * 128], ALU.mult, ALU.add,
                                     [SfR[u], egl.r, bk.r], [SfR[u]])
                        for k4 in range(4):
                            self.cp("act", Sb_[:, k4 * 4:(k4 + 1) * 4, :], Sf[:, k4 * 4:(k4 + 1) * 4, :], SfR[k4 * 4:(k4 + 1) * 4], SbR[k4 * 4:(k4 + 1) * 4])
                    if si < 4:
                        self.ld(ns_d[si].rearrange("u p v -> p u v"), Sf[:, :, :], Sf.r, reads=SfR, writes=[Res("nso")])
                barrier()

            if stage >= 24:
                o = 0
                hid_end = 32768
                o = hid_end
                of_, o = aview(o, [128, 8, 512], BF16, "of")
                ob_, o = aview(o, [128, 8, 512], BF16, "ob")
                zs_, o = aview(o, [128, 8, 512], BF16, "zs")
                ot2, o = aview(o, [128, 8, 512], BF16, "ot2")
                for b in range(NB):
                    sl = slice(b * 512, (b + 1) * 512)
                    crange = range(b * 8, b * 8 + 8)
                    self.ld(of_[:, :, :], OTF[0].rearrange("h p t -> p h t")[:, :, sl], of_.r, reads=[rOTF[0][c_] for c_ in crange], writes=[of_.r])
                    self.ld(ob_[:, :, :], OTF[1].rearrange("h p t -> p h t")[:, :, sl], ob_.r, reads=[rOTF[1][c_] for c_ in crange], writes=[ob_.r])
                    self.ld(zs_[:, :, :], ZT.rearrange("h p t -> p h t")[:, :, sl], zs_.r, reads=[rZT[hh][b] for hh in range(8)], writes=[zs_.r])
                    def g_head(hh):
                        oh = self.nxt(ftmp)
                        self.tt("dve", oh[:, :], of_[:, hh, :], ob_[:, hh, :], ALU.add, [of_.r, ob_.r], [oh.r])
                        sq = self.nxt(btmp)
                        self.act(sq[:, :], oh[:, :], AF.Square, [oh.r], [sq.r])
                        pn = self.nxt(psBC)
                        self.mm(pn[:, :], ones_bf[:, :], sq[:, :], True, True, [ones_bf.r, sq.r], [pn.r])
                        yield
                        rt = self.nxt(rtmp)
                        self.act(rt[:, :], pn[:, :], AF.Ln, [pn.r, epsc.r], [rt.r], scale=1.0 / 128.0, bias=epsc[:, 0:1])
                        rr = self.nxt(rtmp)
                        self.act(rr[:, :], rt[:, :], AF.Exp, [rt.r], [rr.r], scale=-0.5)
                        yh = self.nxt(ftmp)
                        self.tt("dve", yh[:, :], oh[:, :], rr[:, :], ALU.mult, [oh.r, rr.r], [yh.r])
                        self.stt(ot2[:, hh, :], yh[:, :], onorm[:, 0:1], zs_[:, hh, :], ALU.mult, ALU.mult,
                                 [yh.r, onorm.r, zs_.r], [ot2.r])
                    self.interleave((g_head(hh) for hh in range(8)), 3)
                    x = load_x(X1, rX1, b)
                    out_proj_resid(1, x, b, None, None, ot=ot2)
                    resid_mlp(1, x, b)
                    store_x(x, yT_d, rY, b)
                barrier()

        self.final_wait()
        P.emit(self.st)
        return nc

    def final_wait(self):
        P = self.P
        toks = []
        for eng in ENGS:
            for (fn, waits, inc) in P.ops[eng]:
                pass
        final = {}
        for eng in ENGS:
            for (fn, waits, inc) in P.ops[eng]:
                if inc is not None:
                    final[inc[0]] = final.get(inc[0], 0) + inc[1]
        waits = [(k, v) for k, v in final.items() if k != "sp"]
        P.ops["sp"].append((None, waits, None))


_CACHE = {}


def _host_consts():
    GRID_W, HD = 64, 64
    t = np.arange(2048)
    pos = np.stack([t // GRID_W, t % GRID_W], -1).astype(np.float32)
    axis_dim = HD // 2
    inv_freq = (1.0 / (10000.0 ** (np.arange(0, axis_dim, 2, dtype=np.float32) / axis_dim))).astype(np.float32)
    ang = pos[:, :, None] * inv_freq
    cos = np.cos(ang).astype(np.float32)
    sin = np.sin(ang).astype(np.float32)
    cosT = np.zeros((128, 2048), np.float32)
    sinT = np.zeros((128, 2048), np.float32)
    perm = np.zeros((128, 128), np.float32)
    for p in range(128):
        d = p % 64
        ax = d // 32
        j = d % 16
        cosT[p] = cos[:, ax, j]
        sinT[p] = sin[:, ax, j]
        if (d % 32) < 16:
            perm[p + 16, p] = -1.0
        else:
            perm[p - 16, p] = 1.0
    ident = np.eye(128, dtype=np.float32)
    return cosT, sinT, perm, ident


def _l1_consts():
    c = np.zeros((128, 1024), np.float32)
    p = np.arange(128)[:, None]
    t = p % 64
    d = p // 64
    i = np.arange(64)[None, :]
    c[:, 0:64] = (t <= i)
    c[:, 64:128] = (t >= i)
    c[:, 128:192] = (t == i)
    incl = np.where(d == 0, i >= t, i <= t)
    c[:, 192:256] = np.where(incl, 0.0, NEG)
    c[:, 256:320] = np.where(d == 0, i > t, i < t)
    strT = np.where(d == 0, i < t, i > t)
    c[:, 320:384] = np.where(strT, 0.0, NEG)
    q = np.arange(128)[None, :]
    c[:, 384:512] = (p // 64 == q // 64)
    c[:, 512:640] = (p < 64) * np.ones((1, 128))
    c[:, 640:768] = (p >= 64) * np.ones((1, 128))
    c[:, 768:896] = 1.0
    c[:, 896:1024] = (p == q)
    return c


def _l1_masks():
    m = np.zeros((128, 6, 64), np.float32)
    i = (np.arange(128) % 64)[:, None]
    j = np.arange(64)[None, :]
    for li, sz in enumerate((1, 2, 4, 8, 16, 32)):
        m[:, li, :] = ((i // (2 * sz)) == (j // (2 * sz))) & ((i // sz) != (j // sz))
    return m.reshape(128, 384)


def _bias_expand(rel_bias):
    H = 8
    e = np.arange(2)[:, None, None, None, None]
    kc = np.arange(64)[None, :, None, None, None]
    cl = np.arange(8)[None, None, :, None, None]
    i = np.arange(4)[None, None, None, :, None]
    qc = np.arange(64)[None, None, None, None, :]
    dr = -cl + 2 * i + e
    cs = np.clip(qc - 8, 0, 48)
    valid = (kc >= cs) & (kc < cs + 16)
    valid = np.broadcast_to(valid, (2, 64, 8, 4, 64))
    ri = np.broadcast_to(dr + 7, (2, 64, 8, 4, 64))
    ci = np.clip(np.broadcast_to(kc - qc + 15, (2, 64, 8, 4, 64)), 0, 30)
    out = np.empty((H, 2, 64, 8, 4, 64), np.float32)
    for h in range(H):
        g = rel_bias[h][ri, ci]
        out[h] = np.where(valid, g, np.float32(NEG))
    return np.ascontiguousarray(out.reshape(H, 128, 2048))


def kernel(x_prompt, x_sample, c, cache_l0_k, cache_l0_v, state_l1, c_ctx,
           l0_mod_w, l0_mod_b, l0_norm1, l0_w_in, l0_q_norm_a, l0_k_norm_a, l0_q_norm_b, l0_k_norm_b,
           l0_rel_bias, l0_w_out, l0_norm2, l0_mlp_w1, l0_mlp_w2,
           l1_mod_w, l1_mod_b, l1_norm1, l1_w_in, l1_conv_w, l1_a_log, l1_dt_bias, l1_out_norm,
           l1_w_out, l1_norm2, l1_mlp_w1, l1_mlp_w2, _stage=99):
    f = lambda a: np.ascontiguousarray(np.asarray(a, dtype=np.float32))
    key = ("nc", _stage)
    if key not in _CACHE:
        _CACHE[key] = Builder(_stage).build()
    nc = _CACHE[key]
    cosT, sinT, perm, ident = _host_consts()
    colT = lambda v, n: f(np.asarray(v).reshape(n, 128).T)
    t2 = lambda v: np.tile(np.asarray(v, np.float32), 2)
    gains = f(np.stack([t2(l0_q_norm_a), t2(l0_k_norm_a), t2(l0_q_norm_b), t2(l0_k_norm_b)], -1))
    gk_row = np.concatenate([np.tile(np.asarray(l0_k_norm_a, np.float32), 2), np.tile(np.asarray(l0_k_norm_b, np.float32), 8)])
    gk_tm = f(np.broadcast_to(gk_row[None, :], (128, 640)))
    bias_exp = _bias_expand(np.asarray(l0_rel_bias, np.float32))
    shared = {
        "l0_mod_w": f(l0_mod_w), "l0_mod_bT": colT(l0_mod_b, 48), "l0_n1T": colT(l0_norm1, 8), "l0_n2T": colT(l0_norm2, 8),
        "l0_w_in": f(l0_w_in), "l0_w_out": f(l0_w_out), "l0_w1": f(l0_mlp_w1), "l0_w2": f(l0_mlp_w2),
        "l1_mod_w": f(l1_mod_w), "l1_mod_bT": colT(l1_mod_b, 48), "l1_n1T": colT(l1_norm1, 8), "l1_n2T": colT(l1_norm2, 8),
        "l1_w_in": f(l1_w_in), "l1_w_out": f(l1_w_out), "l1_w1": f(l1_mlp_w1), "l1_w2": f(l1_mlp_w2),
        "l1c": _l1_consts(), "l1m": _l1_masks(),
        "convw": f(np.asarray(l1_conv_w, np.float32).reshape(3, 24, 128).transpose(2, 1, 0).reshape(128, 72)),
        "alog_b": f(np.broadcast_to(np.asarray(l1_a_log, np.float32).reshape(1, 16), (128, 16))),
        "dtb_b": f(np.broadcast_to(np.asarray(l1_dt_bias, np.float32).reshape(1, 16), (128, 16))),
        "onorm": f(np.asarray(l1_out_norm, np.float32).reshape(128, 1)),
        "gains": gains, "gk_tm": gk_tm, "cosT": cosT, "sinT": sinT, "perm": perm, "bias_exp": bias_exp, "ident": ident,
    }
    xp = np.asarray(x_prompt, np.float32)
    xs = np.asarray(x_sample, np.float32)
    in_maps = []
    for i in range(8):
        xi = np.concatenate([xp[4 * i:4 * i + 4].reshape(NPR, 1024), xs[i]], 0)
        cpair = np.stack([np.asarray(c_ctx, np.float32), np.asarray(c, np.float32)[i]], -1)
        cTi = cpair.reshape(8, 128, 2).transpose(1, 0, 2).reshape(128, 16)
        m = dict(shared)
        m["xT"] = f(xi.T)
        m["cT"] = f(cTi)
        m["ck"] = f(np.asarray(cache_l0_k, np.float32)[i].reshape(256, 640))
        m["cv"] = f(np.asarray(cache_l0_v, np.float32)[i].reshape(256, 640))
        m["state"] = f(np.asarray(state_l1, np.float32)[i].reshape(16, 128, 128))
        in_maps.append(m)
    res = run_bass_kernel_spmd(nc, in_maps, core_ids=list(range(8)))
    yp = np.empty((32, 256, 1024), np.float32)
    ys = np.empty((8, 2048, 1024), np.float32)
    nk = np.empty((32, 256, 10, 64), np.float32)
    nv = np.empty((32, 256, 10, 64), np.float32)
    ns = np.empty((32, 2, 8, 128, 128), np.float32)
    for i in range(8):
        r = res.results[i]
        y = np.asarray(r["yT"]).T
        yp[4 * i:4 * i + 4] = y[:NPR].reshape(4, 256, 1024)
        ys[i] = y[NPR:]
        nk[4 * i:4 * i + 4] = np.asarray(r["nk"]).reshape(4, 256, 10, 64)
        nv[4 * i:4 * i + 4] = np.asarray(r["nv"]).reshape(4, 256, 10, 64)
        ns[4 * i:4 * i + 4] = np.asarray(r["ns"]).reshape(4, 2, 8, 128, 128)
    return (yp, ys, nk, nv, ns)
```
